# Optimizing a Trainium2 kernel written in Bass

```python
import jax, jax.numpy as jnp
from jax import lax
import numpy as np

D_MODEL = 1024
BATCH = 8
SEQ = 2048
DEPTH = 1

D_MIX = D_MODEL
D_REC = D_MIX // 2
D_ATT = D_MIX - D_REC
N_LRU_BLOCKS = 8
LRU_BLOCK = D_REC // N_LRU_BLOCKS
LRU_C = 8.0
CONV_WIDTH = 4
HEAD_DIM = 64
N_HEADS = D_ATT // HEAD_DIM
DILATED_PATTERNS = ((128, 1), (512, 4), (2048, 16))
ROPE_THETA = 10000.0
NORM_EPS = 1e-6
NEG_INF = -1e30
D_IN_PROJ = 2 * D_REC + 4 * D_ATT
SPLIT_IDX = (D_REC, 2 * D_REC, 2 * D_REC + D_ATT, 2 * D_REC + 2 * D_ATT, 2 * D_REC + 3 * D_ATT)

kernel_name = 'hymba_rglru_dilated_attn_layer'


def rms_norm(x, g):
    xf = x.astype(jnp.float32)
    y = xf * lax.rsqrt(jnp.mean(xf * xf, axis=-1, keepdims=True) + NORM_EPS)
    return (y * g.astype(jnp.float32)).astype(x.dtype)


def rotary(x, positions):
    half = HEAD_DIM // 2
    inv_freq = ROPE_THETA ** (-jnp.arange(half, dtype=jnp.float32) / half)
    ang = positions.astype(jnp.float32)[..., None] * inv_freq
    cos = jnp.cos(ang)[:, :, None, :]
    sin = jnp.sin(ang)[:, :, None, :]
    xf = x.astype(jnp.float32)
    x1, x2 = xf[..., :half], xf[..., half:]
    out = jnp.concatenate([x1 * cos - x2 * sin, x2 * cos + x1 * sin], axis=-1)
    return out.astype(x.dtype)


def causal_depthwise_conv(x, w, b):
    S = x.shape[1]
    xp = jnp.pad(x, ((0, 0), (CONV_WIDTH - 1, 0), (0, 0)))
    return sum(xp[:, k:k + S, :] * w[k] for k in range(CONV_WIDTH)) + b


def rg_lru(x, w_a, b_a, w_x, b_x, lam):
    B, S, C = x.shape
    xb = x.reshape(B, S, N_LRU_BLOCKS, LRU_BLOCK)
    r = jax.nn.sigmoid(jnp.einsum('bshi,hij->bshj', xb, w_a).reshape(B, S, C) + b_a)
    i = jax.nn.sigmoid(jnp.einsum('bshi,hij->bshj', xb, w_x).reshape(B, S, C) + b_x)
    log_a = -LRU_C * r.astype(jnp.float32) * jax.nn.softplus(-lam.astype(jnp.float32))
    a = jnp.exp(log_a)
    mult = jnp.sqrt(-jnp.expm1(2.0 * log_a))
    u = mult * (i * x).astype(jnp.float32)

    def combine(left, right):
        a_l, b_l = left
        a_r, b_r = right
        return a_l * a_r, a_r * b_l + b_r

    _, h = lax.associative_scan(combine, (a, u), axis=1)
    return h.astype(x.dtype)


def dilated_attention(q, k, v, window, dilation):
    B, H, S, Dh = q.shape
    L = S // dilation
    blk = window // dilation
    n_blk = -(-L // blk)
    pad = n_blk * blk - L

    def to_blocks(t):
        t = t.reshape(B, H, L, dilation, Dh).transpose(0, 1, 3, 2, 4)
        t = jnp.pad(t, ((0, 0), (0, 0), (0, 0), (0, pad), (0, 0)))
        return t.reshape(B, H, dilation, n_blk, blk, Dh)

    def with_prev(t):
        prev = jnp.pad(t, ((0, 0), (0, 0), (0, 0), (1, 0), (0, 0), (0, 0)))[:, :, :, :-1]
        return jnp.concatenate([prev, t], axis=4)

    qb = to_blocks(q)
    kc = with_prev(to_blocks(k))
    vc = with_prev(to_blocks(v))
    scores = jnp.einsum('bhrnqd,bhrnkd->bhrnqk', qb, kc, preferred_element_type=jnp.float32)
    qi = jnp.arange(blk)[:, None]
    ki = jnp.arange(2 * blk)[None, :]
    dist = qi + blk - ki
    blk_idx = jnp.arange(n_blk)[:, None, None]
    valid = ((dist >= 0) & (dist <= blk))[None] & (blk_idx * blk + ki[None] - blk >= 0)
    scores = jnp.where(valid, scores, NEG_INF)
    m = jnp.max(scores, axis=-1, keepdims=True)
    p = jnp.exp(scores - m)
    denom = jnp.sum(p, axis=-1, keepdims=True)
    out = jnp.einsum('bhrnqk,bhrnkd->bhrnqd', p, vc.astype(jnp.float32)) / denom
    lse = (m + jnp.log(denom))[..., 0]

    def from_blocks(t):
        t = t.reshape(B, H, dilation, n_blk * blk, *t.shape[5:])[:, :, :, :L]
        t = jnp.moveaxis(t, 2, 3)
        return t.reshape(B, H, S, *t.shape[4:])

    return from_blocks(out), from_blocks(lse)


def setup_inputs(seed: int = 0) -> dict:
    key = jax.random.key(seed)
    ks = jax.random.split(key, 20)
    D = D_MODEL
    nrm = lambda k, shape, fan_in: jax.random.normal(k, shape, jnp.float32) * fan_in ** -0.5
    x = jax.random.normal(ks[0], (BATCH, SEQ, D), jnp.float32)
    c = jax.random.normal(ks[1], (BATCH, D), jnp.float32)
    offsets = jax.random.randint(ks[2], (BATCH, 1), 0, 4096, dtype=jnp.int32)
    positions = offsets + jnp.arange(SEQ, dtype=jnp.int32)[None, :]
    w_ada = nrm(ks[3], (DEPTH, D, 3 * D), D) * 0.5
    b_ada = 0.02 * jax.random.normal(ks[4], (DEPTH, 3 * D), jnp.float32)
    norm_pre = 1.0 + 0.1 * jax.random.normal(ks[5], (DEPTH, D), jnp.float32)
    norm_post = 1.0 + 0.1 * jax.random.normal(ks[6], (DEPTH, D), jnp.float32)
    w_in = nrm(ks[7], (DEPTH, D, D_IN_PROJ), D)
    conv_w = nrm(ks[8], (DEPTH, CONV_WIDTH, D_REC), CONV_WIDTH)
    conv_b = 0.02 * jax.random.normal(ks[9], (DEPTH, D_REC), jnp.float32)
    w_rg_a = nrm(ks[10], (DEPTH, N_LRU_BLOCKS, LRU_BLOCK, LRU_BLOCK), LRU_BLOCK)
    b_rg_a = 0.02 * jax.random.normal(ks[11], (DEPTH, D_REC), jnp.float32)
    w_rg_x = nrm(ks[12], (DEPTH, N_LRU_BLOCKS, LRU_BLOCK, LRU_BLOCK), LRU_BLOCK)
    b_rg_x = 0.02 * jax.random.normal(ks[13], (DEPTH, D_REC), jnp.float32)
    a_c = jax.random.uniform(ks[14], (DEPTH, D_REC), jnp.float32, 0.9, 0.999)
    a_base = a_c ** (1.0 / LRU_C)
    lru_lambda = jnp.log(a_base) - jnp.log1p(-a_base)
    norm_rec = 1.0 + 0.1 * jax.random.normal(ks[15], (DEPTH, D_REC), jnp.float32)
    norm_att = 1.0 + 0.1 * jax.random.normal(ks[16], (DEPTH, D_ATT), jnp.float32)
    w_out = nrm(ks[17], (DEPTH, D_MIX, D), D_MIX)
    return {'x': x, 'c': c, 'positions': positions, 'w_ada': w_ada, 'b_ada': b_ada,
            'norm_pre': norm_pre, 'norm_post': norm_post, 'w_in': w_in, 'conv_w': conv_w,
            'conv_b': conv_b, 'w_rg_a': w_rg_a, 'b_rg_a': b_rg_a, 'w_rg_x': w_rg_x,
            'b_rg_x': b_rg_x, 'lru_lambda': lru_lambda, 'norm_rec': norm_rec,
            'norm_att': norm_att, 'w_out': w_out}


def reference(x, c, positions, w_ada, b_ada, norm_pre, norm_post, w_in, conv_w, conv_b,
              w_rg_a, b_rg_a, w_rg_x, b_rg_x, lru_lambda, norm_rec, norm_att, w_out):
    B, S, _ = x.shape
    for l in range(DEPTH):
        mod = jnp.einsum('bd,de->be', jax.nn.silu(c), w_ada[l]) + b_ada[l]
        shift, scale, gate = jnp.split(mod, 3, axis=-1)
        h = rms_norm(x, norm_pre[l]) * (1.0 + scale[:, None, :]) + shift[:, None, :]

        proj = jnp.einsum('bsd,de->bse', h, w_in[l])
        xa, ga, q, k, v, gb = jnp.split(proj, SPLIT_IDX, axis=-1)

        xa = causal_depthwise_conv(xa, conv_w[l], conv_b[l])
        ya = rg_lru(xa, w_rg_a[l], b_rg_a[l], w_rg_x[l], b_rg_x[l], lru_lambda[l]) * jax.nn.silu(ga)
        ya = rms_norm(ya, norm_rec[l])

        heads = lambda t: t.reshape(B, S, N_HEADS, HEAD_DIM)
        qh = (rotary(heads(q), positions) * HEAD_DIM ** -0.5).transpose(0, 2, 1, 3)
        kh = rotary(heads(k), positions).transpose(0, 2, 1, 3)
        vh = heads(v).transpose(0, 2, 1, 3)
        outs, lses = zip(*[dilated_attention(qh, kh, vh, w, d) for (w, d) in DILATED_PATTERNS])
        wts = jax.nn.softmax(jnp.stack(lses, axis=0), axis=0)
        att = jnp.einsum('pbhs,pbhsd->bhsd', wts, jnp.stack(outs, axis=0))
        yb = att.transpose(0, 2, 1, 3).reshape(B, S, D_ATT).astype(x.dtype) * jax.nn.silu(gb)
        yb = rms_norm(yb, norm_att[l])

        mix = jnp.einsum('bse,ed->bsd', jnp.concatenate([ya, yb], axis=-1), w_out[l])
        x = x + gate[:, None, :] * rms_norm(mix, norm_post[l])
    return x
```

```python
import contextlib
import numpy as np
import concourse.bass as bass
import concourse.mybir as mybir
from concourse.ap import AP
from concourse.bass_utils import run_bass_kernel_spmd

F32 = mybir.dt.float32
BF16 = mybir.dt.bfloat16
I32 = mybir.dt.int32
ALU = mybir.AluOpType
AF = mybir.ActivationFunctionType

D = 1024
S = 2048
NB = 16
EPS = 1e-6
PI = float(np.pi)
ENGS = ("tensor", "scalar", "vector", "gpsimd", "sync")
ALL_E = ENGS


class Prog:
    def __init__(self, nc):
        self.nc = nc
        self.ops = []
        self.touch = {}

    def op(self, eng, fn, reads=(), writes=(), dma_sem=None, ndma=1, extra=()):
        i = len(self.ops)
        self.ops.append(dict(eng=eng, fn=fn, reads=tuple(reads), writes=tuple(writes),
                             dma_sem=dma_sem, ndma=ndma, extra=set(extra)))
        for k in tuple(reads) + tuple(writes):
            self.touch.setdefault(k, set()).add(i)
        return i

    def snapshot(self):
        last = {}
        for i, o in enumerate(self.ops):
            if o["fn"] is None:
                continue
            last[("d", o["dma_sem"]) if o["dma_sem"] else ("e", o["eng"])] = i
        return set(last.values())

    def fence(self, engs, keys=None, dep=None):
        if dep is None:
            dep = self.snapshot()
        for e in engs:
            self.op(e, None, extra=dep)

    def emit(self):
        nc = self.nc
        ops = self.ops
        last_writer = {}
        readers = {}
        for i, o in enumerate(ops):
            deps = set(o["extra"])
            for k in o["reads"]:
                if k in last_writer:
                    deps.add(last_writer[k])
            for k in o["writes"]:
                if k in last_writer:
                    deps.add(last_writer[k])
                for r in readers.get(k, ()):
                    deps.add(r)
            deps.discard(i)
            deps = {d for d in deps if ops[d]["fn"] is not None}
            if o["eng"] == "tensor":
                deps = {d for d in deps if ops[d]["eng"] != "tensor"}
            o["deps"] = deps
            for k in o["reads"]:
                readers.setdefault(k, []).append(i)
            for k in o["writes"]:
                last_writer[k] = i
                readers[k] = []
        signal = set()
        for o in ops:
            signal |= o["deps"]
        counts = {}
        sem_names = []
        for i, o in enumerate(ops):
            o["sem"] = None
            if o["fn"] is None:
                continue
            if o["dma_sem"] is not None:
                name = "d_" + o["dma_sem"]
                counts[name] = counts.get(name, 0) + 16 * o["ndma"]
                o["sem"], o["val"] = name, counts[name]
            elif i in signal:
                name = "e_" + o["eng"]
                counts[name] = counts.get(name, 0) + 1
                o["sem"], o["val"] = name, counts[name]
            if o["sem"] and o["sem"] not in sem_names:
                sem_names.append(o["sem"])
        with contextlib.ExitStack() as st:
            sems = {n: st.enter_context(nc.semaphore(n)) for n in sem_names}
            block = st.enter_context(nc.Block())
            per_eng = {e: [i for i, o in enumerate(ops) if o["eng"] == e] for e in ENGS}

            def make(ename):
                def body(eng):
                    waited = {}
                    for i in per_eng[ename]:
                        o = ops[i]
                        need = {}
                        for d in o["deps"]:
                            od = ops[d]
                            need[od["sem"]] = max(need.get(od["sem"], 0), od["val"])
                        for s, v in need.items():
                            if waited.get(s, 0) < v:
                                eng.wait_ge(sems[s], v)
                                waited[s] = v
                        if o["fn"] is None:
                            continue
                        ins = o["fn"](eng)
                        if o["sem"] is not None:
                            if o["dma_sem"] is not None:
                                lst = ins if isinstance(ins, (list, tuple)) else [ins]
                                assert len(lst) == o["ndma"]
                                for x in lst:
                                    x.then_inc(sems[o["sem"]], 16)
                            else:
                                ins.then_inc(sems[o["sem"]], 1)
                return body

            for e in ENGS:
                if per_eng[e]:
                    getattr(block, e)(make(e))


def pipeline(n, stages):
    ml = max(l for l, _ in stages)
    for t in range(n + ml):
        for lag, fn in stages:
            i = t - lag
            if 0 <= i < n:
                fn(i)


def view(ap, dims, off=0):
    return AP(ap.tensor, ap.offset + off, [list(ap.ap[0])] + [list(d) for d in dims])


PP_C, PP_NPRE, PP_CONVW, PP_CONVB, PP_BA, PP_BX, PP_LAM, PP_NREC, PP_NATT, PP_INVF = 0, 8, 16, 32, 36, 40, 44, 48, 52, 56
NPP = 88
SM_SC, SM_G1, SM_SHIFT, SM_C8, SM_NC8, SM_N2C8, SM_SS, SM_STD, SM_RSTD = 0, 8, 16, 24, 28, 32, 36, 52, 68
SM_CARRY, SM_ONE, SM_SSP, SM_STDP, SM_RSTDP, SM_POSF, SM_SP, SM_ZERO = 84, 88, 96, 112, 128, 144, 160, 200


def build_program():
    nc = bass.Bass("TRN2", target_bir_lowering=False)
    x_d = nc.dram_tensor("x", [S, D], F32, kind="ExternalInput").ap()
    pp_d = nc.dram_tensor("pp", [128, NPP], F32, kind="ExternalInput").ap()
    pos_d = nc.dram_tensor("pos", [128, NB], I32, kind="ExternalInput").ap()
    wada_d = nc.dram_tensor("w_ada", [D, 3 * D], F32, kind="ExternalInput").ap()
    rows_d = nc.dram_tensor("rows", [1, 4 * D], F32, kind="ExternalInput").ap()
    win_d = nc.dram_tensor("w_in", [D, 3 * D], F32, kind="ExternalInput").ap()
    wout_d = nc.dram_tensor("w_out", [D, D], F32, kind="ExternalInput").ap()
    wbd_d = nc.dram_tensor("wbd", [128, 8 * 128], F32, kind="ExternalInput").ap()
    mask_d = nc.dram_tensor("maskT", [128, 256], F32, kind="ExternalInput").ap()
    vscr_d = nc.dram_tensor("vscr", [S, 512], BF16, kind="Internal").ap()
    out_d = nc.dram_tensor("out", [S, D], F32, kind="ExternalOutput").ap()

    NW = 53000
    with contextlib.ExitStack() as st:
        big = st.enter_context(nc.sbuf_tensor("big", [128, NW], F32))
        ps = st.enter_context(nc.psum_tensor("ps", [128, 4096], F32))
        p = Prog(nc)
        OP = p.op

        def bank(b, lo=0, hi=512, p0=0, p1=128):
            return ps[p0:p1, b * 512 + lo:b * 512 + hi]

        def bank16(b, p0=0, p1=128):
            return ps[p0:p1, b * 512:(b + 1) * 512].bitcast(BF16)

        cur = [0]

        def alloc(nwords):
            o = cur[0]
            cur[0] += nwords
            assert cur[0] <= NW, cur[0]
            return o

        def f32v(off, n, p0=0, p1=128):
            return big[p0:p1, off:off + n]

        def b16v(off, n, p0=0, p1=128):
            return big[p0:p1, off:off + n // 2].bitcast(BF16)

        o_yT = alloc(8 * S // 2)
        o_wout = alloc(8 * D // 2)
        o_gn = alloc(D)
        o_mask = alloc(128)
        o_ones64 = alloc(32)
        o_ident = alloc(64)
        o_ones = alloc(64)
        o_onesrow = alloc(128)
        o_pp = alloc(NPP)
        o_small = alloc(256)
        o_wbd = alloc(1024)
        yT = b16v(o_yT, 8 * S).rearrange("p (e t) -> p e t", e=8)
        woutb = b16v(o_wout, 8 * D).rearrange("p (e n) -> p e n", e=8)
        gn_bc = f32v(o_gn, D)
        maskT = b16v(o_mask, 256)
        ones64 = b16v(o_ones64, 64)
        ident = b16v(o_ident, 128)
        onesb = b16v(o_ones, 128)
        onesrow = f32v(o_onesrow, 128, 0, 1)
        pp = f32v(o_pp, NPP)
        sm = f32v(o_small, 256)
        wbd = f32v(o_wbd, 1024).rearrange("p (g m) -> p g m", g=8)

        def smc(c0, n=1):
            return sm[:, c0:c0 + n]

        def ppc(c0, n=1):
            return pp[:, c0:c0 + n]

        o_hT = alloc(8 * S // 2)
        o_W = alloc(2 * 2048)
        o_cs = alloc(2 * NB * 64)
        RA = cur[0]
        hT = b16v(o_hT, 8 * S).rearrange("p (k t) -> p k t", k=8)
        Wsl = [b16v(o_W + s * 2048, 8 * 512).rearrange("p (k n) -> p k n", k=8) for s in range(2)]
        cosF = f32v(o_cs, NB * 64)
        sinS = f32v(o_cs + NB * 64, NB * 64)

        cur[0] = RA
        o_rows = alloc(4 * D)
        o_wada = alloc(3 * 3 * D)
        o_modrow = alloc(3 * D)
        o_gnrow = alloc(D)
        o_ident32 = o_yT
        o_mask32 = o_yT + 128
        o_wbd32 = o_yT + 128 + 256
        o_ang = o_yT + 128 + 256 + 1024
        assert o_ang + 4 * 512 + 64 <= o_yT + 8192
        rows_sb = f32v(o_rows, 4 * D, 0, 1)
        wada_sl = [f32v(o_wada + s * 3 * D, 3 * D) for s in range(3)]
        modrow = f32v(o_modrow, 3 * D, 0, 1)
        gnrow = f32v(o_gnrow, D, 0, 1)
        ident32 = f32v(o_ident32, 128)
        mask32 = f32v(o_mask32, 256)
        wbd32 = f32v(o_wbd32, 1024)
        posi = big[:, o_ang + 2048:o_ang + 2048 + NB].bitcast(I32)
        ANG = f32v(o_ang, 512)
        KI = big[:, o_ang + 512:o_ang + 1024].bitcast(I32)
        KF = f32v(o_ang + 1024, 512)
        SN = f32v(o_ang + 1536, 512)

        OP("sync", lambda e: e.dma_start(out=pp, in_=pp_d), writes=["pp"], dma_sem="c0")
        OP("sync", lambda e: e.dma_start(out=posi, in_=pos_d), writes=["posi"], dma_sem="c1")
        OP("sync", lambda e: e.dma_start(out=rows_sb, in_=rows_d), writes=["rows"], dma_sem="c2")
        OP("sync", lambda e: e.dma_start(out=mask32, in_=mask_d), writes=["mask32"], dma_sem="c3")
        OP("sync", lambda e: e.dma_start(out=wbd.rearrange("p g m -> p (g m)"), in_=wbd_d), writes=["wbd"], dma_sem="c4")
        win_v = win_d.rearrange("(k p) n -> p k n", p=128)

        def load_w(g, s):
            OP("gpsimd", (lambda e, g=g, s=s: e.dma_start(out=Wsl[s], in_=win_v[:, :, g * 512:(g + 1) * 512])),
               writes=[("W", s)], dma_sem="W%d" % s)

        OP("gpsimd", lambda e: e.memset(sm, 0.0), writes=["sm"])
        OP("gpsimd", lambda e: e.memset(smc(SM_ONE, 8), 1.0), reads=["sm"], writes=["sm_one"])
        OP("gpsimd", lambda e: e.memset(onesrow, 1.0), writes=["onesrow"])
        OP("gpsimd", lambda e: e.memset(ident32, 0.0), writes=["ident32"])
        OP("gpsimd", lambda e: e.affine_select(out=ident32, in_=ident32, pattern=[[-1, 128]],
                                                compare_op=ALU.not_equal, fill=1.0, base=0, channel_multiplier=1),
           reads=["ident32"], writes=["ident32"])
        OP("vector", lambda e: e.tensor_copy(out=ident, in_=ident32), reads=["ident32"], writes=["ident"])
        OP("gpsimd", lambda e: e.memset(onesb, 1.0), writes=["onesb"])
        OP("gpsimd", lambda e: e.memset(ones64, 1.0), writes=["ones64"])
        load_w(0, 0)
        load_w(1, 1)
        OP("vector", lambda e: e.tensor_copy(out=maskT, in_=mask32), reads=["mask32"], writes=["maskT"])
        OP("scalar", lambda e: e.activation(out=smc(SM_SC, 8), in_=ppc(PP_C, 8), func=AF.Silu),
           reads=["pp", "sm"], writes=["sc"])
        OP("scalar", lambda e: e.activation(out=smc(SM_SP, 4), in_=ppc(PP_LAM, 4), func=AF.Exp, scale=-1.0),
           reads=["pp", "sm"], writes=["sp"])
        OP("scalar", lambda e: e.activation(out=smc(SM_SP, 4), in_=smc(SM_SP, 4), func=AF.Ln, bias=1.0),
           reads=["sp"], writes=["sp"])
        OP("vector", lambda e: e.tensor_scalar(out=smc(SM_C8, 4), in0=smc(SM_SP, 4), scalar1=8.0, scalar2=None, op0=ALU.mult),
           reads=["sp", "sm"], writes=["c8"])
        OP("vector", lambda e: e.tensor_scalar(out=smc(SM_NC8, 4), in0=smc(SM_SP, 4), scalar1=-8.0, scalar2=None, op0=ALU.mult),
           reads=["sp", "sm"], writes=["nc8"])
        OP("vector", lambda e: e.tensor_scalar(out=smc(SM_N2C8, 4), in0=smc(SM_SP, 4), scalar1=-16.0, scalar2=None, op0=ALU.mult),
           reads=["sp", "sm"], writes=["n2c8"])
        OP("vector", lambda e: e.tensor_copy(out=smc(SM_POSF, NB), in_=posi), reads=["posi", "sm"], writes=["posf"])
        OP("vector", lambda e: e.tensor_tensor(
            out=ANG.rearrange("p (b f) -> p b f", b=NB),
            in0=view(smc(SM_POSF, NB), [[1, NB], [0, 32]]),
            in1=view(ppc(PP_INVF, 32), [[0, NB], [1, 32]]), op=ALU.mult),
           reads=["posf", "pp"], writes=["ANG"])
        C1 = 6.28125
        C2 = 2.0 * np.pi - 6.28125
        OP("vector", lambda e: e.tensor_scalar(out=KI, in0=ANG, scalar1=1.0 / (2 * PI), scalar2=None, op0=ALU.mult),
           reads=["ANG"], writes=["KI"])
        OP("vector", lambda e: e.tensor_copy(out=KF, in_=KI), reads=["KI"], writes=["KF"])
        OP("vector", lambda e: e.scalar_tensor_tensor(out=ANG, in0=KF, scalar=-C1, in1=ANG, op0=ALU.mult, op1=ALU.add),
           reads=["KF", "ANG"], writes=["ANG"])
        OP("vector", lambda e: e.scalar_tensor_tensor(out=ANG, in0=KF, scalar=-float(C2), in1=ANG, op0=ALU.mult, op1=ALU.add),
           reads=["KF", "ANG"], writes=["ANG"])

        def wrap(T):
            OP("vector", lambda e: e.tensor_scalar(out=KF, in0=T, scalar1=PI, scalar2=-2 * PI, op0=ALU.is_gt, op1=ALU.mult),
               reads=["ANG", "KF"], writes=["KF"])
            OP("vector", lambda e: e.tensor_tensor(out=T, in0=T, in1=KF, op=ALU.add), reads=["KF", "ANG"], writes=["ANG"])
            OP("vector", lambda e: e.tensor_scalar(out=KF, in0=T, scalar1=-PI, scalar2=2 * PI, op0=ALU.is_lt, op1=ALU.mult),
               reads=["ANG", "KF"], writes=["KF"])
            OP("vector", lambda e: e.tensor_tensor(out=T, in0=T, in1=KF, op=ALU.add), reads=["KF", "ANG"], writes=["ANG"])

        wrap(ANG)
        OP("scalar", lambda e: e.activation(out=SN, in_=ANG, func=AF.Sin), reads=["ANG"], writes=["SN"])
        sin3 = SN.rearrange("p (b f) -> p b f", b=NB)
        sinS3 = sinS.rearrange("p (b f) -> p b f", b=NB)
        cosF3 = cosF.rearrange("p (b f) -> p b f", b=NB)
        OP("vector", lambda e: e.tensor_scalar(out=sinS3[:, :, 0:32], in0=sin3, scalar1=-1.0, scalar2=None, op0=ALU.mult),
           reads=["SN"], writes=["sinS"])
        OP("vector", lambda e: e.tensor_copy(out=sinS3[:, :, 32:64], in_=sin3), reads=["SN"], writes=["sinS"])
        OP("vector", lambda e: e.tensor_scalar(out=ANG, in0=ANG, scalar1=PI / 2, scalar2=None, op0=ALU.add),
           reads=["ANG", "SN"], writes=["ANG"])
        wrap(ANG)
        OP("scalar", lambda e: e.activation(out=SN, in_=ANG, func=AF.Sin), reads=["ANG", "sinS"], writes=["SN"])
        OP("vector", lambda e: e.tensor_copy(out=cosF3[:, :, 0:32], in_=sin3), reads=["SN"], writes=["cosF"])
        OP("vector", lambda e: e.tensor_copy(out=cosF3[:, :, 32:64], in_=sin3), reads=["SN"], writes=["cosF"])

        o_xs = alloc(3 * D)
        o_xn = alloc(3 * D // 2)
        o_junk = alloc(D // 2)
        xs = [f32v(o_xs + s * D, D) for s in range(3)]
        xn = [b16v(o_xn + s * (D // 2), D) for s in range(3)]
        junk = b16v(o_junk, D)

        def wada_step(kc):
            s = kc % 3
            OP("sync", (lambda e: e.dma_start(out=wada_sl[s], in_=wada_d[kc * 128:(kc + 1) * 128, :])),
               writes=[("wada", s)], dma_sem="wada%d" % s)
            for n in range(6):
                OP("tensor", (lambda e, n=n: e.matmul(
                    bank(n, 0, 512, 0, 1), lhsT=smc(SM_SC + kc), rhs=wada_sl[s][:, n * 512:(n + 1) * 512],
                    start=(kc == 0), stop=(kc == 7))), reads=[("wada", s), "sc"], writes=[("pb", n)])

        def p1_a(blk):
            s = blk % 3
            r = blk % 3
            OP("sync", (lambda e: e.dma_start(out=xs[s], in_=x_d[blk * 128:(blk + 1) * 128, :])),
               writes=[("xs", s)], dma_sem="xs%d" % s)
            OP("scalar", (lambda e: e.activation(out=junk, in_=xs[s], func=AF.Square, accum_out=smc(SM_SS + blk))),
               reads=[("xs", s), "sm"], writes=["junk", ("ss", blk)])
            OP("scalar", (lambda e: e.activation(out=smc(SM_STD + blk), in_=smc(SM_SS + blk), func=AF.Sqrt,
                                                 scale=1.0 / D, bias=EPS)),
               reads=[("ss", blk)], writes=[("std", blk)])
            OP("vector", (lambda e: e.reciprocal(out=smc(SM_RSTD + blk), in_=smc(SM_STD + blk))),
               reads=[("std", blk)], writes=[("rstd", blk)])
            OP("vector", (lambda e: e.tensor_scalar(out=xn[r], in0=xs[s], scalar1=smc(SM_RSTD + blk), scalar2=None,
                                                    op0=ALU.mult)),
               reads=[("xs", s), ("rstd", blk)], writes=[("xn", r)])

        def p1_c(blk):
            r = blk % 3
            pb = 6 + (blk % 2)
            for kc in range(8):
                OP("tensor", (lambda e, kc=kc: e.transpose(out=bank16(pb)[:, kc * 128:(kc + 1) * 128],
                                                           in_=xn[r][:, kc * 128:(kc + 1) * 128], identity=ident)),
                   reads=[("xn", r), "ident"], writes=[("pb", pb)])
            OP("scalar", (lambda e: e.activation(out=hT[:, :, blk * 128:(blk + 1) * 128],
                                                 in_=bank16(pb).rearrange("p (k t) -> p k t", k=8), func=AF.Copy)),
               reads=[("pb", pb)], writes=[("hT", kc_, blk) for kc_ in range(8)])

        def p01(i):
            if i % 2 == 0 and i // 2 < 8:
                wada_step(i // 2)
            p1_a(i)

        pipeline(NB, [(0, p01), (1, p1_c)])
        OP("gpsimd", lambda e: e.dma_start(out=woutb, in_=wout_d.rearrange("(e p) n -> p e n", p=128)),
           writes=["wout"], dma_sem="c5", extra=[len(p.ops) - 1])
        OP("vector", lambda e: e.tensor_tensor(out=modrow, in0=ps[0:1, 0:3 * D], in1=rows_sb[:, 0:3 * D], op=ALU.add),
           reads=[("pb", n) for n in range(6)] + ["rows"], writes=["modrow"])
        for kc in range(8):
            for w in range(2):
                OP("tensor", (lambda e, kc=kc, w=w: e.matmul(
                    bank(6, w * 8 + kc, w * 8 + kc + 1), lhsT=modrow[:, w * D + kc * 128:w * D + (kc + 1) * 128],
                    rhs=sm[0:1, SM_ONE:SM_ONE + 1], start=True, stop=True)),
                   reads=["modrow", "sm_one"], writes=[("pb", 6)])
        OP("vector", lambda e: e.tensor_copy(out=smc(SM_SHIFT, 8), in_=bank(6, 0, 8)),
           reads=[("pb", 6), "sm"], writes=["shift"])
        OP("vector", lambda e: e.scalar_tensor_tensor(out=smc(SM_G1, 8), in0=bank(6, 8, 16), scalar=1.0,
                                                       in1=ppc(PP_NPRE, 8), op0=ALU.add, op1=ALU.mult),
           reads=[("pb", 6), "pp", "sm"], writes=["g1"])
        snapP1 = p.snapshot()
        for kc in range(8):
            OP("vector", (lambda e, kc=kc: e.tensor_scalar(out=hT[:, kc, :], in0=hT[:, kc, :], scalar1=smc(SM_G1 + kc),
                                                           scalar2=smc(SM_SHIFT + kc), op0=ALU.mult, op1=ALU.add)),
               reads=[("hT", kc, b) for b in range(NB)] + ["g1", "shift"], writes=[("hT", kc, b) for b in range(NB)])

        OP("vector", lambda e: e.tensor_tensor(out=gnrow, in0=modrow[:, 2 * D:3 * D], in1=rows_sb[:, 3 * D:4 * D], op=ALU.mult),
           reads=["modrow", "rows"], writes=["gnrow"])
        for h in range(2):
            OP("tensor", (lambda e, h=h: e.matmul(bank(h), lhsT=onesrow, rhs=gnrow[:, h * 512:(h + 1) * 512],
                                                  start=True, stop=True)),
               reads=["gnrow", "onesrow", "modrow"], writes=[("pb", h)])
            OP("vector", (lambda e, h=h: e.tensor_copy(out=gn_bc[:, h * 512:(h + 1) * 512], in_=bank(h))),
               reads=[("pb", h)], writes=["gn_bc"])

        p.fence(["tensor"], dep=snapP1)
        p.fence([e_ for e_ in ALL_E if e_ != "tensor"])
        cur[0] = RA
        TH_ = 1024
        o_xa = alloc(2 * 2052)
        o_sg = alloc(2 * S // 2)
        o_tA = alloc(6 * TH_ + TH_ + 512 + 512)
        o_sqa = alloc(4 * S // 2)
        o_tB2 = alloc(4096)
        o_sgt = alloc(2 * 512)
        SGT = [f32v(o_sgt + s_ * 512, 512) for s_ in range(2)]
        o_tB1 = o_yT + 4096
        XA = [f32v(o_xa + s * 2052, 2052) for s in range(2)]
        SGA = [b16v(o_sg + s * (S // 2), S) for s in range(2)]
        SQall = b16v(o_sqa, 4 * S).rearrange("p (c t) -> p c t", c=4)
        RBC = f32v(o_tB2 + 2048, S)
        for s_ in range(2):
            OP("gpsimd", (lambda e, s_=s_: e.memset(XA[s_][:, 0:4], 0.0)), writes=[("XA", s_)])
        pbi = [0]

        def fm_proj(s, cj, tc, evac, b=None):
            if b is None:
                b = pbi[0] % 4
                pbi[0] += 1
            for kc in range(8):
                OP("tensor", (lambda e, kc=kc, b=b: e.matmul(bank(b), lhsT=Wsl[s][:, kc, cj * 128:(cj + 1) * 128],
                                                            rhs=hT[:, kc, tc * 512:(tc + 1) * 512], start=(kc == 0), stop=(kc == 7))),
                   reads=[("W", s)] + [("hT", kc, 4 * tc + i) for i in range(4)], writes=[("pb", b)])
            evac(b)

        def inproj_mm(s_, cj):
            for tc in range(4):
                b = 4 + tc
                for kc in range(8):
                    OP("tensor", (lambda e, kc=kc, b=b, tc=tc: e.matmul(bank(b), lhsT=Wsl[s_][:, kc, cj * 128:(cj + 1) * 128],
                                                                        rhs=hT[:, kc, tc * 512:(tc + 1) * 512], start=(kc == 0), stop=(kc == 7))),
                       reads=[("W", s_)] + [("hT", kc, 4 * tc + i) for i in range(4)], writes=[("pb", b)])

        def evac_xa(cj):
            sl = cj % 2
            for tc in range(4):
                OP("scalar", (lambda e, tc=tc: e.activation(out=XA[sl][:, 4 + tc * 512:4 + (tc + 1) * 512], in_=bank(4 + tc), func=AF.Copy)),
                   reads=[("pb", 4 + tc)], writes=[("XA", sl)])

        def evac_ga(cj):
            sl = cj % 2
            for tc in range(4):
                OP("scalar", (lambda e, tc=tc: e.activation(out=SGT[tc % 2], in_=bank(4 + tc), func=AF.Sigmoid)),
                   reads=[("pb", 4 + tc)], writes=[("SGT", tc % 2)])
                OP("vector", (lambda e, tc=tc: e.tensor_tensor(out=SGA[sl][:, tc * 512:(tc + 1) * 512], in0=bank(4 + tc), in1=SGT[tc % 2], op=ALU.mult)),
                   reads=[("pb", 4 + tc), ("SGT", tc % 2)], writes=[("SGA", sl)])

        tb = [
            [f32v(o_tA + i * TH_, TH_) for i in range(7)] + [b16v(o_tA + 7 * TH_, TH_), b16v(o_tA + 7 * TH_ + 512, TH_)],
            [f32v(o_tB1 + i * TH_, TH_) for i in range(4)] + [f32v(o_tB2 + i * TH_, TH_) for i in range(3)]
            + [b16v(o_tB2 + 3 * TH_, TH_), b16v(o_tB2 + 3 * TH_ + 512, TH_)],
        ]

        def rec_conv0(it):
            cj, hh = it // 2, it % 2
            par = it % 2
            XC = tb[par][0]
            t0 = hh * TH_
            X = XA[cj % 2]
            OP("vector", (lambda e: e.tensor_scalar(out=XC, in0=X[:, 4 + t0:4 + t0 + TH_], scalar1=ppc(PP_CONVW + cj * 4 + 3),
                                                    scalar2=ppc(PP_CONVB + cj), op0=ALU.mult, op1=ALU.add)),
               reads=[("XA", cj % 2), "pp"], writes=[("XC", par)])

        def rec_conv(it):
            cj, hh = it // 2, it % 2
            par = it % 2
            XC, RR, II, TT, AA, A2, CT, XCB, SQ = tb[par]
            K = lambda n: (n, par)
            t0 = hh * TH_
            X = XA[cj % 2]
            cw = lambda k: ppc(PP_CONVW + cj * 4 + k)
            for k in (2, 1, 0):
                OP("vector", (lambda e, k=k: e.scalar_tensor_tensor(out=XC, in0=X[:, 1 + k + t0:1 + k + t0 + TH_], scalar=cw(k), in1=XC,
                                                                    op0=ALU.mult, op1=ALU.add)),
                   reads=[("XA", cj % 2), "pp", K("XC")], writes=[K("XC")])
            for g in range(2):
                for q in range(2):
                    b = g * 2 + q
                    OP("tensor", (lambda e, g=g, q=q, b=b: e.matmul(bank(b), lhsT=wbd[:, g * 4 + cj, :], rhs=XC[:, q * 512:(q + 1) * 512],
                                                                   start=True, stop=True)),
                       reads=[K("XC"), "wbd"], writes=[("pb", b)])

        def rec_act(it, evac=None):
            cj, hh = it // 2, it % 2
            par = it % 2
            XC, RR, II, TT, AA, A2, CT, XCB, SQ = tb[par]
            K = lambda n: (n, par)
            OP("scalar", (lambda e: e.activation(out=RR, in_=ps[:, 0:1024], func=AF.Sigmoid, bias=ppc(PP_BA + cj))),
               reads=[("pb", 0), ("pb", 1), "pp"], writes=[K("RR")])
            OP("scalar", (lambda e: e.activation(out=II, in_=ps[:, 1024:2048], func=AF.Sigmoid, bias=ppc(PP_BX + cj))),
               reads=[("pb", 2), ("pb", 3), "pp"], writes=[K("II")])
            OP("scalar", (lambda e: e.activation(out=TT, in_=RR, func=AF.Tanh, scale=smc(SM_C8 + cj))),
               reads=[K("RR"), "c8"], writes=[K("TT")])
            if evac is not None:
                evac()
            OP("scalar", (lambda e: e.activation(out=AA, in_=RR, func=AF.Exp, scale=smc(SM_NC8 + cj))),
               reads=[K("RR"), "nc8"], writes=[K("AA")])
            OP("scalar", (lambda e: e.activation(out=A2, in_=RR, func=AF.Exp, scale=smc(SM_N2C8 + cj))),
               reads=[K("RR"), "n2c8"], writes=[K("A2")])

        def rec_s2(it):
            cj, hh = it // 2, it % 2
            par = it % 2
            XC, RR, II, TT, AA, A2, CT, XCB, SQ = tb[par]
            K = lambda n: (n, par)
            t0 = hh * TH_
            OP("vector", lambda e: e.scalar_tensor_tensor(out=A2, in0=A2, scalar=1.0, in1=TT, op0=ALU.add, op1=ALU.mult),
               reads=[K("A2"), K("TT")], writes=[K("A2")])
            OP("scalar", lambda e: e.activation(out=A2, in_=A2, func=AF.Ln), reads=[K("A2")], writes=[K("A2")])
            OP("scalar", lambda e: e.activation(out=A2, in_=A2, func=AF.Exp, scale=0.5), reads=[K("A2")], writes=[K("A2")])

        def rec_s2b(it):
            cj, hh = it // 2, it % 2
            par = it % 2
            XC, RR, II, TT, AA, A2, CT, XCB, SQ = tb[par]
            K = lambda n: (n, par)
            t0 = hh * TH_
            OP("vector", lambda e: e.tensor_tensor(out=TT, in0=II, in1=XC, op=ALU.mult), reads=[K("II"), K("XC"), K("TT")], writes=[K("TT")])
            OP("vector", lambda e: e.tensor_tensor(out=TT, in0=TT, in1=A2, op=ALU.mult), reads=[K("TT"), K("A2")], writes=[K("TT")])
            init = 0.0 if hh == 0 else smc(SM_CARRY + cj)
            OP("vector", (lambda e: e.tensor_tensor_scan(out=II, data0=AA, data1=TT, initial=init, op0=ALU.mult, op1=ALU.add)),
               reads=[K("AA"), K("TT"), K("II"), ("carry", cj)], writes=[K("II")])
            if hh == 0:
                OP("vector", lambda e: e.tensor_copy(out=smc(SM_CARRY + cj), in_=II[:, TH_ - 1:TH_]),
                   reads=[K("II"), "sm"], writes=[("carry", cj)])

        def rec_s2c(it):
            cj, hh = it // 2, it % 2
            par = it % 2
            XC, RR, II, TT, AA, A2, CT, XCB, SQ = tb[par]
            K = lambda n: (n, par)
            t0 = hh * TH_
            OP("gpsimd", (lambda e: e.tensor_tensor(out=yT[:, cj, t0:t0 + TH_], in0=II, in1=SGA[cj % 2][:, t0:t0 + TH_], op=ALU.mult)),
               reads=[K("II"), ("SGA", cj % 2)], writes=[("yT", cj)])
            OP("gpsimd", (lambda e: e.tensor_tensor(out=SQall[:, cj, t0:t0 + TH_], in0=yT[:, cj, t0:t0 + TH_], in1=yT[:, cj, t0:t0 + TH_],
                                                    op=ALU.mult)),
               reads=[("yT", cj)], writes=[("SQall", cj)])

        V1 = b16v(RA, NB * 512).rearrange("p (b h m) -> p b h m", b=NB, h=8)

        def v_mm(batch):
            for q in range(4):
                blk = 4 * batch + q
                b = 4 + q
                for kc in range(8):
                    OP("tensor", (lambda e, kc=kc, b=b, blk=blk: e.matmul(bank(b), lhsT=hT[:, kc, blk * 128:(blk + 1) * 128],
                                                                           rhs=Wsl[0][:, kc, :], start=(kc == 0), stop=(kc == 7))),
                       reads=[("W", 0), ("hT", kc, blk)], writes=[("pb", b)])

        def v_evac(batch, dep):
            for q in range(4):
                blk = 4 * batch + q
                b = 4 + q
                OP("scalar", (lambda e, blk=blk, b=b: e.activation(out=V1[:, blk].rearrange("p h m -> p (h m)"), in_=bank(b), func=AF.Copy)),
                   reads=[("pb", b)], writes=[("V", blk)], extra=dep)

        xa_dead = {}
        inproj_mm(0, 0)
        evac_xa(0)
        inproj_mm(1, 0)
        evac_ga(0)
        rec_conv0(0)
        rec_conv(0)
        for t in range(9):
            cjn = t // 2 + 1
            ev = None
            if t < 8 and cjn <= 3:
                inproj_mm(t % 2, cjn)
                ev = (lambda cjn=cjn: evac_xa(cjn)) if t % 2 == 0 else (lambda cjn=cjn: evac_ga(cjn))
                if cjn == 3:
                    load_w(4 if t % 2 == 0 else 3, t % 2)
            if t >= 1:
                rec_s2(t - 1)
            if t >= 1:
                rec_s2b(t - 1)
            if t < 8:
                rec_act(t, ev)
            if t in (7, 8):
                v_evac(t - 7, xa_dead[0])
            if t + 1 < 8:
                rec_conv0(t + 1)
                rec_conv(t + 1)
                if t + 1 == 5:
                    xa_dead[0] = p.snapshot()
                if t + 1 == 7:
                    xa_dead[1] = p.snapshot()
            if t in (6, 7, 8):
                v_mm(t - 6)
            if t >= 1:
                rec_s2c(t - 1)
        v_evac(2, xa_dead[1])
        v_mm(3)
        v_evac(3, xa_dead[1])
        snap3 = p.snapshot()
        tail_last = [None]

        def rec_tail():
            for q in (3, 0, 1, 2):
                for cj in range(4):
                    OP("tensor", (lambda e, q=q, cj=cj: e.matmul(bank(q), lhsT=onesb, rhs=SQall[:, cj, q * 512:(q + 1) * 512],
                                                                 start=(cj == 0), stop=(cj == 3))),
                       reads=[("SQall", cj), "onesb"], writes=[("pb", q)])
            for q in (3, 0, 1, 2):
                OP("scalar", (lambda e, q=q: e.activation(out=RBC[:, q * 512:(q + 1) * 512], in_=bank(q), func=AF.Ln, scale=1.0 / 512, bias=EPS)),
                   reads=[("pb", q)], writes=[("RBCq", q)])
            OP("scalar", lambda e: e.activation(out=RBC, in_=RBC, func=AF.Exp, scale=-0.5), reads=[("RBCq", q) for q in range(4)], writes=["RBC"])
            for cj in range(4):
                tail_last[0] = OP("vector", (lambda e, cj=cj: e.scalar_tensor_tensor(out=yT[:, cj, :], in0=yT[:, cj, :], scalar=ppc(PP_NREC + cj),
                                                                                     in1=RBC, op0=ALU.mult, op1=ALU.mult)),
                                  reads=["RBC", ("yT", cj), "pp"], writes=[("yT", cj)])

        p.fence(ALL_E, dep=snap3)
        cur[0] = RA
        o_V = alloc(NB * 512 // 2)
        assert o_V == RA
        o_KT = alloc(4 * S // 2)
        o_QT = alloc(4 * S // 2)
        o_V2 = alloc(NB * 512 // 2)
        o_sgb = alloc(4 * S // 2)
        o_rt = o_yT + 4096
        QT = b16v(o_QT, 4 * S).rearrange("p (j t) -> p j t", j=4)
        KT = b16v(o_KT, 4 * S).rearrange("p (j t) -> p j t", j=4)
        V2 = b16v(o_V2, NB * 512).rearrange("p (b h m) -> p b h m", b=NB, h=8)
        o_spare = cur[0]
        V3h = [b16v(o_cs, 8 * 512).rearrange("p (b h m) -> p b h m", b=8, h=8),
               b16v(o_spare, 8 * 512).rearrange("p (b h m) -> p b h m", b=8, h=8)]
        sgb = b16v(o_sgb, 4 * S).rearrange("p (j t) -> p j t", j=4)
        T1 = [f32v(o_rt + s * 512, 512) for s in range(3)]
        T2 = [f32v(o_rt + 1536 + s * 512, 512) for s in range(3)]
        QR = [b16v(o_rt + 3072 + s * 256, 512) for s in range(3)]
        load_w(2, 0)
        it = [0]

        def qk_a(i):
            if i == 3:
                rec_tail()
            if i == 4:
                OP("sync", lambda e: e.dma_start(out=vscr_d.rearrange("(b p) e -> p b e", p=128), in_=V1.rearrange("p b h m -> p b (h m)")),
                   reads=[("V", b) for b in range(NB)], writes=["vscr"], dma_sem="vs0")
                OP("sync", lambda e: e.dma_start(out=V2.rearrange("p (n r) h m -> p n (r h m)", n=4),
                                                 in_=vscr_d.rearrange("(n i r) e -> i n (r e)", n=4, r=4)),
                   reads=["vscr"], writes=["V2"], dma_sem="vs1", extra=[tail_last[0]])
            if i == NB + 2:
                load_w(5, 1)
            s, blk = 1 - i // NB, i % NB
            b = i % 4
            r = i % 3
            for kc in range(8):
                OP("tensor", (lambda e, kc=kc: e.matmul(bank(b), lhsT=hT[:, kc, blk * 128:(blk + 1) * 128],
                                                        rhs=Wsl[s][:, kc, :], start=(kc == 0), stop=(kc == 7))),
                   reads=[("W", s), ("hT", kc, blk)], writes=[("pb", b)])
            pbk = bank(b)
            cos_b = view(cosF[:, blk * 64:blk * 64 + 1], [[0, 8], [1, 64]])
            sin_b = view(sinS[:, blk * 64:blk * 64 + 1], [[0, 8], [32, 2], [1, 32]])
            swp = view(pbk[:, 32:33], [[64, 8], [-32, 2], [1, 32]])
            OP("vector", (lambda e: e.tensor_tensor(
                out=T1[r].rearrange("p (h d) -> p h d", h=8), in0=pbk.rearrange("p (h d) -> p h d", h=8), in1=cos_b, op=ALU.mult)),
               reads=[("pb", b), "cosF"], writes=[("T1", r)])
            OP("vector", (lambda e: e.tensor_tensor(
                out=T2[r].rearrange("p (h s d) -> p h s d", h=8, s=2), in0=swp, in1=sin_b, op=ALU.mult)),
               reads=[("pb", b), "sinS"], writes=[("T2", r)])
            OP("gpsimd", (lambda e: e.tensor_tensor(out=QR[r], in0=T1[r], in1=T2[r], op=ALU.add)),
               reads=[("T1", r), ("T2", r)], writes=[("QR", r)])

        def qk_c(i):
            s, blk = 1 - i // NB, i % NB
            r = i % 3
            tb_ = 4 + (i % 4)
            dstT = QT if s == 0 else KT
            for j in range(4):
                OP("tensor", (lambda e, j=j: e.transpose(out=bank16(tb_)[:, j * 128:(j + 1) * 128],
                                                         in_=QR[r][:, j * 128:(j + 1) * 128], identity=ident)),
                   reads=[("QR", r), "ident"], writes=[("pb", tb_)])
            OP("scalar", (lambda e: e.activation(
                out=dstT[:, :, blk * 128:(blk + 1) * 128], in_=bank16(tb_)[:, 0:512].rearrange("p (j t) -> p j t", j=4), func=AF.Copy)),
               reads=[("pb", tb_)], writes=[("QKT", s, blk)])

        pipeline(2 * NB, [(0, qk_a), (2, qk_c)])
        vsrc3 = vscr_d.rearrange("(i r) e -> i r e", r=16)
        for hf in range(2):
            OP("sync", (lambda e, hf=hf: e.dma_start(out=V3h[hf].rearrange("p b h m -> p b (h m)"), in_=vsrc3[:, 8 * hf:8 * hf + 8, :])),
               reads=["vscr"], writes=["V3", "cosF", "sinS"], dma_sem="vs2", extra=[tail_last[0]])
        for cj in range(4):
            for tc in range(4):
                fm_proj(1, cj, tc, lambda b, cj=cj, tc=tc: OP(
                    "scalar", (lambda e: e.activation(out=sgb[:, cj, tc * 512:(tc + 1) * 512], in_=bank(b), func=AF.Silu)),
                    reads=[("pb", b)], writes=[("sgb", cj)], extra=[tail_last[0]]))

        p.fence(ALL_E)
        cur[0] = o_hT
        o_P = alloc(5 * 512)
        o_LD = alloc(2 * 512)
        o_TN = alloc(512)
        o_YB = alloc(4 * 512)
        o_SQ = alloc(512)
        o_SQS = alloc(512)
        o_SQB = alloc(256)
        o_RB = alloc(512)
        assert cur[0] <= o_W
        Pt = [b16v(o_P + s * 512, 1024).rearrange("p (a n) -> p a n", a=2) for s in range(5)]
        LD = [f32v(o_LD + s * 512, 512) for s in range(2)]
        RD = LD
        TN = [f32v(o_TN, 512)] * 2
        YB4 = [f32v(o_YB + s * 512, 512) for s in range(4)]
        SQ4 = f32v(o_SQ, 512)
        SQS = f32v(o_SQS, 512)
        SQB = b16v(o_SQB, 512)
        RB = f32v(o_RB, 512)
        TRI_LE = maskT[:, 0:128]
        TRI_GE = maskT[:, 128:256]
        items = [(c, j, k) for c in range(4) for j in range(4) for k in range(5) if not (c == 0 and k == 3)]

        def s_specs(c, j, a, kind):
            rows = slice(a * 64, (a + 1) * 64)
            out = []
            if kind in (0, 1):
                for u in range(4):
                    qb = 4 * c + u
                    kb = qb - kind
                    if kb < 0:
                        continue
                    out.append((KT[rows, j, 128 * kb:128 * kb + 128], QT[rows, j, 128 * qb:128 * qb + 128], 128 * u, 128 * u + 128, 128))
            elif kind in (2, 3):
                ck = c - (kind - 2)
                for r in range(4):
                    out.append((KT[rows, j, 512 * ck + r:512 * (ck + 1):4], QT[rows, j, 512 * c + r:512 * (c + 1):4],
                                128 * r, 128 * r + 128, 128))
            else:
                Mk = 128
                for r in range(16):
                    out.append((KT[rows, j, r:S:16], QT[rows, j, 512 * c + r:512 * (c + 1):16], 32 * r, 32 * r + 32, Mk))
            return out

        def at_a(i):
            c, j, kind = items[i]
            sb_ = (i % 2) * 2
            pt = i % 5
            Pn = 128
            sp = [s_specs(c, j, a, kind) for a in range(2)]
            for idx in range(len(sp[0])):
                for a in range(2):
                    (lh, rh, lo, hi, M) = sp[a][idx]
                    OP("tensor", (lambda e, lh=lh, rh=rh, lo=lo, hi=hi, M=M, a=a: e.matmul(
                        bank(sb_ + a, lo, hi, 0, M), lhsT=lh, rhs=rh, start=True, stop=True)),
                       reads=[("QKT", 0, 4 * c + u) for u in range(4)] + [("QKT", 1, u) for u in range(4 * c + 4)], writes=[("pb", sb_ + a)])
            sview = view(ps[0:Pn, sb_ * 512:sb_ * 512 + 1], [[512, 2], [1, 512]])
            OP("scalar", (lambda e: e.activation(out=Pt[pt][0:Pn], in_=sview, func=AF.Exp, scale=0.125)),
               reads=[("pb", sb_), ("pb", sb_ + 1)], writes=[("P", pt)])
            if kind == 4:
                mview = view(TRI_LE[0:Pn, 32 * c:32 * c + 1], [[0, 2], [0, 16], [1, 32]])
                pv = Pt[pt][0:Pn].rearrange("p a (r i) -> p a r i", r=16)
            else:
                mview = view((TRI_LE if kind in (0, 2) else TRI_GE)[:, 0:1], [[0, 2], [0, 4], [1, 128]])
                pv = Pt[pt].rearrange("p a (u q) -> p a u q", u=4)
            OP("vector", (lambda e: e.tensor_tensor(out=pv, in0=pv, in1=mview, op=ALU.mult)),
               reads=[("P", pt), "maskT"], writes=[("P", pt)])

        def at_c(i):
            c, j, kind = items[i]
            pt = i % 5
            pair = c * 4 + j
            par = pair % 2
            bN, bD = 4 + 2 * par, 5 + 2 * par
            Pn = 128
            mms = []
            for a in range(2):
                h = 2 * j + a
                r0, r1 = a * 64, (a + 1) * 64
                mm = []
                if kind in (0, 1):
                    us = [u for u in range(4) if 4 * c + u - kind >= 0]
                    for u in us:
                        kb = 4 * c + u - kind
                        mm.append((bank(bN, 128 * u, 128 * u + 128, r0, r1), V1[:, kb, h, :], Pt[pt][:, a, 128 * u:128 * u + 128]))
                    lo = 128 * us[0]
                    mm.append((bank(bD, lo, 512, r0, r1), None, Pt[pt][:, a, lo:512]))
                elif kind in (2, 3):
                    ck = c - (kind - 2)
                    for r in range(4):
                        mm.append((bank(bN, r, 512, r0, r1)[:, ::4], V2[:, 4 * ck + r, h, :], Pt[pt][:, a, 128 * r:128 * r + 128]))
                        mm.append((bank(bD, r, 512, r0, r1)[:, ::4], None, Pt[pt][:, a, 128 * r:128 * r + 128]))
                else:
                    for r in range(16):
                        mm.append((bank(bN, r, 512, r0, r1)[:, ::16], V3h[r // 8][0:Pn, r % 8, h, :], Pt[pt][0:Pn, a, 32 * r:32 * r + 32]))
                        mm.append((bank(bD, r, 512, r0, r1)[:, ::16], None, Pt[pt][0:Pn, a, 32 * r:32 * r + 32]))
                mms.append(mm)
            first = [[kind == 0, kind == 0], [kind == 0, kind == 0]]
            for idx in range(len(mms[0])):
                for a in range(2):
                    (of, lh, rh) = mms[a][idx]
                    if lh is not None:
                        OP("tensor", (lambda e, of=of, lh=lh, rh=rh, st=first[a][0]: e.matmul(of, lhsT=lh, rhs=rh, start=st, stop=False,
                                                                                            skip_group_check=True)),
                           reads=[("P", pt), "V1", "V2", "V3"], writes=[("pb", bN)])
                        first[a][0] = False
                    else:
                        OP("tensor", (lambda e, of=of, rh=rh, st=first[a][1]: e.matmul(of, lhsT=ones64[0:Pn, :], rhs=rh, start=st, stop=False,
                                                                                      skip_group_check=True)),
                           reads=[("P", pt), "ones64"], writes=[("pb", bD)])
                        first[a][1] = False
            if kind != 4:
                return
            K = lambda n: (n, par)
            OP("scalar", lambda e: e.activation(out=LD[par], in_=bank(bD), func=AF.Ln), reads=[("pb", bD)], writes=[K("LD"), K("RD")])
            OP("scalar", lambda e: e.activation(out=RD[par], in_=LD[par], func=AF.Exp, scale=-1.0), reads=[K("LD")], writes=[K("RD")])
            OP("vector", lambda e: e.tensor_tensor(out=TN[par], in0=bank(bN), in1=RD[par], op=ALU.mult),
               reads=[("pb", bN), K("RD")], writes=["TN"])
            OP("gpsimd", (lambda e: e.tensor_tensor(out=YB4[j], in0=TN[par], in1=sgb[:, j, c * 512:(c + 1) * 512], op=ALU.mult)),
               reads=["TN", ("sgb", j)], writes=[("YB4", j)])
            if j == 0:
                OP("gpsimd", lambda e: e.tensor_tensor(out=SQS, in0=YB4[j], in1=YB4[j], op=ALU.mult), reads=[("YB4", j)], writes=["SQS"])
            else:
                OP("gpsimd", lambda e: e.tensor_tensor(out=SQ4, in0=YB4[j], in1=YB4[j], op=ALU.mult), reads=[("YB4", j)], writes=["SQ4"])
                OP("gpsimd", lambda e: e.tensor_tensor(out=SQS, in0=SQS, in1=SQ4, op=ALU.add), reads=["SQS", "SQ4"], writes=["SQS"])

        late_d = []
        late_parts = []

        def at_d0(i):
            c, j, kind = items[i]
            if kind == 4 and j == 3:
                OP("vector", lambda e: e.tensor_copy(out=SQB, in_=SQS), reads=["SQS"], writes=["SQB"])

        def at_d(i, force=False):
            c, j, kind = items[i]
            if kind != 4 or j != 3:
                return
            if i == len(items) - 1 and not force:
                late_d.append(i)
                return
            pair = c * 4 + j
            par = pair % 2
            bN, bD = 4 + 2 * par, 5 + 2 * par
            OP("tensor", (lambda e: e.matmul(bank(bD), lhsT=onesb, rhs=SQB, start=True, stop=True)),
               reads=["SQB", "onesb"], writes=[("pb", bD)])
            def stt(jj):
                OP("vector", (lambda e: e.scalar_tensor_tensor(out=yT[:, 4 + jj, c * 512:(c + 1) * 512], in0=YB4[jj],
                                                               scalar=ppc(PP_NATT + jj), in1=RB, op0=ALU.mult, op1=ALU.mult)),
                   reads=["RB", ("YB4", jj), "pp"], writes=[("yT", 4 + jj)])

            if force:
                OP("scalar", lambda e: e.activation(out=RB, in_=bank(bD), func=AF.Sqrt, scale=1.0 / 512, bias=EPS),
                   reads=[("pb", bD)], writes=["RB"])
                late_parts.append(lambda: OP("vector", lambda e: e.reciprocal(out=RB, in_=RB), reads=["RB"], writes=["RB"]))
                late_parts.append(lambda: (stt(0), stt(1)))
                late_parts.append(lambda: (stt(2), stt(3)))
                return
            OP("scalar", lambda e: e.activation(out=RB, in_=bank(bD), func=AF.Ln, scale=1.0 / 512, bias=EPS),
               reads=[("pb", bD)], writes=["RB"])
            OP("scalar", lambda e: e.activation(out=RB, in_=RB, func=AF.Exp, scale=-0.5), reads=["RB"], writes=["RB"])
            for jj in range(4):
                stt(jj)

        pipeline(len(items), [(0, at_a), (3, at_c), (5, at_d0), (7, at_d)])

        last_pe = max(i_ for i_, o_ in enumerate(p.ops) if o_["eng"] == "tensor" and o_["fn"] is not None)
        p.fence(ALL_E, dep={last_pe})
        cur[0] = RA
        o_xs5 = alloc(4 * D)
        o_t5 = alloc(4 * D)
        o_j5 = alloc(D // 2)
        xs5 = [f32v(o_xs5 + s * D, D) for s in range(4)]
        T5 = [f32v(o_t5 + s * D, D) for s in range(4)]
        junk5 = b16v(o_j5, D)
        stores = []

        def p5_load(blk):
            s = blk % 4
            OP("sync", (lambda e: e.dma_start(out=xs5[s], in_=x_d[blk * 128:(blk + 1) * 128, :])),
               writes=[("xs5", s)], dma_sem="xr%d" % s)

        def p5_main(blk):
            s = blk % 4
            b0 = (blk % 2) * 2
            for h in range(2):
                for ec in range(8):
                    OP("tensor", (lambda e, h=h, ec=ec: e.matmul(
                        bank(b0 + h), lhsT=yT[:, ec, blk * 128:(blk + 1) * 128], rhs=woutb[:, ec, h * 512:(h + 1) * 512],
                        start=(ec == 0), stop=(ec == 7))),
                       reads=[("yT", ec), "wout"], writes=[("pb", b0 + h)])
            mix = ps[:, b0 * 512:b0 * 512 + 1024]
            OP("scalar", (lambda e: e.activation(out=junk5, in_=mix, func=AF.Square, accum_out=smc(SM_SSP + blk))),
               reads=[("pb", b0), ("pb", b0 + 1), "sm"], writes=["junk5", ("ssp", blk)])
            OP("scalar", (lambda e: e.activation(out=smc(SM_STDP + blk), in_=smc(SM_SSP + blk), func=AF.Sqrt,
                                                 scale=1.0 / D, bias=EPS)),
               reads=[("ssp", blk)], writes=[("stdp", blk)])
            OP("vector", (lambda e: e.reciprocal(out=smc(SM_RSTDP + blk), in_=smc(SM_STDP + blk))),
               reads=[("stdp", blk)], writes=[("rstdp", blk)])
            OP("vector", (lambda e: e.scalar_tensor_tensor(out=T5[s], in0=mix, scalar=smc(SM_RSTDP + blk), in1=gn_bc,
                                                           op0=ALU.mult, op1=ALU.mult)),
               reads=[("pb", b0), ("pb", b0 + 1), ("rstdp", blk), "gn_bc"], writes=[("T5", s)])
            OP("gpsimd", (lambda e: e.tensor_tensor(out=T5[s], in0=T5[s], in1=xs5[s], op=ALU.add)),
               reads=[("T5", s), ("xs5", s)], writes=[("T5", s)])
            stores.append(OP("sync", (lambda e: e.dma_start(out=out_d[blk * 128:(blk + 1) * 128, :], in_=T5[s])),
                             reads=[("T5", s)], dma_sem="st%d" % s))
            if blk == 4:
                at_d(late_d[0], force=True)
            if blk in (5, 6, 7):
                late_parts[blk - 5]()

        pipeline(NB, [(0, p5_load), (2, p5_main)])
        OP("sync", None, extra=stores)
        p.emit()
    return nc


_CACHE = {}


def _c_mult(j):
    c = np.zeros_like(j, dtype=np.float32)
    c += ((j >= 0) & (j <= 128))
    c += ((j >= 0) & (j % 4 == 0) & (j <= 512))
    c += ((j >= 0) & (j % 16 == 0))
    return c.astype(np.float32)


def kernel(x, c, positions, w_ada, b_ada, norm_pre, norm_post, w_in, conv_w, conv_b,
           w_rg_a, b_rg_a, w_rg_x, b_rg_x, lru_lambda, norm_rec, norm_att, w_out):
    x = np.asarray(x, np.float32)
    B = x.shape[0]
    if "nc" not in _CACHE:
        _CACHE["nc"] = build_program()
    nc = _CACHE["nc"]
    f = lambda a: np.ascontiguousarray(np.asarray(a, np.float32))
    col = lambda v, n: f(v).reshape(n, 128).T
    rows = np.concatenate([f(b_ada)[0], f(norm_post)[0]])[None, :]
    wbd = np.zeros((128, 2, 4, 128), np.float32)
    for g, w in enumerate((f(w_rg_a)[0], f(w_rg_x)[0])):
        for cj in range(4):
            wbd[0:64, g, cj, 0:64] = w[2 * cj]
            wbd[64:128, g, cj, 64:128] = w[2 * cj + 1]
    wbd = wbd.reshape(128, 1024)
    pidx = np.arange(128)[:, None]
    xidx = np.arange(S)[None, :]
    q128 = np.arange(128)[None, :]
    maskT = np.concatenate([(pidx <= q128), (pidx >= q128)], axis=1).astype(np.float32)
    half = 32
    invf = (10000.0 ** (-np.arange(half, dtype=np.float32) / half)).astype(np.float32)
    pp_shared = np.zeros((128, NPP), np.float32)
    pp_shared[:, PP_NPRE:PP_NPRE + 8] = col(norm_pre[0], 8)
    pp_shared[:, PP_CONVW:PP_CONVW + 16] = f(conv_w)[0].reshape(4, 4, 128).transpose(2, 1, 0).reshape(128, 16)
    pp_shared[:, PP_CONVB:PP_CONVB + 4] = col(conv_b[0], 4)
    pp_shared[:, PP_BA:PP_BA + 4] = col(b_rg_a[0], 4)
    pp_shared[:, PP_BX:PP_BX + 4] = col(b_rg_x[0], 4)
    pp_shared[:, PP_LAM:PP_LAM + 4] = col(lru_lambda[0], 4)
    pp_shared[:, PP_NREC:PP_NREC + 4] = col(norm_rec[0], 4)
    pp_shared[:, PP_NATT:PP_NATT + 4] = col(norm_att[0], 4)
    pp_shared[:, PP_INVF:PP_INVF + 32] = invf[None, :]
    w_ada2 = f(w_ada)[0]
    w_in2 = f(w_in)[0]
    w_out2 = f(w_out)[0]
    in_maps = []
    for b in range(B):
        pp = pp_shared.copy()
        pp[:, PP_C:PP_C + 8] = col(np.asarray(c)[b], 8)
        pos = np.ascontiguousarray(np.asarray(positions)[b].astype(np.int32).reshape(NB, 128).T)
        in_maps.append({"x": np.ascontiguousarray(x[b]), "pp": pp, "pos": pos, "w_ada": w_ada2, "rows": rows,
                        "w_in": w_in2, "w_out": w_out2, "wbd": wbd, "maskT": maskT})
    res = run_bass_kernel_spmd(nc, in_maps, core_ids=list(range(B)))
    return np.stack([np.asarray(r["out"], np.float32) for r in res.results], axis=0)
```

```python
import contextlib
import numpy as np
import concourse.bass as bass
import concourse.mybir as mybir
from concourse.ap import AP
from concourse.bass_utils import run_bass_kernel_spmd

F32 = mybir.dt.float32
BF16 = mybir.dt.bfloat16
I32 = mybir.dt.int32
ALU = mybir.AluOpType
AF = mybir.ActivationFunctionType

D = 1024
S = 2048
NB = 16
EPS = 1e-6
PI = float(np.pi)
ENGS = ("tensor", "scalar", "vector", "gpsimd", "sync")
ALL_E = ENGS


class Prog:
    def __init__(self, nc):
        self.nc = nc
        self.ops = []
        self.touch = {}

    def op(self, eng, fn, reads=(), writes=(), dma_sem=None, ndma=1, extra=()):
        i = len(self.ops)
        self.ops.append(dict(eng=eng, fn=fn, reads=tuple(reads), writes=tuple(writes),
                             dma_sem=dma_sem, ndma=ndma, extra=set(extra)))
        for k in tuple(reads) + tuple(writes):
            self.touch.setdefault(k, set()).add(i)
        return i

    def snapshot(self):
        last = {}
        for i, o in enumerate(self.ops):
            if o["fn"] is None:
                continue
            last[("d", o["dma_sem"]) if o["dma_sem"] else ("e", o["eng"])] = i
        return set(last.values())

    def fence(self, engs, keys=None, dep=None):
        if dep is None:
            dep = self.snapshot()
        for e in engs:
            self.op(e, None, extra=dep)

    def emit(self):
        nc = self.nc
        ops = self.ops
        last_writer = {}
        readers = {}
        for i, o in enumerate(ops):
            deps = set(o["extra"])
            for k in o["reads"]:
                if k in last_writer:
                    deps.add(last_writer[k])
            for k in o["writes"]:
                if k in last_writer:
                    deps.add(last_writer[k])
                for r in readers.get(k, ()):
                    deps.add(r)
            deps.discard(i)
            deps = {d for d in deps if ops[d]["fn"] is not None}
            if o["eng"] == "tensor":
                deps = {d for d in deps if ops[d]["eng"] != "tensor"}
            o["deps"] = deps
            for k in o["reads"]:
                readers.setdefault(k, []).append(i)
            for k in o["writes"]:
                last_writer[k] = i
                readers[k] = []
        signal = set()
        for o in ops:
            signal |= o["deps"]
        counts = {}
        sem_names = []
        for i, o in enumerate(ops):
            o["sem"] = None
            if o["fn"] is None:
                continue
            if o["dma_sem"] is not None:
                name = "d_" + o["dma_sem"]
                counts[name] = counts.get(name, 0) + 16 * o["ndma"]
                o["sem"], o["val"] = name, counts[name]
            elif i in signal:
                name = "e_" + o["eng"]
                counts[name] = counts.get(name, 0) + 1
                o["sem"], o["val"] = name, counts[name]
            if o["sem"] and o["sem"] not in sem_names:
                sem_names.append(o["sem"])
        with contextlib.ExitStack() as st:
            sems = {n: st.enter_context(nc.semaphore(n)) for n in sem_names}
            block = st.enter_context(nc.Block())
            per_eng = {e: [i for i, o in enumerate(ops) if o["eng"] == e] for e in ENGS}

            def make(ename):
                def body(eng):
                    waited = {}
                    for i in per_eng[ename]:
                        o = ops[i]
                        need = {}
                        for d in o["deps"]:
                            od = ops[d]
                            need[od["sem"]] = max(need.get(od["sem"], 0), od["val"])
                        for s, v in need.items():
                            if waited.get(s, 0) < v:
                                eng.wait_ge(sems[s], v)
                                waited[s] = v
                        if o["fn"] is None:
                            continue
                        ins = o["fn"](eng)
                        if o["sem"] is not None:
                            if o["dma_sem"] is not None:
                                lst = ins if isinstance(ins, (list, tuple)) else [ins]
                                assert len(lst) == o["ndma"]
                                for x in lst:
                                    x.then_inc(sems[o["sem"]], 16)
                            else:
                                ins.then_inc(sems[o["sem"]], 1)
                return body

            for e in ENGS:
                if per_eng[e]:
                    getattr(block, e)(make(e))


def pipeline(n, stages):
    ml = max(l for l, _ in stages)
    for t in range(n + ml):
        for lag, fn in stages:
            i = t - lag
            if 0 <= i < n:
                fn(i)


def view(ap, dims, off=0):
    return AP(ap.tensor, ap.offset + off, [list(ap.ap[0])] + [list(d) for d in dims])


PP_C, PP_NPRE, PP_CONVW, PP_CONVB, PP_BA, PP_BX, PP_LAM, PP_NREC, PP_NATT, PP_INVF = 0, 8, 16, 32, 36, 40, 44, 48, 52, 56
NPP = 88
SM_SC, SM_G1, SM_SHIFT, SM_C8, SM_NC8, SM_N2C8, SM_SS, SM_STD, SM_RSTD = 0, 8, 16, 24, 28, 32, 36, 52, 68
SM_CARRY, SM_ONE, SM_SSP, SM_STDP, SM_RSTDP, SM_POSF, SM_SP, SM_ZERO = 84, 88, 96, 112, 128, 144, 160, 200


def build_program():
    nc = bass.Bass("TRN2", target_bir_lowering=False)
    x_d = nc.dram_tensor("x", [S, D], F32, kind="ExternalInput").ap()
    pp_d = nc.dram_tensor("pp", [128, NPP], F32, kind="ExternalInput").ap()
    pos_d = nc.dram_tensor("pos", [128, NB], I32, kind="ExternalInput").ap()
    wada_d = nc.dram_tensor("w_ada", [D, 3 * D], F32, kind="ExternalInput").ap()
    rows_d = nc.dram_tensor("rows", [1, 4 * D], F32, kind="ExternalInput").ap()
    win_d = nc.dram_tensor("w_in", [D, 3 * D], F32, kind="ExternalInput").ap()
    wout_d = nc.dram_tensor("w_out", [D, D], F32, kind="ExternalInput").ap()
    wbd_d = nc.dram_tensor("wbd", [128, 8 * 128], F32, kind="ExternalInput").ap()
    mask_d = nc.dram_tensor("maskT", [128, 256], F32, kind="ExternalInput").ap()
    vscr_d = nc.dram_tensor("vscr", [S, 512], BF16, kind="Internal").ap()
    out_d = nc.dram_tensor("out", [S, D], F32, kind="ExternalOutput").ap()

    NW = 53000
    with contextlib.ExitStack() as st:
        big = st.enter_context(nc.sbuf_tensor("big", [128, NW], F32))
        ps = st.enter_context(nc.psum_tensor("ps", [128, 4096], F32))
        p = Prog(nc)
        OP = p.op

        def bank(b, lo=0, hi=512, p0=0, p1=128):
            return ps[p0:p1, b * 512 + lo:b * 512 + hi]

        def bank16(b, p0=0, p1=128):
            return ps[p0:p1, b * 512:(b + 1) * 512].bitcast(BF16)

        cur = [0]

        def alloc(nwords):
            o = cur[0]
            cur[0] += nwords
            assert cur[0] <= NW, cur[0]
            return o

        def f32v(off, n, p0=0, p1=128):
            return big[p0:p1, off:off + n]

        def b16v(off, n, p0=0, p1=128):
            return big[p0:p1, off:off + n // 2].bitcast(BF16)

        o_yT = alloc(8 * S // 2)
        o_wout = alloc(8 * D // 2)
        o_gn = alloc(D)
        o_mask = alloc(128)
        o_ones64 = alloc(32)
        o_ident = alloc(64)
        o_ones = alloc(64)
        o_onesrow = alloc(128)
        o_pp = alloc(NPP)
        o_small = alloc(256)
        o_wbd = alloc(1024)
        yT = b16v(o_yT, 8 * S).rearrange("p (e t) -> p e t", e=8)
        woutb = b16v(o_wout, 8 * D).rearrange("p (e n) -> p e n", e=8)
        gn_bc = f32v(o_gn, D)
        maskT = b16v(o_mask, 256)
        ones64 = b16v(o_ones64, 64)
        ident = b16v(o_ident, 128)
        onesb = b16v(o_ones, 128)
        onesrow = f32v(o_onesrow, 128, 0, 1)
        pp = f32v(o_pp, NPP)
        sm = f32v(o_small, 256)
        wbd = f32v(o_wbd, 1024).rearrange("p (g m) -> p g m", g=8)

        def smc(c0, n=1):
            return sm[:, c0:c0 + n]

        def ppc(c0, n=1):
            return pp[:, c0:c0 + n]

        o_hT = alloc(8 * S // 2)
        o_W = alloc(2 * 2048)
        o_cs = alloc(2 * NB * 64)
        RA = cur[0]
        hT = b16v(o_hT, 8 * S).rearrange("p (k t) -> p k t", k=8)
        Wsl = [b16v(o_W + s * 2048, 8 * 512).rearrange("p (k n) -> p k n", k=8) for s in range(2)]
        cosF = f32v(o_cs, NB * 64)
        sinS = f32v(o_cs + NB * 64, NB * 64)

        cur[0] = RA
        o_rows = alloc(4 * D)
        o_wada = alloc(3 * 3 * D)
        o_modrow = alloc(3 * D)
        o_gnrow = alloc(D)
        o_ident32 = o_yT
        o_mask32 = o_yT + 128
        o_wbd32 = o_yT + 128 + 256
        o_ang = o_yT + 128 + 256 + 1024
        assert o_ang + 4 * 512 + 64 <= o_yT + 8192
        rows_sb = f32v(o_rows, 4 * D, 0, 1)
        wada_sl = [f32v(o_wada + s * 3 * D, 3 * D) for s in range(3)]
        modrow = f32v(o_modrow, 3 * D, 0, 1)
        gnrow = f32v(o_gnrow, D, 0, 1)
        ident32 = f32v(o_ident32, 128)
        mask32 = f32v(o_mask32, 256)
        wbd32 = f32v(o_wbd32, 1024)
        posi = big[:, o_ang + 2048:o_ang + 2048 + NB].bitcast(I32)
        ANG = f32v(o_ang, 512)
        KI = big[:, o_ang + 512:o_ang + 1024].bitcast(I32)
        KF = f32v(o_ang + 1024, 512)
        SN = f32v(o_ang + 1536, 512)

        OP("sync", lambda e: e.dma_start(out=pp, in_=pp_d), writes=["pp"], dma_sem="c0")
        OP("sync", lambda e: e.dma_start(out=posi, in_=pos_d), writes=["posi"], dma_sem="c1")
        OP("sync", lambda e: e.dma_start(out=rows_sb, in_=rows_d), writes=["rows"], dma_sem="c2")
        OP("sync", lambda e: e.dma_start(out=mask32, in_=mask_d), writes=["mask32"], dma_sem="c3")
        OP("sync", lambda e: e.dma_start(out=wbd.rearrange("p g m -> p (g m)"), in_=wbd_d), writes=["wbd"], dma_sem="c4")
        win_v = win_d.rearrange("(k p) n -> p k n", p=128)

        def load_w(g, s):
            OP("gpsimd", (lambda e, g=g, s=s: e.dma_start(out=Wsl[s], in_=win_v[:, :, g * 512:(g + 1) * 512])),
               writes=[("W", s)], dma_sem="W%d" % s)

        OP("gpsimd", lambda e: e.memset(sm, 0.0), writes=["sm"])
        OP("gpsimd", lambda e: e.memset(smc(SM_ONE, 8), 1.0), reads=["sm"], writes=["sm_one"])
        OP("gpsimd", lambda e: e.memset(onesrow, 1.0), writes=["onesrow"])
        OP("gpsimd", lambda e: e.memset(ident32, 0.0), writes=["ident32"])
        OP("gpsimd", lambda e: e.affine_select(out=ident32, in_=ident32, pattern=[[-1, 128]],
                                                compare_op=ALU.not_equal, fill=1.0, base=0, channel_multiplier=1),
           reads=["ident32"], writes=["ident32"])
        OP("vector", lambda e: e.tensor_copy(out=ident, in_=ident32), reads=["ident32"], writes=["ident"])
        OP("gpsimd", lambda e: e.memset(onesb, 1.0), writes=["onesb"])
        OP("gpsimd", lambda e: e.memset(ones64, 1.0), writes=["ones64"])
        load_w(0, 0)
        load_w(1, 1)
        OP("vector", lambda e: e.tensor_copy(out=maskT, in_=mask32), reads=["mask32"], writes=["maskT"])
        OP("scalar", lambda e: e.activation(out=smc(SM_SC, 8), in_=ppc(PP_C, 8), func=AF.Silu),
           reads=["pp", "sm"], writes=["sc"])
        OP("scalar", lambda e: e.activation(out=smc(SM_SP, 4), in_=ppc(PP_LAM, 4), func=AF.Exp, scale=-1.0),
           reads=["pp", "sm"], writes=["sp"])
        OP("scalar", lambda e: e.activation(out=smc(SM_SP, 4), in_=smc(SM_SP, 4), func=AF.Ln, bias=1.0),
           reads=["sp"], writes=["sp"])
        OP("vector", lambda e: e.tensor_scalar(out=smc(SM_C8, 4), in0=smc(SM_SP, 4), scalar1=8.0, scalar2=None, op0=ALU.mult),
           reads=["sp", "sm"], writes=["c8"])
        OP("vector", lambda e: e.tensor_scalar(out=smc(SM_NC8, 4), in0=smc(SM_SP, 4), scalar1=-8.0, scalar2=None, op0=ALU.mult),
           reads=["sp", "sm"], writes=["nc8"])
        OP("vector", lambda e: e.tensor_scalar(out=smc(SM_N2C8, 4), in0=smc(SM_SP, 4), scalar1=-16.0, scalar2=None, op0=ALU.mult),
           reads=["sp", "sm"], writes=["n2c8"])
        OP("vector", lambda e: e.tensor_copy(out=smc(SM_POSF, NB), in_=posi), reads=["posi", "sm"], writes=["posf"])
        OP("vector", lambda e: e.tensor_tensor(
            out=ANG.rearrange("p (b f) -> p b f", b=NB),
            in0=view(smc(SM_POSF, NB), [[1, NB], [0, 32]]),
            in1=view(ppc(PP_INVF, 32), [[0, NB], [1, 32]]), op=ALU.mult),
           reads=["posf", "pp"], writes=["ANG"])
        C1 = 6.28125
        C2 = 2.0 * np.pi - 6.28125
        OP("vector", lambda e: e.tensor_scalar(out=KI, in0=ANG, scalar1=1.0 / (2 * PI), scalar2=None, op0=ALU.mult),
           reads=["ANG"], writes=["KI"])
        OP("vector", lambda e: e.tensor_copy(out=KF, in_=KI), reads=["KI"], writes=["KF"])
        OP("vector", lambda e: e.scalar_tensor_tensor(out=ANG, in0=KF, scalar=-C1, in1=ANG, op0=ALU.mult, op1=ALU.add),
           reads=["KF", "ANG"], writes=["ANG"])
        OP("vector", lambda e: e.scalar_tensor_tensor(out=ANG, in0=KF, scalar=-float(C2), in1=ANG, op0=ALU.mult, op1=ALU.add),
           reads=["KF", "ANG"], writes=["ANG"])

        def wrap(T):
            OP("vector", lambda e: e.tensor_scalar(out=KF, in0=T, scalar1=PI, scalar2=-2 * PI, op0=ALU.is_gt, op1=ALU.mult),
               reads=["ANG", "KF"], writes=["KF"])
            OP("vector", lambda e: e.tensor_tensor(out=T, in0=T, in1=KF, op=ALU.add), reads=["KF", "ANG"], writes=["ANG"])
            OP("vector", lambda e: e.tensor_scalar(out=KF, in0=T, scalar1=-PI, scalar2=2 * PI, op0=ALU.is_lt, op1=ALU.mult),
               reads=["ANG", "KF"], writes=["KF"])
            OP("vector", lambda e: e.tensor_tensor(out=T, in0=T, in1=KF, op=ALU.add), reads=["KF", "ANG"], writes=["ANG"])

        wrap(ANG)
        OP("scalar", lambda e: e.activation(out=SN, in_=ANG, func=AF.Sin), reads=["ANG"], writes=["SN"])
        sin3 = SN.rearrange("p (b f) -> p b f", b=NB)
        sinS3 = sinS.rearrange("p (b f) -> p b f", b=NB)
        cosF3 = cosF.rearrange("p (b f) -> p b f", b=NB)
        OP("vector", lambda e: e.tensor_scalar(out=sinS3[:, :, 0:32], in0=sin3, scalar1=-1.0, scalar2=None, op0=ALU.mult),
           reads=["SN"], writes=["sinS"])
        OP("vector", lambda e: e.tensor_copy(out=sinS3[:, :, 32:64], in_=sin3), reads=["SN"], writes=["sinS"])
        OP("vector", lambda e: e.tensor_scalar(out=ANG, in0=ANG, scalar1=PI / 2, scalar2=None, op0=ALU.add),
           reads=["ANG", "SN"], writes=["ANG"])
        wrap(ANG)
        OP("scalar", lambda e: e.activation(out=SN, in_=ANG, func=AF.Sin), reads=["ANG", "sinS"], writes=["SN"])
        OP("vector", lambda e: e.tensor_copy(out=cosF3[:, :, 0:32], in_=sin3), reads=["SN"], writes=["cosF"])
        OP("vector", lambda e: e.tensor_copy(out=cosF3[:, :, 32:64], in_=sin3), reads=["SN"], writes=["cosF"])

        o_xs = alloc(3 * D)
        o_xn = alloc(3 * D // 2)
        o_junk = alloc(D // 2)
        xs = [f32v(o_xs + s * D, D) for s in range(3)]
        xn = [b16v(o_xn + s * (D // 2), D) for s in range(3)]
        junk = b16v(o_junk, D)

        def wada_step(kc):
            s = kc % 3
            OP("sync", (lambda e: e.dma_start(out=wada_sl[s], in_=wada_d[kc * 128:(kc + 1) * 128, :])),
               writes=[("wada", s)], dma_sem="wada%d" % s)
            for n in range(6):
                OP("tensor", (lambda e, n=n: e.matmul(
                    bank(n, 0, 512, 0, 1), lhsT=smc(SM_SC + kc), rhs=wada_sl[s][:, n * 512:(n + 1) * 512],
                    start=(kc == 0), stop=(kc == 7))), reads=[("wada", s), "sc"], writes=[("pb", n)])

        def p1_a(blk):
            s = blk % 3
            r = blk % 3
            OP("sync", (lambda e: e.dma_start(out=xs[s], in_=x_d[blk * 128:(blk + 1) * 128, :])),
               writes=[("xs", s)], dma_sem="xs%d" % s)
            OP("scalar", (lambda e: e.activation(out=junk, in_=xs[s], func=AF.Square, accum_out=smc(SM_SS + blk))),
               reads=[("xs", s), "sm"], writes=["junk", ("ss", blk)])
            OP("scalar", (lambda e: e.activation(out=smc(SM_STD + blk), in_=smc(SM_SS + blk), func=AF.Sqrt,
                                                 scale=1.0 / D, bias=EPS)),
               reads=[("ss", blk)], writes=[("std", blk)])
            OP("vector", (lambda e: e.reciprocal(out=smc(SM_RSTD + blk), in_=smc(SM_STD + blk))),
               reads=[("std", blk)], writes=[("rstd", blk)])
            OP("vector", (lambda e: e.tensor_scalar(out=xn[r], in0=xs[s], scalar1=smc(SM_RSTD + blk), scalar2=None,
                                                    op0=ALU.mult)),
               reads=[("xs", s), ("rstd", blk)], writes=[("xn", r)])

        def p1_c(blk):
            r = blk % 3
            pb = 6 + (blk % 2)
            for kc in range(8):
                OP("tensor", (lambda e, kc=kc: e.transpose(out=bank16(pb)[:, kc * 128:(kc + 1) * 128],
                                                           in_=xn[r][:, kc * 128:(kc + 1) * 128], identity=ident)),
                   reads=[("xn", r), "ident"], writes=[("pb", pb)])
            OP("scalar", (lambda e: e.activation(out=hT[:, :, blk * 128:(blk + 1) * 128],
                                                 in_=bank16(pb).rearrange("p (k t) -> p k t", k=8), func=AF.Copy)),
               reads=[("pb", pb)], writes=[("hT", kc_, blk) for kc_ in range(8)])

        def p01(i):
            if i % 2 == 0 and i // 2 < 8:
                wada_step(i // 2)
            p1_a(i)

        pipeline(NB, [(0, p01), (1, p1_c)])
        OP("gpsimd", lambda e: e.dma_start(out=woutb, in_=wout_d.rearrange("(e p) n -> p e n", p=128)),
           writes=["wout"], dma_sem="c5", extra=[len(p.ops) - 1])
        OP("vector", lambda e: e.tensor_tensor(out=modrow, in0=ps[0:1, 0:3 * D], in1=rows_sb[:, 0:3 * D], op=ALU.add),
           reads=[("pb", n) for n in range(6)] + ["rows"], writes=["modrow"])
        for kc in range(8):
            for w in range(2):
                OP("tensor", (lambda e, kc=kc, w=w: e.matmul(
                    bank(6, w * 8 + kc, w * 8 + kc + 1), lhsT=modrow[:, w * D + kc * 128:w * D + (kc + 1) * 128],
                    rhs=sm[0:1, SM_ONE:SM_ONE + 1], start=True, stop=True)),
                   reads=["modrow", "sm_one"], writes=[("pb", 6)])
        OP("vector", lambda e: e.tensor_copy(out=smc(SM_SHIFT, 8), in_=bank(6, 0, 8)),
           reads=[("pb", 6), "sm"], writes=["shift"])
        OP("vector", lambda e: e.scalar_tensor_tensor(out=smc(SM_G1, 8), in0=bank(6, 8, 16), scalar=1.0,
                                                       in1=ppc(PP_NPRE, 8), op0=ALU.add, op1=ALU.mult),
           reads=[("pb", 6), "pp", "sm"], writes=["g1"])
        snapP1 = p.snapshot()
        for kc in range(8):
            OP("vector", (lambda e, kc=kc: e.tensor_scalar(out=hT[:, kc, :], in0=hT[:, kc, :], scalar1=smc(SM_G1 + kc),
                                                           scalar2=smc(SM_SHIFT + kc), op0=ALU.mult, op1=ALU.add)),
               reads=[("hT", kc, b) for b in range(NB)] + ["g1", "shift"], writes=[("hT", kc, b) for b in range(NB)])

        gn_last = [None]

        def rec_gn():
            OP("vector", lambda e: e.tensor_tensor(out=gnrow, in0=modrow[:, 2 * D:3 * D], in1=rows_sb[:, 3 * D:4 * D], op=ALU.mult),
               reads=["modrow", "rows"], writes=["gnrow"])
            for h in range(2):
                OP("tensor", (lambda e, h=h: e.matmul(bank(4 + h), lhsT=onesrow, rhs=gnrow[:, h * 512:(h + 1) * 512],
                                                      start=True, stop=True)),
                   reads=["gnrow", "onesrow", "modrow"], writes=[("pb", 4 + h)])
                gn_last[0] = OP("vector", (lambda e, h=h: e.tensor_copy(out=gn_bc[:, h * 512:(h + 1) * 512], in_=bank(4 + h))),
                                reads=[("pb", 4 + h)], writes=["gn_bc"])

        p.fence(ALL_E, dep=snapP1)
        cur[0] = RA
        TH_ = 1024
        o_xa = alloc(2 * 2052)
        o_sg = alloc(2 * S // 2)
        o_tA = alloc(6 * TH_ + TH_ + 512 + 512)
        o_sqa = alloc(4 * S // 2)
        o_tB2 = alloc(4096)
        o_sgt = alloc(2 * 512)
        SGT = [f32v(o_sgt + s_ * 512, 512) for s_ in range(2)]
        o_tB1 = o_yT + 4096
        XA = [f32v(o_xa + s * 2052, 2052) for s in range(2)]
        SGA = [b16v(o_sg + s * (S // 2), S) for s in range(2)]
        SQall = b16v(o_sqa, 4 * S).rearrange("p (c t) -> p c t", c=4)
        RBC = f32v(o_tB2 + 2048, S)
        for s_ in range(2):
            OP("gpsimd", (lambda e, s_=s_: e.memset(XA[s_][:, 0:4], 0.0)), writes=[("XA", s_)])
        pbi = [0]

        def fm_proj(s, cj, tc, evac, b=None):
            if b is None:
                b = pbi[0] % 4
                pbi[0] += 1
            for kc in range(8):
                OP("tensor", (lambda e, kc=kc, b=b: e.matmul(bank(b), lhsT=Wsl[s][:, kc, cj * 128:(cj + 1) * 128],
                                                            rhs=hT[:, kc, tc * 512:(tc + 1) * 512], start=(kc == 0), stop=(kc == 7))),
                   reads=[("W", s)] + [("hT", kc, 4 * tc + i) for i in range(4)], writes=[("pb", b)])
            evac(b)

        def inproj_mm(s_, cj):
            for tc in range(4):
                b = 4 + tc
                for kc in range(8):
                    OP("tensor", (lambda e, kc=kc, b=b, tc=tc: e.matmul(bank(b), lhsT=Wsl[s_][:, kc, cj * 128:(cj + 1) * 128],
                                                                        rhs=hT[:, kc, tc * 512:(tc + 1) * 512], start=(kc == 0), stop=(kc == 7))),
                       reads=[("W", s_)] + [("hT", kc, 4 * tc + i) for i in range(4)], writes=[("pb", b)])

        def evac_xa(cj):
            sl = cj % 2
            for tc in range(4):
                OP("scalar", (lambda e, tc=tc: e.activation(out=XA[sl][:, 4 + tc * 512:4 + (tc + 1) * 512], in_=bank(4 + tc), func=AF.Copy)),
                   reads=[("pb", 4 + tc)], writes=[("XA", sl)], extra=([gn_last[0]] if cj == 1 else ()))

        def evac_ga(cj):
            sl = cj % 2
            for tc in range(4):
                OP("scalar", (lambda e, tc=tc: e.activation(out=SGT[tc % 2], in_=bank(4 + tc), func=AF.Sigmoid)),
                   reads=[("pb", 4 + tc)], writes=[("SGT", tc % 2)])
                OP("vector", (lambda e, tc=tc: e.tensor_tensor(out=SGA[sl][:, tc * 512:(tc + 1) * 512], in0=bank(4 + tc), in1=SGT[tc % 2], op=ALU.mult)),
                   reads=[("pb", 4 + tc), ("SGT", tc % 2)], writes=[("SGA", sl)])

        tb = [
            [f32v(o_tA + i * TH_, TH_) for i in range(7)] + [b16v(o_tA + 7 * TH_, TH_), b16v(o_tA + 7 * TH_ + 512, TH_)],
            [f32v(o_tB1 + i * TH_, TH_) for i in range(4)] + [f32v(o_tB2 + i * TH_, TH_) for i in range(3)]
            + [b16v(o_tB2 + 3 * TH_, TH_), b16v(o_tB2 + 3 * TH_ + 512, TH_)],
        ]

        def rec_conv0(it):
            cj, hh = it // 2, it % 2
            par = it % 2
            XC = tb[par][0]
            t0 = hh * TH_
            X = XA[cj % 2]
            OP("vector", (lambda e: e.tensor_scalar(out=XC, in0=X[:, 4 + t0:4 + t0 + TH_], scalar1=ppc(PP_CONVW + cj * 4 + 3),
                                                    scalar2=ppc(PP_CONVB + cj), op0=ALU.mult, op1=ALU.add)),
               reads=[("XA", cj % 2), "pp"], writes=[("XC", par)])

        def rec_conv(it):
            cj, hh = it // 2, it % 2
            par = it % 2
            XC, RR, II, TT, AA, A2, CT, XCB, SQ = tb[par]
            K = lambda n: (n, par)
            t0 = hh * TH_
            X = XA[cj % 2]
            cw = lambda k: ppc(PP_CONVW + cj * 4 + k)
            for k in (2, 1, 0):
                OP("vector", (lambda e, k=k: e.scalar_tensor_tensor(out=XC, in0=X[:, 1 + k + t0:1 + k + t0 + TH_], scalar=cw(k), in1=XC,
                                                                    op0=ALU.mult, op1=ALU.add)),
                   reads=[("XA", cj % 2), "pp", K("XC")], writes=[K("XC")])
            for g in range(2):
                for q in range(2):
                    b = g * 2 + q
                    OP("tensor", (lambda e, g=g, q=q, b=b: e.matmul(bank(b), lhsT=wbd[:, g * 4 + cj, :], rhs=XC[:, q * 512:(q + 1) * 512],
                                                                   start=True, stop=True)),
                       reads=[K("XC"), "wbd"], writes=[("pb", b)])

        def rec_act(it, evac=None):
            cj, hh = it // 2, it % 2
            par = it % 2
            XC, RR, II, TT, AA, A2, CT, XCB, SQ = tb[par]
            K = lambda n: (n, par)
            OP("scalar", (lambda e: e.activation(out=RR, in_=ps[:, 0:1024], func=AF.Sigmoid, bias=ppc(PP_BA + cj))),
               reads=[("pb", 0), ("pb", 1), "pp"], writes=[K("RR")])
            OP("scalar", (lambda e: e.activation(out=II, in_=ps[:, 1024:2048], func=AF.Sigmoid, bias=ppc(PP_BX + cj))),
               reads=[("pb", 2), ("pb", 3), "pp"], writes=[K("II")])
            OP("scalar", (lambda e: e.activation(out=TT, in_=RR, func=AF.Tanh, scale=smc(SM_C8 + cj))),
               reads=[K("RR"), "c8"], writes=[K("TT")])
            if evac is not None:
                evac()
            OP("scalar", (lambda e: e.activation(out=AA, in_=RR, func=AF.Exp, scale=smc(SM_NC8 + cj))),
               reads=[K("RR"), "nc8"], writes=[K("AA")])
            OP("scalar", (lambda e: e.activation(out=A2, in_=RR, func=AF.Exp, scale=smc(SM_N2C8 + cj))),
               reads=[K("RR"), "n2c8"], writes=[K("A2")])

        def rec_s2(it):
            cj, hh = it // 2, it % 2
            par = it % 2
            XC, RR, II, TT, AA, A2, CT, XCB, SQ = tb[par]
            K = lambda n: (n, par)
            t0 = hh * TH_
            OP("vector", lambda e: e.scalar_tensor_tensor(out=A2, in0=A2, scalar=1.0, in1=TT, op0=ALU.add, op1=ALU.mult),
               reads=[K("A2"), K("TT")], writes=[K("A2")])
            OP("scalar", lambda e: e.activation(out=A2, in_=A2, func=AF.Ln), reads=[K("A2")], writes=[K("A2")])
            OP("scalar", lambda e: e.activation(out=A2, in_=A2, func=AF.Exp, scale=0.5), reads=[K("A2")], writes=[K("A2")])

        def rec_s2b(it):
            cj, hh = it // 2, it % 2
            par = it % 2
            XC, RR, II, TT, AA, A2, CT, XCB, SQ = tb[par]
            K = lambda n: (n, par)
            t0 = hh * TH_
            OP("vector", lambda e: e.tensor_tensor(out=TT, in0=II, in1=XC, op=ALU.mult), reads=[K("II"), K("XC"), K("TT")], writes=[K("TT")])
            OP("vector", lambda e: e.tensor_tensor(out=TT, in0=TT, in1=A2, op=ALU.mult), reads=[K("TT"), K("A2")], writes=[K("TT")])
            init = 0.0 if hh == 0 else smc(SM_CARRY + cj)
            OP("vector", (lambda e: e.tensor_tensor_scan(out=II, data0=AA, data1=TT, initial=init, op0=ALU.mult, op1=ALU.add)),
               reads=[K("AA"), K("TT"), K("II"), ("carry", cj)], writes=[K("II")])
            if hh == 0:
                OP("vector", lambda e: e.tensor_copy(out=smc(SM_CARRY + cj), in_=II[:, TH_ - 1:TH_]),
                   reads=[K("II"), "sm"], writes=[("carry", cj)])

        def rec_s2c(it):
            cj, hh = it // 2, it % 2
            par = it % 2
            XC, RR, II, TT, AA, A2, CT, XCB, SQ = tb[par]
            K = lambda n: (n, par)
            t0 = hh * TH_
            OP("gpsimd", (lambda e: e.tensor_tensor(out=yT[:, cj, t0:t0 + TH_], in0=II, in1=SGA[cj % 2][:, t0:t0 + TH_], op=ALU.mult)),
               reads=[K("II"), ("SGA", cj % 2)], writes=[("yT", cj)])
            OP("gpsimd", (lambda e: e.tensor_tensor(out=SQall[:, cj, t0:t0 + TH_], in0=yT[:, cj, t0:t0 + TH_], in1=yT[:, cj, t0:t0 + TH_],
                                                    op=ALU.mult)),
               reads=[("yT", cj)], writes=[("SQall", cj)], extra=([gn_last[0]] if it == 0 else ()))

        V1 = b16v(RA, NB * 512).rearrange("p (b h m) -> p b h m", b=NB, h=8)

        def v_mm(batch):
            for q in range(4):
                blk = 4 * batch + q
                b = 4 + q
                for kc in range(8):
                    OP("tensor", (lambda e, kc=kc, b=b, blk=blk: e.matmul(bank(b), lhsT=hT[:, kc, blk * 128:(blk + 1) * 128],
                                                                           rhs=Wsl[0][:, kc, :], start=(kc == 0), stop=(kc == 7))),
                       reads=[("W", 0), ("hT", kc, blk)], writes=[("pb", b)])

        def v_evac(batch, dep):
            for q in range(4):
                blk = 4 * batch + q
                b = 4 + q
                OP("scalar", (lambda e, blk=blk, b=b: e.activation(out=V1[:, blk].rearrange("p h m -> p (h m)"), in_=bank(b), func=AF.Copy)),
                   reads=[("pb", b)], writes=[("V", blk)], extra=dep)

        xa_dead = {}
        inproj_mm(0, 0)
        evac_xa(0)
        inproj_mm(1, 0)
        evac_ga(0)
        rec_conv0(0)
        rec_conv(0)
        rec_gn()
        for t in range(9):
            cjn = t // 2 + 1
            ev = None
            if t < 8 and cjn <= 3:
                inproj_mm(t % 2, cjn)
                ev = (lambda cjn=cjn: evac_xa(cjn)) if t % 2 == 0 else (lambda cjn=cjn: evac_ga(cjn))
                if cjn == 3:
                    load_w(4 if t % 2 == 0 else 3, t % 2)
            if t >= 1:
                rec_s2(t - 1)
            if t >= 1:
                rec_s2b(t - 1)
            if t < 8:
                rec_act(t, ev)
            if t in (7, 8):
                v_evac(t - 7, xa_dead[0])
            if t + 1 < 8:
                rec_conv0(t + 1)
                rec_conv(t + 1)
                if t + 1 == 5:
                    xa_dead[0] = p.snapshot()
                if t + 1 == 7:
                    xa_dead[1] = p.snapshot()
            if t in (6, 7, 8):
                v_mm(t - 6)
            if t >= 1:
                rec_s2c(t - 1)
        v_evac(2, xa_dead[1])
        v_mm(3)
        v_evac(3, xa_dead[1])
        snap3 = p.snapshot()
        tail_last = [None]

        def rec_tail():
            for q in (3, 0, 1, 2):
                for cj in range(4):
                    OP("tensor", (lambda e, q=q, cj=cj: e.matmul(bank(q), lhsT=onesb, rhs=SQall[:, cj, q * 512:(q + 1) * 512],
                                                                 start=(cj == 0), stop=(cj == 3))),
                       reads=[("SQall", cj), "onesb"], writes=[("pb", q)])
            for q in (3, 0, 1, 2):
                OP("scalar", (lambda e, q=q: e.activation(out=RBC[:, q * 512:(q + 1) * 512], in_=bank(q), func=AF.Ln, scale=1.0 / 512, bias=EPS)),
                   reads=[("pb", q)], writes=[("RBCq", q)])
            OP("scalar", lambda e: e.activation(out=RBC, in_=RBC, func=AF.Exp, scale=-0.5), reads=[("RBCq", q) for q in range(4)], writes=["RBC"])
            for cj in range(4):
                tail_last[0] = OP("vector", (lambda e, cj=cj: e.scalar_tensor_tensor(out=yT[:, cj, :], in0=yT[:, cj, :], scalar=ppc(PP_NREC + cj),
                                                                                     in1=RBC, op0=ALU.mult, op1=ALU.mult)),
                                  reads=["RBC", ("yT", cj), "pp"], writes=[("yT", cj)])

        p.fence(ALL_E, dep=snap3)
        cur[0] = RA
        o_V = alloc(NB * 512 // 2)
        assert o_V == RA
        o_KT = alloc(4 * S // 2)
        o_QT = alloc(4 * S // 2)
        o_V2 = alloc(NB * 512 // 2)
        o_sgb = alloc(4 * S // 2)
        o_rt = o_yT + 4096
        QT = b16v(o_QT, 4 * S).rearrange("p (j t) -> p j t", j=4)
        KT = b16v(o_KT, 4 * S).rearrange("p (j t) -> p j t", j=4)
        V2 = b16v(o_V2, NB * 512).rearrange("p (b h m) -> p b h m", b=NB, h=8)
        o_spare = cur[0]
        V3h = [b16v(o_cs, 8 * 512).rearrange("p (b h m) -> p b h m", b=8, h=8),
               b16v(o_spare, 8 * 512).rearrange("p (b h m) -> p b h m", b=8, h=8)]
        sgb = b16v(o_sgb, 4 * S).rearrange("p (j t) -> p j t", j=4)
        T1 = [f32v(o_rt + s * 512, 512) for s in range(3)]
        T2 = [f32v(o_rt + 1536 + s * 512, 512) for s in range(3)]
        QR = [b16v(o_rt + 3072 + s * 256, 512) for s in range(3)]
        load_w(2, 0)
        it = [0]

        def qk_a(i):
            if i == 3:
                rec_tail()
            if i == 4:
                OP("sync", lambda e: e.dma_start(out=vscr_d.rearrange("(b p) e -> p b e", p=128), in_=V1.rearrange("p b h m -> p b (h m)")),
                   reads=[("V", b) for b in range(NB)], writes=["vscr"], dma_sem="vs0")
                OP("sync", lambda e: e.dma_start(out=V2.rearrange("p (n r) h m -> p n (r h m)", n=4),
                                                 in_=vscr_d.rearrange("(n i r) e -> i n (r e)", n=4, r=4)),
                   reads=["vscr"], writes=["V2"], dma_sem="vs1", extra=[tail_last[0]])
            if i == NB + 2:
                load_w(5, 1)
            s, blk = 1 - i // NB, i % NB
            b = i % 4
            r = i % 3
            for kc in range(8):
                OP("tensor", (lambda e, kc=kc: e.matmul(bank(b), lhsT=hT[:, kc, blk * 128:(blk + 1) * 128],
                                                        rhs=Wsl[s][:, kc, :], start=(kc == 0), stop=(kc == 7))),
                   reads=[("W", s), ("hT", kc, blk)], writes=[("pb", b)])
            pbk = bank(b)
            cos_b = view(cosF[:, blk * 64:blk * 64 + 1], [[0, 8], [1, 64]])
            sin_b = view(sinS[:, blk * 64:blk * 64 + 1], [[0, 8], [32, 2], [1, 32]])
            swp = view(pbk[:, 32:33], [[64, 8], [-32, 2], [1, 32]])
            OP("vector", (lambda e: e.tensor_tensor(
                out=T1[r].rearrange("p (h d) -> p h d", h=8), in0=pbk.rearrange("p (h d) -> p h d", h=8), in1=cos_b, op=ALU.mult)),
               reads=[("pb", b), "cosF"], writes=[("T1", r)])
            OP("vector", (lambda e: e.tensor_tensor(
                out=T2[r].rearrange("p (h s d) -> p h s d", h=8, s=2), in0=swp, in1=sin_b, op=ALU.mult)),
               reads=[("pb", b), "sinS"], writes=[("T2", r)])
            OP("gpsimd", (lambda e: e.tensor_tensor(out=QR[r], in0=T1[r], in1=T2[r], op=ALU.add)),
               reads=[("T1", r), ("T2", r)], writes=[("QR", r)])

        def qk_c(i):
            s, blk = 1 - i // NB, i % NB
            r = i % 3
            tb_ = 4 + (i % 4)
            dstT = QT if s == 0 else KT
            for j in range(4):
                OP("tensor", (lambda e, j=j: e.transpose(out=bank16(tb_)[:, j * 128:(j + 1) * 128],
                                                         in_=QR[r][:, j * 128:(j + 1) * 128], identity=ident)),
                   reads=[("QR", r), "ident"], writes=[("pb", tb_)])
            OP("scalar", (lambda e: e.activation(
                out=dstT[:, :, blk * 128:(blk + 1) * 128], in_=bank16(tb_)[:, 0:512].rearrange("p (j t) -> p j t", j=4), func=AF.Copy)),
               reads=[("pb", tb_)], writes=[("QKT", s, blk)])

        pipeline(2 * NB, [(0, qk_a), (2, qk_c)])
        vsrc3 = vscr_d.rearrange("(i r) e -> i r e", r=16)
        for hf in range(2):
            OP("sync", (lambda e, hf=hf: e.dma_start(out=V3h[hf].rearrange("p b h m -> p b (h m)"), in_=vsrc3[:, 8 * hf:8 * hf + 8, :])),
               reads=["vscr"], writes=["V3", "cosF", "sinS"], dma_sem="vs2", extra=[tail_last[0]])
        for cj in range(4):
            for tc in range(4):
                fm_proj(1, cj, tc, lambda b, cj=cj, tc=tc: OP(
                    "scalar", (lambda e: e.activation(out=sgb[:, cj, tc * 512:(tc + 1) * 512], in_=bank(b), func=AF.Silu)),
                    reads=[("pb", b)], writes=[("sgb", cj)], extra=[tail_last[0]]))

        p.fence(ALL_E)
        cur[0] = o_hT
        o_P = alloc(5 * 512)
        o_LD = alloc(2 * 512)
        o_TN = alloc(512)
        o_YB = alloc(4 * 512)
        o_SQ = alloc(512)
        o_SQS = alloc(512)
        o_SQB = alloc(256)
        o_RB = alloc(512)
        assert cur[0] <= o_W
        Pt = [b16v(o_P + s * 512, 1024).rearrange("p (a n) -> p a n", a=2) for s in range(5)]
        LD = [f32v(o_LD + s * 512, 512) for s in range(2)]
        RD = LD
        TN = [f32v(o_TN, 512)] * 2
        YB4 = [f32v(o_YB + s * 512, 512) for s in range(4)]
        SQ4 = f32v(o_SQ, 512)
        SQS = f32v(o_SQS, 512)
        SQB = b16v(o_SQB, 512)
        RB = f32v(o_RB, 512)
        TRI_LE = maskT[:, 0:128]
        TRI_GE = maskT[:, 128:256]
        items = [(c, j, k) for c in range(4) for j in range(4) for k in range(5) if not (c == 0 and k == 3)]

        def s_specs(c, j, a, kind):
            rows = slice(a * 64, (a + 1) * 64)
            out = []
            if kind in (0, 1):
                for u in range(4):
                    qb = 4 * c + u
                    kb = qb - kind
                    if kb < 0:
                        continue
                    out.append((KT[rows, j, 128 * kb:128 * kb + 128], QT[rows, j, 128 * qb:128 * qb + 128], 128 * u, 128 * u + 128, 128))
            elif kind in (2, 3):
                ck = c - (kind - 2)
                for r in range(4):
                    out.append((KT[rows, j, 512 * ck + r:512 * (ck + 1):4], QT[rows, j, 512 * c + r:512 * (c + 1):4],
                                128 * r, 128 * r + 128, 128))
            else:
                Mk = 128
                for r in range(16):
                    out.append((KT[rows, j, r:S:16], QT[rows, j, 512 * c + r:512 * (c + 1):16], 32 * r, 32 * r + 32, Mk))
            return out

        def at_a(i):
            c, j, kind = items[i]
            sb_ = (i % 2) * 2
            pt = i % 5
            Pn = 128
            sp = [s_specs(c, j, a, kind) for a in range(2)]
            for idx in range(len(sp[0])):
                for a in range(2):
                    (lh, rh, lo, hi, M) = sp[a][idx]
                    OP("tensor", (lambda e, lh=lh, rh=rh, lo=lo, hi=hi, M=M, a=a: e.matmul(
                        bank(sb_ + a, lo, hi, 0, M), lhsT=lh, rhs=rh, start=True, stop=True)),
                       reads=[("QKT", 0, 4 * c + u) for u in range(4)] + [("QKT", 1, u) for u in range(4 * c + 4)], writes=[("pb", sb_ + a)])
            sview = view(ps[0:Pn, sb_ * 512:sb_ * 512 + 1], [[512, 2], [1, 512]])
            OP("scalar", (lambda e: e.activation(out=Pt[pt][0:Pn], in_=sview, func=AF.Exp, scale=0.125)),
               reads=[("pb", sb_), ("pb", sb_ + 1)], writes=[("P", pt)])
            if kind == 4:
                mview = view(TRI_LE[0:Pn, 32 * c:32 * c + 1], [[0, 2], [0, 16], [1, 32]])
                pv = Pt[pt][0:Pn].rearrange("p a (r i) -> p a r i", r=16)
            else:
                mview = view((TRI_LE if kind in (0, 2) else TRI_GE)[:, 0:1], [[0, 2], [0, 4], [1, 128]])
                pv = Pt[pt].rearrange("p a (u q) -> p a u q", u=4)
            OP("vector", (lambda e: e.tensor_tensor(out=pv, in0=pv, in1=mview, op=ALU.mult)),
               reads=[("P", pt), "maskT"], writes=[("P", pt)])

        def at_c(i):
            c, j, kind = items[i]
            pt = i % 5
            pair = c * 4 + j
            par = pair % 2
            bN, bD = 4 + 2 * par, 5 + 2 * par
            Pn = 128
            mms = []
            for a in range(2):
                h = 2 * j + a
                r0, r1 = a * 64, (a + 1) * 64
                mm = []
                if kind in (0, 1):
                    us = [u for u in range(4) if 4 * c + u - kind >= 0]
                    for u in us:
                        kb = 4 * c + u - kind
                        mm.append((bank(bN, 128 * u, 128 * u + 128, r0, r1), V1[:, kb, h, :], Pt[pt][:, a, 128 * u:128 * u + 128]))
                    lo = 128 * us[0]
                    mm.append((bank(bD, lo, 512, r0, r1), None, Pt[pt][:, a, lo:512]))
                elif kind in (2, 3):
                    ck = c - (kind - 2)
                    for r in range(4):
                        mm.append((bank(bN, r, 512, r0, r1)[:, ::4], V2[:, 4 * ck + r, h, :], Pt[pt][:, a, 128 * r:128 * r + 128]))
                        mm.append((bank(bD, r, 512, r0, r1)[:, ::4], None, Pt[pt][:, a, 128 * r:128 * r + 128]))
                else:
                    for r in range(16):
                        mm.append((bank(bN, r, 512, r0, r1)[:, ::16], V3h[r // 8][0:Pn, r % 8, h, :], Pt[pt][0:Pn, a, 32 * r:32 * r + 32]))
                        mm.append((bank(bD, r, 512, r0, r1)[:, ::16], None, Pt[pt][0:Pn, a, 32 * r:32 * r + 32]))
                mms.append(mm)
            first = [[kind == 0, kind == 0], [kind == 0, kind == 0]]
            for idx in range(len(mms[0])):
                for a in range(2):
                    (of, lh, rh) = mms[a][idx]
                    if lh is not None:
                        OP("tensor", (lambda e, of=of, lh=lh, rh=rh, st=first[a][0]: e.matmul(of, lhsT=lh, rhs=rh, start=st, stop=False,
                                                                                            skip_group_check=True)),
                           reads=[("P", pt), "V1", "V2", "V3"], writes=[("pb", bN)])
                        first[a][0] = False
                    else:
                        OP("tensor", (lambda e, of=of, rh=rh, st=first[a][1]: e.matmul(of, lhsT=ones64[0:Pn, :], rhs=rh, start=st, stop=False,
                                                                                      skip_group_check=True)),
                           reads=[("P", pt), "ones64"], writes=[("pb", bD)])
                        first[a][1] = False
            if kind != 4:
                return
            K = lambda n: (n, par)
            OP("scalar", lambda e: e.activation(out=LD[par], in_=bank(bD), func=AF.Ln), reads=[("pb", bD)], writes=[K("LD"), K("RD")])
            OP("scalar", lambda e: e.activation(out=RD[par], in_=LD[par], func=AF.Exp, scale=-1.0), reads=[K("LD")], writes=[K("RD")])
            OP("vector", lambda e: e.tensor_tensor(out=TN[par], in0=bank(bN), in1=RD[par], op=ALU.mult),
               reads=[("pb", bN), K("RD")], writes=["TN"])
            OP("gpsimd", (lambda e: e.tensor_tensor(out=YB4[j], in0=TN[par], in1=sgb[:, j, c * 512:(c + 1) * 512], op=ALU.mult)),
               reads=["TN", ("sgb", j)], writes=[("YB4", j)])
            if j == 0:
                OP("gpsimd", lambda e: e.tensor_tensor(out=SQS, in0=YB4[j], in1=YB4[j], op=ALU.mult), reads=[("YB4", j)], writes=["SQS"])
            else:
                OP("gpsimd", lambda e: e.tensor_tensor(out=SQ4, in0=YB4[j], in1=YB4[j], op=ALU.mult), reads=[("YB4", j)], writes=["SQ4"])
                OP("gpsimd", lambda e: e.tensor_tensor(out=SQS, in0=SQS, in1=SQ4, op=ALU.add), reads=["SQS", "SQ4"], writes=["SQS"])

        late_d = []

        def at_d0(i):
            c, j, kind = items[i]
            if kind == 4 and j == 3:
                OP("vector", lambda e: e.tensor_copy(out=SQB, in_=SQS), reads=["SQS"], writes=["SQB"])

        def at_d(i, force=False):
            c, j, kind = items[i]
            if kind != 4 or j != 3:
                return
            if i == len(items) - 1 and not force:
                late_d.append(i)
                return
            pair = c * 4 + j
            par = pair % 2
            bN, bD = 4 + 2 * par, 5 + 2 * par
            OP("tensor", (lambda e: e.matmul(bank(bD), lhsT=onesb, rhs=SQB, start=True, stop=True)),
               reads=["SQB", "onesb"], writes=[("pb", bD)])
            OP("scalar", lambda e: e.activation(out=RB, in_=bank(bD), func=AF.Ln, scale=1.0 / 512, bias=EPS),
               reads=[("pb", bD)], writes=["RB"])
            OP("scalar", lambda e: e.activation(out=RB, in_=RB, func=AF.Exp, scale=-0.5), reads=["RB"], writes=["RB"])
            for jj in range(4):
                OP("vector", (lambda e, jj=jj: e.scalar_tensor_tensor(out=yT[:, 4 + jj, c * 512:(c + 1) * 512], in0=YB4[jj],
                                                                      scalar=ppc(PP_NATT + jj), in1=RB, op0=ALU.mult, op1=ALU.mult)),
                   reads=["RB", ("YB4", jj), "pp"], writes=[("yT", 4 + jj)])

        pipeline(len(items), [(0, at_a), (3, at_c), (5, at_d0), (7, at_d)])

        last_pe = max(i_ for i_, o_ in enumerate(p.ops) if o_["eng"] == "tensor" and o_["fn"] is not None)
        p.fence(ALL_E, dep={last_pe})
        cur[0] = RA
        o_xs5 = alloc(4 * D)
        o_t5 = alloc(4 * D)
        o_j5 = alloc(D // 2)
        xs5 = [f32v(o_xs5 + s * D, D) for s in range(4)]
        T5 = [f32v(o_t5 + s * D, D) for s in range(4)]
        junk5 = b16v(o_j5, D)
        stores = []

        def p5_load(blk):
            s = blk % 4
            OP("sync", (lambda e: e.dma_start(out=xs5[s], in_=x_d[blk * 128:(blk + 1) * 128, :])),
               writes=[("xs5", s)], dma_sem="xr%d" % s)

        def p5_main(blk):
            s = blk % 4
            b0 = (blk % 2) * 2
            for h in range(2):
                for ec in range(8):
                    OP("tensor", (lambda e, h=h, ec=ec: e.matmul(
                        bank(b0 + h), lhsT=yT[:, ec, blk * 128:(blk + 1) * 128], rhs=woutb[:, ec, h * 512:(h + 1) * 512],
                        start=(ec == 0), stop=(ec == 7))),
                       reads=[("yT", ec), "wout"], writes=[("pb", b0 + h)])
            mix = ps[:, b0 * 512:b0 * 512 + 1024]
            OP("scalar", (lambda e: e.activation(out=junk5, in_=mix, func=AF.Square, accum_out=smc(SM_SSP + blk))),
               reads=[("pb", b0), ("pb", b0 + 1), "sm"], writes=["junk5", ("ssp", blk)])
            OP("scalar", (lambda e: e.activation(out=smc(SM_STDP + blk), in_=smc(SM_SSP + blk), func=AF.Sqrt,
                                                 scale=1.0 / D, bias=EPS)),
               reads=[("ssp", blk)], writes=[("stdp", blk)])
            OP("vector", (lambda e: e.reciprocal(out=smc(SM_RSTDP + blk), in_=smc(SM_STDP + blk))),
               reads=[("stdp", blk)], writes=[("rstdp", blk)])
            OP("vector", (lambda e: e.scalar_tensor_tensor(out=T5[s], in0=mix, scalar=smc(SM_RSTDP + blk), in1=gn_bc,
                                                           op0=ALU.mult, op1=ALU.mult)),
               reads=[("pb", b0), ("pb", b0 + 1), ("rstdp", blk), "gn_bc"], writes=[("T5", s)])
            OP("gpsimd", (lambda e: e.tensor_tensor(out=T5[s], in0=T5[s], in1=xs5[s], op=ALU.add)),
               reads=[("T5", s), ("xs5", s)], writes=[("T5", s)])
            stores.append(OP("sync", (lambda e: e.dma_start(out=out_d[blk * 128:(blk + 1) * 128, :], in_=T5[s])),
                             reads=[("T5", s)], dma_sem="st%d" % s))
            if blk == 7:
                at_d(late_d[0], force=True)

        pipeline(NB, [(0, p5_load), (2, p5_main)])
        OP("sync", None, extra=stores)
        p.emit()
    return nc


_CACHE = {}


def _c_mult(j):
    c = np.zeros_like(j, dtype=np.float32)
    c += ((j >= 0) & (j <= 128))
    c += ((j >= 0) & (j % 4 == 0) & (j <= 512))
    c += ((j >= 0) & (j % 16 == 0))
    return c.astype(np.float32)


def kernel(x, c, positions, w_ada, b_ada, norm_pre, norm_post, w_in, conv_w, conv_b,
           w_rg_a, b_rg_a, w_rg_x, b_rg_x, lru_lambda, norm_rec, norm_att, w_out):
    x = np.asarray(x, np.float32)
    B = x.shape[0]
    if "nc" not in _CACHE:
        _CACHE["nc"] = build_program()
    nc = _CACHE["nc"]
    f = lambda a: np.ascontiguousarray(np.asarray(a, np.float32))
    col = lambda v, n: f(v).reshape(n, 128).T
    rows = np.concatenate([f(b_ada)[0], f(norm_post)[0]])[None, :]
    wbd = np.zeros((128, 2, 4, 128), np.float32)
    for g, w in enumerate((f(w_rg_a)[0], f(w_rg_x)[0])):
        for cj in range(4):
            wbd[0:64, g, cj, 0:64] = w[2 * cj]
            wbd[64:128, g, cj, 64:128] = w[2 * cj + 1]
    wbd = wbd.reshape(128, 1024)
    pidx = np.arange(128)[:, None]
    xidx = np.arange(S)[None, :]
    q128 = np.arange(128)[None, :]
    maskT = np.concatenate([(pidx <= q128), (pidx >= q128)], axis=1).astype(np.float32)
    half = 32
    invf = (10000.0 ** (-np.arange(half, dtype=np.float32) / half)).astype(np.float32)
    pp_shared = np.zeros((128, NPP), np.float32)
    pp_shared[:, PP_NPRE:PP_NPRE + 8] = col(norm_pre[0], 8)
    pp_shared[:, PP_CONVW:PP_CONVW + 16] = f(conv_w)[0].reshape(4, 4, 128).transpose(2, 1, 0).reshape(128, 16)
    pp_shared[:, PP_CONVB:PP_CONVB + 4] = col(conv_b[0], 4)
    pp_shared[:, PP_BA:PP_BA + 4] = col(b_rg_a[0], 4)
    pp_shared[:, PP_BX:PP_BX + 4] = col(b_rg_x[0], 4)
    pp_shared[:, PP_LAM:PP_LAM + 4] = col(lru_lambda[0], 4)
    pp_shared[:, PP_NREC:PP_NREC + 4] = col(norm_rec[0], 4)
    pp_shared[:, PP_NATT:PP_NATT + 4] = col(norm_att[0], 4)
    pp_shared[:, PP_INVF:PP_INVF + 32] = invf[None, :]
    w_ada2 = f(w_ada)[0]
    w_in2 = f(w_in)[0]
    w_out2 = f(w_out)[0]
    in_maps = []
    for b in range(B):
        pp = pp_shared.copy()
        pp[:, PP_C:PP_C + 8] = col(np.asarray(c)[b], 8)
        pos = np.ascontiguousarray(np.asarray(positions)[b].astype(np.int32).reshape(NB, 128).T)
        in_maps.append({"x": np.ascontiguousarray(x[b]), "pp": pp, "pos": pos, "w_ada": w_ada2, "rows": rows,
                        "w_in": w_in2, "w_out": w_out2, "wbd": wbd, "maskT": maskT})
    res = run_bass_kernel_spmd(nc, in_maps, core_ids=list(range(B)))
    return np.stack([np.asarray(r["out"], np.float32) for r in res.results], axis=0)
```

```python
import contextlib
import numpy as np
import concourse.bass as bass
import concourse.mybir as mybir
from concourse.ap import AP
from concourse.bass_utils import run_bass_kernel_spmd

F32 = mybir.dt.float32
BF16 = mybir.dt.bfloat16
I32 = mybir.dt.int32
ALU = mybir.AluOpType
AF = mybir.ActivationFunctionType

D = 1024
S = 2048
NB = 16
EPS = 1e-6
PI = float(np.pi)
ENGS = ("tensor", "scalar", "vector", "gpsimd", "sync")
ALL_E = ENGS


class Prog:
    def __init__(self, nc):
        self.nc = nc
        self.ops = []
        self.touch = {}

    def op(self, eng, fn, reads=(), writes=(), dma_sem=None, ndma=1, extra=()):
        i = len(self.ops)
        self.ops.append(dict(eng=eng, fn=fn, reads=tuple(reads), writes=tuple(writes),
                             dma_sem=dma_sem, ndma=ndma, extra=set(extra)))
        for k in tuple(reads) + tuple(writes):
            self.touch.setdefault(k, set()).add(i)
        return i

    def snapshot(self):
        last = {}
        for i, o in enumerate(self.ops):
            if o["fn"] is None:
                continue
            last[("d", o["dma_sem"]) if o["dma_sem"] else ("e", o["eng"])] = i
        return set(last.values())

    def fence(self, engs, keys=None, dep=None):
        if dep is None:
            dep = self.snapshot()
        for e in engs:
            self.op(e, None, extra=dep)

    def emit(self):
        nc = self.nc
        ops = self.ops
        last_writer = {}
        readers = {}
        for i, o in enumerate(ops):
            deps = set(o["extra"])
            for k in o["reads"]:
                if k in last_writer:
                    deps.add(last_writer[k])
            for k in o["writes"]:
                if k in last_writer:
                    deps.add(last_writer[k])
                for r in readers.get(k, ()):
                    deps.add(r)
            deps.discard(i)
            deps = {d for d in deps if ops[d]["fn"] is not None}
            if o["eng"] == "tensor":
                deps = {d for d in deps if ops[d]["eng"] != "tensor"}
            o["deps"] = deps
            for k in o["reads"]:
                readers.setdefault(k, []).append(i)
            for k in o["writes"]:
                last_writer[k] = i
                readers[k] = []
        signal = set()
        for o in ops:
            signal |= o["deps"]
        counts = {}
        sem_names = []
        for i, o in enumerate(ops):
            o["sem"] = None
            if o["fn"] is None:
                continue
            if o["dma_sem"] is not None:
                name = "d_" + o["dma_sem"]
                counts[name] = counts.get(name, 0) + 16 * o["ndma"]
                o["sem"], o["val"] = name, counts[name]
            elif i in signal:
                name = "e_" + o["eng"]
                counts[name] = counts.get(name, 0) + 1
                o["sem"], o["val"] = name, counts[name]
            if o["sem"] and o["sem"] not in sem_names:
                sem_names.append(o["sem"])
        with contextlib.ExitStack() as st:
            sems = {n: st.enter_context(nc.semaphore(n)) for n in sem_names}
            block = st.enter_context(nc.Block())
            per_eng = {e: [i for i, o in enumerate(ops) if o["eng"] == e] for e in ENGS}

            def make(ename):
                def body(eng):
                    waited = {}
                    for i in per_eng[ename]:
                        o = ops[i]
                        need = {}
                        for d in o["deps"]:
                            od = ops[d]
                            need[od["sem"]] = max(need.get(od["sem"], 0), od["val"])
                        for s, v in need.items():
                            if waited.get(s, 0) < v:
                                eng.wait_ge(sems[s], v)
                                waited[s] = v
                        if o["fn"] is None:
                            continue
                        ins = o["fn"](eng)
                        if o["sem"] is not None:
                            if o["dma_sem"] is not None:
                                lst = ins if isinstance(ins, (list, tuple)) else [ins]
                                assert len(lst) == o["ndma"]
                                for x in lst:
                                    x.then_inc(sems[o["sem"]], 16)
                            else:
                                ins.then_inc(sems[o["sem"]], 1)
                return body

            for e in ENGS:
                if per_eng[e]:
                    getattr(block, e)(make(e))


def pipeline(n, stages):
    ml = max(l for l, _ in stages)
    for t in range(n + ml):
        for lag, fn in stages:
            i = t - lag
            if 0 <= i < n:
                fn(i)


def view(ap, dims, off=0):
    return AP(ap.tensor, ap.offset + off, [list(ap.ap[0])] + [list(d) for d in dims])


PP_C, PP_NPRE, PP_CONVW, PP_CONVB, PP_BA, PP_BX, PP_LAM, PP_NREC, PP_NATT, PP_INVF = 0, 8, 16, 32, 36, 40, 44, 48, 52, 56
NPP = 88
SM_SC, SM_G1, SM_SHIFT, SM_C8, SM_NC8, SM_N2C8, SM_SS, SM_STD, SM_RSTD = 0, 8, 16, 24, 28, 32, 36, 52, 68
SM_CARRY, SM_ONE, SM_SSP, SM_STDP, SM_RSTDP, SM_POSF, SM_SP, SM_ZERO = 84, 88, 96, 112, 128, 144, 160, 200


def build_program():
    nc = bass.Bass("TRN2", target_bir_lowering=False)
    x_d = nc.dram_tensor("x", [S, D], F32, kind="ExternalInput").ap()
    pp_d = nc.dram_tensor("pp", [128, NPP], F32, kind="ExternalInput").ap()
    pos_d = nc.dram_tensor("pos", [128, NB], I32, kind="ExternalInput").ap()
    wada_d = nc.dram_tensor("w_ada", [D, 3 * D], F32, kind="ExternalInput").ap()
    rows_d = nc.dram_tensor("rows", [1, 4 * D], F32, kind="ExternalInput").ap()
    win_d = nc.dram_tensor("w_in", [D, 3 * D], F32, kind="ExternalInput").ap()
    wout_d = nc.dram_tensor("w_out", [D, D], F32, kind="ExternalInput").ap()
    wbd_d = nc.dram_tensor("wbd", [128, 8 * 128], F32, kind="ExternalInput").ap()
    mask_d = nc.dram_tensor("maskT", [128, 256], F32, kind="ExternalInput").ap()
    vscr_d = nc.dram_tensor("vscr", [S, 512], BF16, kind="Internal").ap()
    out_d = nc.dram_tensor("out", [S, D], F32, kind="ExternalOutput").ap()

    NW = 53000
    with contextlib.ExitStack() as st:
        big = st.enter_context(nc.sbuf_tensor("big", [128, NW], F32))
        ps = st.enter_context(nc.psum_tensor("ps", [128, 4096], F32))
        p = Prog(nc)
        OP = p.op

        def bank(b, lo=0, hi=512, p0=0, p1=128):
            return ps[p0:p1, b * 512 + lo:b * 512 + hi]

        def bank16(b, p0=0, p1=128):
            return ps[p0:p1, b * 512:(b + 1) * 512].bitcast(BF16)

        cur = [0]

        def alloc(nwords):
            o = cur[0]
            cur[0] += nwords
            assert cur[0] <= NW, cur[0]
            return o

        def f32v(off, n, p0=0, p1=128):
            return big[p0:p1, off:off + n]

        def b16v(off, n, p0=0, p1=128):
            return big[p0:p1, off:off + n // 2].bitcast(BF16)

        o_yT = alloc(8 * S // 2)
        o_wout = alloc(8 * D // 2)
        o_gn = alloc(D)
        o_mask = alloc(128)
        o_ones64 = alloc(32)
        o_ident = alloc(64)
        o_ones = alloc(64)
        o_onesrow = alloc(128)
        o_pp = alloc(NPP)
        o_small = alloc(256)
        o_wbd = alloc(1024)
        yT = b16v(o_yT, 8 * S).rearrange("p (e t) -> p e t", e=8)
        woutb = b16v(o_wout, 8 * D).rearrange("p (e n) -> p e n", e=8)
        gn_bc = f32v(o_gn, D)
        maskT = b16v(o_mask, 256)
        ones64 = b16v(o_ones64, 64)
        ident = b16v(o_ident, 128)
        onesb = b16v(o_ones, 128)
        onesrow = f32v(o_onesrow, 128, 0, 1)
        pp = f32v(o_pp, NPP)
        sm = f32v(o_small, 256)
        wbd = f32v(o_wbd, 1024).rearrange("p (g m) -> p g m", g=8)

        def smc(c0, n=1):
            return sm[:, c0:c0 + n]

        def ppc(c0, n=1):
            return pp[:, c0:c0 + n]

        o_hT = alloc(8 * S // 2)
        o_W = alloc(2 * 2048)
        o_cs = alloc(2 * NB * 64)
        RA = cur[0]
        hT = b16v(o_hT, 8 * S).rearrange("p (k t) -> p k t", k=8)
        Wsl = [b16v(o_W + s * 2048, 8 * 512).rearrange("p (k n) -> p k n", k=8) for s in range(2)]
        cosF = f32v(o_cs, NB * 64)
        sinS = f32v(o_cs + NB * 64, NB * 64)

        cur[0] = RA
        o_rows = alloc(4 * D)
        o_wada = alloc(3 * 3 * D)
        o_modrow = alloc(3 * D)
        o_gnrow = alloc(D)
        o_ident32 = o_yT
        o_mask32 = o_yT + 128
        o_wbd32 = o_yT + 128 + 256
        o_ang = o_yT + 128 + 256 + 1024
        assert o_ang + 4 * 512 + 64 <= o_yT + 8192
        rows_sb = f32v(o_rows, 4 * D, 0, 1)
        wada_sl = [f32v(o_wada + s * 3 * D, 3 * D) for s in range(3)]
        modrow = f32v(o_modrow, 3 * D, 0, 1)
        gnrow = f32v(o_gnrow, D, 0, 1)
        ident32 = f32v(o_ident32, 128)
        mask32 = f32v(o_mask32, 256)
        wbd32 = f32v(o_wbd32, 1024)
        posi = big[:, o_ang + 2048:o_ang + 2048 + NB].bitcast(I32)
        ANG = f32v(o_ang, 512)
        KI = big[:, o_ang + 512:o_ang + 1024].bitcast(I32)
        KF = f32v(o_ang + 1024, 512)
        SN = f32v(o_ang + 1536, 512)

        OP("sync", lambda e: e.dma_start(out=pp, in_=pp_d), writes=["pp"], dma_sem="c0")
        OP("sync", lambda e: e.dma_start(out=posi, in_=pos_d), writes=["posi"], dma_sem="c1")
        OP("sync", lambda e: e.dma_start(out=rows_sb, in_=rows_d), writes=["rows"], dma_sem="c2")
        OP("sync", lambda e: e.dma_start(out=mask32, in_=mask_d), writes=["mask32"], dma_sem="c3")
        OP("sync", lambda e: e.dma_start(out=wbd.rearrange("p g m -> p (g m)"), in_=wbd_d), writes=["wbd"], dma_sem="c4")
        win_v = win_d.rearrange("(k p) n -> p k n", p=128)

        def load_w(g, s):
            OP("gpsimd", (lambda e, g=g, s=s: e.dma_start(out=Wsl[s], in_=win_v[:, :, g * 512:(g + 1) * 512])),
               writes=[("W", s)], dma_sem="W%d" % s)

        OP("gpsimd", lambda e: e.memset(sm, 0.0), writes=["sm"])
        OP("gpsimd", lambda e: e.memset(smc(SM_ONE, 8), 1.0), reads=["sm"], writes=["sm_one"])
        OP("gpsimd", lambda e: e.memset(onesrow, 1.0), writes=["onesrow"])
        OP("gpsimd", lambda e: e.memset(ident32, 0.0), writes=["ident32"])
        OP("gpsimd", lambda e: e.affine_select(out=ident32, in_=ident32, pattern=[[-1, 128]],
                                                compare_op=ALU.not_equal, fill=1.0, base=0, channel_multiplier=1),
           reads=["ident32"], writes=["ident32"])
        OP("vector", lambda e: e.tensor_copy(out=ident, in_=ident32), reads=["ident32"], writes=["ident"])
        OP("gpsimd", lambda e: e.memset(onesb, 1.0), writes=["onesb"])
        OP("gpsimd", lambda e: e.memset(ones64, 1.0), writes=["ones64"])
        load_w(0, 0)
        load_w(1, 1)
        OP("vector", lambda e: e.tensor_copy(out=maskT, in_=mask32), reads=["mask32"], writes=["maskT"])
        OP("scalar", lambda e: e.activation(out=smc(SM_SC, 8), in_=ppc(PP_C, 8), func=AF.Silu),
           reads=["pp", "sm"], writes=["sc"])
        OP("scalar", lambda e: e.activation(out=smc(SM_SP, 4), in_=ppc(PP_LAM, 4), func=AF.Exp, scale=-1.0),
           reads=["pp", "sm"], writes=["sp"])
        OP("scalar", lambda e: e.activation(out=smc(SM_SP, 4), in_=smc(SM_SP, 4), func=AF.Ln, bias=1.0),
           reads=["sp"], writes=["sp"])
        OP("vector", lambda e: e.tensor_scalar(out=smc(SM_C8, 4), in0=smc(SM_SP, 4), scalar1=8.0, scalar2=None, op0=ALU.mult),
           reads=["sp", "sm"], writes=["c8"])
        OP("vector", lambda e: e.tensor_scalar(out=smc(SM_NC8, 4), in0=smc(SM_SP, 4), scalar1=-8.0, scalar2=None, op0=ALU.mult),
           reads=["sp", "sm"], writes=["nc8"])
        OP("vector", lambda e: e.tensor_scalar(out=smc(SM_N2C8, 4), in0=smc(SM_SP, 4), scalar1=-16.0, scalar2=None, op0=ALU.mult),
           reads=["sp", "sm"], writes=["n2c8"])
        OP("vector", lambda e: e.tensor_copy(out=smc(SM_POSF, NB), in_=posi), reads=["posi", "sm"], writes=["posf"])
        OP("vector", lambda e: e.tensor_tensor(
            out=ANG.rearrange("p (b f) -> p b f", b=NB),
            in0=view(smc(SM_POSF, NB), [[1, NB], [0, 32]]),
            in1=view(ppc(PP_INVF, 32), [[0, NB], [1, 32]]), op=ALU.mult),
           reads=["posf", "pp"], writes=["ANG"])
        C1 = 6.28125
        C2 = 2.0 * np.pi - 6.28125
        OP("vector", lambda e: e.tensor_scalar(out=KI, in0=ANG, scalar1=1.0 / (2 * PI), scalar2=None, op0=ALU.mult),
           reads=["ANG"], writes=["KI"])
        OP("vector", lambda e: e.tensor_copy(out=KF, in_=KI), reads=["KI"], writes=["KF"])
        OP("vector", lambda e: e.scalar_tensor_tensor(out=ANG, in0=KF, scalar=-C1, in1=ANG, op0=ALU.mult, op1=ALU.add),
           reads=["KF", "ANG"], writes=["ANG"])
        OP("vector", lambda e: e.scalar_tensor_tensor(out=ANG, in0=KF, scalar=-float(C2), in1=ANG, op0=ALU.mult, op1=ALU.add),
           reads=["KF", "ANG"], writes=["ANG"])

        def wrap(T):
            OP("vector", lambda e: e.tensor_scalar(out=KF, in0=T, scalar1=PI, scalar2=-2 * PI, op0=ALU.is_gt, op1=ALU.mult),
               reads=["ANG", "KF"], writes=["KF"])
            OP("vector", lambda e: e.tensor_tensor(out=T, in0=T, in1=KF, op=ALU.add), reads=["KF", "ANG"], writes=["ANG"])
            OP("vector", lambda e: e.tensor_scalar(out=KF, in0=T, scalar1=-PI, scalar2=2 * PI, op0=ALU.is_lt, op1=ALU.mult),
               reads=["ANG", "KF"], writes=["KF"])
            OP("vector", lambda e: e.tensor_tensor(out=T, in0=T, in1=KF, op=ALU.add), reads=["KF", "ANG"], writes=["ANG"])

        wrap(ANG)
        OP("scalar", lambda e: e.activation(out=SN, in_=ANG, func=AF.Sin), reads=["ANG"], writes=["SN"])
        sin3 = SN.rearrange("p (b f) -> p b f", b=NB)
        sinS3 = sinS.rearrange("p (b f) -> p b f", b=NB)
        cosF3 = cosF.rearrange("p (b f) -> p b f", b=NB)
        OP("vector", lambda e: e.tensor_scalar(out=sinS3[:, :, 0:32], in0=sin3, scalar1=-1.0, scalar2=None, op0=ALU.mult),
           reads=["SN"], writes=["sinS"])
        OP("vector", lambda e: e.tensor_copy(out=sinS3[:, :, 32:64], in_=sin3), reads=["SN"], writes=["sinS"])
        OP("vector", lambda e: e.tensor_scalar(out=ANG, in0=ANG, scalar1=PI / 2, scalar2=None, op0=ALU.add),
           reads=["ANG", "SN"], writes=["ANG"])
        wrap(ANG)
        OP("scalar", lambda e: e.activation(out=SN, in_=ANG, func=AF.Sin), reads=["ANG", "sinS"], writes=["SN"])
        OP("vector", lambda e: e.tensor_copy(out=cosF3[:, :, 0:32], in_=sin3), reads=["SN"], writes=["cosF"])
        OP("vector", lambda e: e.tensor_copy(out=cosF3[:, :, 32:64], in_=sin3), reads=["SN"], writes=["cosF"])

        o_xs = alloc(3 * D)
        o_xn = alloc(3 * D // 2)
        o_junk = alloc(D // 2)
        xs = [f32v(o_xs + s * D, D) for s in range(3)]
        xn = [b16v(o_xn + s * (D // 2), D) for s in range(3)]
        junk = b16v(o_junk, D)

        def wada_step(kc):
            s = kc % 3
            OP("sync", (lambda e: e.dma_start(out=wada_sl[s], in_=wada_d[kc * 128:(kc + 1) * 128, :])),
               writes=[("wada", s)], dma_sem="wada%d" % s)
            for n in range(6):
                OP("tensor", (lambda e, n=n: e.matmul(
                    bank(n, 0, 512, 0, 1), lhsT=smc(SM_SC + kc), rhs=wada_sl[s][:, n * 512:(n + 1) * 512],
                    start=(kc == 0), stop=(kc == 7))), reads=[("wada", s), "sc"], writes=[("pb", n)])

        def p1_a(blk):
            s = blk % 3
            r = blk % 3
            OP("sync", (lambda e: e.dma_start(out=xs[s], in_=x_d[blk * 128:(blk + 1) * 128, :])),
               writes=[("xs", s)], dma_sem="xs%d" % s)
            OP("scalar", (lambda e: e.activation(out=junk, in_=xs[s], func=AF.Square, accum_out=smc(SM_SS + blk))),
               reads=[("xs", s), "sm"], writes=["junk", ("ss", blk)])
            OP("scalar", (lambda e: e.activation(out=smc(SM_STD + blk), in_=smc(SM_SS + blk), func=AF.Sqrt,
                                                 scale=1.0 / D, bias=EPS)),
               reads=[("ss", blk)], writes=[("std", blk)])
            OP("vector", (lambda e: e.reciprocal(out=smc(SM_RSTD + blk), in_=smc(SM_STD + blk))),
               reads=[("std", blk)], writes=[("rstd", blk)])
            OP("vector", (lambda e: e.tensor_scalar(out=xn[r], in0=xs[s], scalar1=smc(SM_RSTD + blk), scalar2=None,
                                                    op0=ALU.mult)),
               reads=[("xs", s), ("rstd", blk)], writes=[("xn", r)])

        def p1_c(blk):
            r = blk % 3
            pb = 6 + (blk % 2)
            for kc in range(8):
                OP("tensor", (lambda e, kc=kc: e.transpose(out=bank16(pb)[:, kc * 128:(kc + 1) * 128],
                                                           in_=xn[r][:, kc * 128:(kc + 1) * 128], identity=ident)),
                   reads=[("xn", r), "ident"], writes=[("pb", pb)])
            OP("scalar", (lambda e: e.activation(out=hT[:, :, blk * 128:(blk + 1) * 128],
                                                 in_=bank16(pb).rearrange("p (k t) -> p k t", k=8), func=AF.Copy)),
               reads=[("pb", pb)], writes=[("hT", kc_, blk) for kc_ in range(8)])

        def p01(i):
            if i % 2 == 0 and i // 2 < 8:
                wada_step(i // 2)
            p1_a(i)

        pipeline(NB, [(0, p01), (1, p1_c)])
        OP("gpsimd", lambda e: e.dma_start(out=woutb, in_=wout_d.rearrange("(e p) n -> p e n", p=128)),
           writes=["wout"], dma_sem="c5", extra=[len(p.ops) - 1])
        OP("vector", lambda e: e.tensor_tensor(out=modrow, in0=ps[0:1, 0:3 * D], in1=rows_sb[:, 0:3 * D], op=ALU.add),
           reads=[("pb", n) for n in range(6)] + ["rows"], writes=["modrow"])
        for kc in range(8):
            for w in range(2):
                OP("tensor", (lambda e, kc=kc, w=w: e.matmul(
                    bank(6, w * 8 + kc, w * 8 + kc + 1), lhsT=modrow[:, w * D + kc * 128:w * D + (kc + 1) * 128],
                    rhs=sm[0:1, SM_ONE:SM_ONE + 1], start=True, stop=True)),
                   reads=["modrow", "sm_one"], writes=[("pb", 6)])
        OP("vector", lambda e: e.tensor_copy(out=smc(SM_SHIFT, 8), in_=bank(6, 0, 8)),
           reads=[("pb", 6), "sm"], writes=["shift"])
        OP("vector", lambda e: e.scalar_tensor_tensor(out=smc(SM_G1, 8), in0=bank(6, 8, 16), scalar=1.0,
                                                       in1=ppc(PP_NPRE, 8), op0=ALU.add, op1=ALU.mult),
           reads=[("pb", 6), "pp", "sm"], writes=["g1"])
        snapP1 = p.snapshot()
        for kc in range(8):
            OP("vector", (lambda e, kc=kc: e.tensor_scalar(out=hT[:, kc, :], in0=hT[:, kc, :], scalar1=smc(SM_G1 + kc),
                                                           scalar2=smc(SM_SHIFT + kc), op0=ALU.mult, op1=ALU.add)),
               reads=[("hT", kc, b) for b in range(NB)] + ["g1", "shift"], writes=[("hT", kc, b) for b in range(NB)])

        OP("vector", lambda e: e.tensor_tensor(out=gnrow, in0=modrow[:, 2 * D:3 * D], in1=rows_sb[:, 3 * D:4 * D], op=ALU.mult),
           reads=["modrow", "rows"], writes=["gnrow"])
        for h in range(2):
            OP("tensor", (lambda e, h=h: e.matmul(bank(h), lhsT=onesrow, rhs=gnrow[:, h * 512:(h + 1) * 512],
                                                  start=True, stop=True)),
               reads=["gnrow", "onesrow", "modrow"], writes=[("pb", h)])
            OP("vector", (lambda e, h=h: e.tensor_copy(out=gn_bc[:, h * 512:(h + 1) * 512], in_=bank(h))),
               reads=[("pb", h)], writes=["gn_bc"])

        p.fence(["tensor"], dep=snapP1)
        p.fence([e_ for e_ in ALL_E if e_ != "tensor"])
        cur[0] = RA
        TH_ = 1024
        o_xa = alloc(2 * 2052)
        o_sg = alloc(2 * S // 2)
        o_tA = alloc(6 * TH_ + TH_ + 512 + 512)
        o_sqa = alloc(4 * S // 2)
        o_tB2 = alloc(4096)
        o_sgt = alloc(2 * 512)
        SGT = [f32v(o_sgt + s_ * 512, 512) for s_ in range(2)]
        o_tB1 = o_yT + 4096
        XA = [f32v(o_xa + s * 2052, 2052) for s in range(2)]
        SGA = [b16v(o_sg + s * (S // 2), S) for s in range(2)]
        SQall = b16v(o_sqa, 4 * S).rearrange("p (c t) -> p c t", c=4)
        RBC = f32v(o_tB2 + 2048, S)
        for s_ in range(2):
            OP("gpsimd", (lambda e, s_=s_: e.memset(XA[s_][:, 0:4], 0.0)), writes=[("XA", s_)])
        pbi = [0]

        def fm_proj(s, cj, tc, evac, b=None):
            if b is None:
                b = pbi[0] % 4
                pbi[0] += 1
            for kc in range(8):
                OP("tensor", (lambda e, kc=kc, b=b: e.matmul(bank(b), lhsT=Wsl[s][:, kc, cj * 128:(cj + 1) * 128],
                                                            rhs=hT[:, kc, tc * 512:(tc + 1) * 512], start=(kc == 0), stop=(kc == 7))),
                   reads=[("W", s)] + [("hT", kc, 4 * tc + i) for i in range(4)], writes=[("pb", b)])
            evac(b)

        def inproj_mm(s_, cj):
            for tc in range(4):
                b = 4 + tc
                for kc in range(8):
                    OP("tensor", (lambda e, kc=kc, b=b, tc=tc: e.matmul(bank(b), lhsT=Wsl[s_][:, kc, cj * 128:(cj + 1) * 128],
                                                                        rhs=hT[:, kc, tc * 512:(tc + 1) * 512], start=(kc == 0), stop=(kc == 7))),
                       reads=[("W", s_)] + [("hT", kc, 4 * tc + i) for i in range(4)], writes=[("pb", b)])

        def evac_xa(cj):
            sl = cj % 2
            for tc in range(4):
                OP("scalar", (lambda e, tc=tc: e.activation(out=XA[sl][:, 4 + tc * 512:4 + (tc + 1) * 512], in_=bank(4 + tc), func=AF.Copy)),
                   reads=[("pb", 4 + tc)], writes=[("XA", sl)])

        def evac_ga(cj):
            sl = cj % 2
            for tc in range(4):
                OP("scalar", (lambda e, tc=tc: e.activation(out=SGT[tc % 2], in_=bank(4 + tc), func=AF.Sigmoid)),
                   reads=[("pb", 4 + tc)], writes=[("SGT", tc % 2)])
                OP("vector", (lambda e, tc=tc: e.tensor_tensor(out=SGA[sl][:, tc * 512:(tc + 1) * 512], in0=bank(4 + tc), in1=SGT[tc % 2], op=ALU.mult)),
                   reads=[("pb", 4 + tc), ("SGT", tc % 2)], writes=[("SGA", sl)])

        tb = [
            [f32v(o_tA + i * TH_, TH_) for i in range(7)] + [b16v(o_tA + 7 * TH_, TH_), b16v(o_tA + 7 * TH_ + 512, TH_)],
            [f32v(o_tB1 + i * TH_, TH_) for i in range(4)] + [f32v(o_tB2 + i * TH_, TH_) for i in range(3)]
            + [b16v(o_tB2 + 3 * TH_, TH_), b16v(o_tB2 + 3 * TH_ + 512, TH_)],
        ]

        def rec_conv0(it):
            cj, hh = it // 2, it % 2
            par = it % 2
            XC = tb[par][0]
            t0 = hh * TH_
            X = XA[cj % 2]
            OP("vector", (lambda e: e.tensor_scalar(out=XC, in0=X[:, 4 + t0:4 + t0 + TH_], scalar1=ppc(PP_CONVW + cj * 4 + 3),
                                                    scalar2=ppc(PP_CONVB + cj), op0=ALU.mult, op1=ALU.add)),
               reads=[("XA", cj % 2), "pp"], writes=[("XC", par)])

        def rec_conv(it):
            cj, hh = it // 2, it % 2
            par = it % 2
            XC, RR, II, TT, AA, A2, CT, XCB, SQ = tb[par]
            K = lambda n: (n, par)
            t0 = hh * TH_
            X = XA[cj % 2]
            cw = lambda k: ppc(PP_CONVW + cj * 4 + k)
            for k in (2, 1, 0):
                OP("vector", (lambda e, k=k: e.scalar_tensor_tensor(out=XC, in0=X[:, 1 + k + t0:1 + k + t0 + TH_], scalar=cw(k), in1=XC,
                                                                    op0=ALU.mult, op1=ALU.add)),
                   reads=[("XA", cj % 2), "pp", K("XC")], writes=[K("XC")])
            for g in range(2):
                for q in range(2):
                    b = g * 2 + q
                    OP("tensor", (lambda e, g=g, q=q, b=b: e.matmul(bank(b), lhsT=wbd[:, g * 4 + cj, :], rhs=XC[:, q * 512:(q + 1) * 512],
                                                                   start=True, stop=True)),
                       reads=[K("XC"), "wbd"], writes=[("pb", b)])

        def rec_act(it, evac=None):
            cj, hh = it // 2, it % 2
            par = it % 2
            XC, RR, II, TT, AA, A2, CT, XCB, SQ = tb[par]
            K = lambda n: (n, par)
            OP("scalar", (lambda e: e.activation(out=RR, in_=ps[:, 0:1024], func=AF.Sigmoid, bias=ppc(PP_BA + cj))),
               reads=[("pb", 0), ("pb", 1), "pp"], writes=[K("RR")])
            OP("scalar", (lambda e: e.activation(out=II, in_=ps[:, 1024:2048], func=AF.Sigmoid, bias=ppc(PP_BX + cj))),
               reads=[("pb", 2), ("pb", 3), "pp"], writes=[K("II")])
            OP("scalar", (lambda e: e.activation(out=TT, in_=RR, func=AF.Tanh, scale=smc(SM_C8 + cj))),
               reads=[K("RR"), "c8"], writes=[K("TT")])
            if evac is not None:
                evac()
            OP("scalar", (lambda e: e.activation(out=AA, in_=RR, func=AF.Exp, scale=smc(SM_NC8 + cj))),
               reads=[K("RR"), "nc8"], writes=[K("AA")])
            OP("scalar", (lambda e: e.activation(out=A2, in_=RR, func=AF.Exp, scale=smc(SM_N2C8 + cj))),
               reads=[K("RR"), "n2c8"], writes=[K("A2")])

        def rec_s2(it):
            cj, hh = it // 2, it % 2
            par = it % 2
            XC, RR, II, TT, AA, A2, CT, XCB, SQ = tb[par]
            K = lambda n: (n, par)
            t0 = hh * TH_
            OP("vector", lambda e: e.scalar_tensor_tensor(out=A2, in0=A2, scalar=1.0, in1=TT, op0=ALU.add, op1=ALU.mult),
               reads=[K("A2"), K("TT")], writes=[K("A2")])
            OP("scalar", lambda e: e.activation(out=A2, in_=A2, func=AF.Ln), reads=[K("A2")], writes=[K("A2")])
            OP("scalar", lambda e: e.activation(out=A2, in_=A2, func=AF.Exp, scale=0.5), reads=[K("A2")], writes=[K("A2")])

        def rec_s2b(it):
            cj, hh = it // 2, it % 2
            par = it % 2
            XC, RR, II, TT, AA, A2, CT, XCB, SQ = tb[par]
            K = lambda n: (n, par)
            t0 = hh * TH_
            OP("vector", lambda e: e.tensor_tensor(out=TT, in0=II, in1=XC, op=ALU.mult), reads=[K("II"), K("XC"), K("TT")], writes=[K("TT")])
            OP("vector", lambda e: e.tensor_tensor(out=TT, in0=TT, in1=A2, op=ALU.mult), reads=[K("TT"), K("A2")], writes=[K("TT")])
            init = 0.0 if hh == 0 else smc(SM_CARRY + cj)
            OP("vector", (lambda e: e.tensor_tensor_scan(out=II, data0=AA, data1=TT, initial=init, op0=ALU.mult, op1=ALU.add)),
               reads=[K("AA"), K("TT"), K("II"), ("carry", cj)], writes=[K("II")])
            if hh == 0:
                OP("vector", lambda e: e.tensor_copy(out=smc(SM_CARRY + cj), in_=II[:, TH_ - 1:TH_]),
                   reads=[K("II"), "sm"], writes=[("carry", cj)])

        def rec_s2c(it):
            cj, hh = it // 2, it % 2
            par = it % 2
            XC, RR, II, TT, AA, A2, CT, XCB, SQ = tb[par]
            K = lambda n: (n, par)
            t0 = hh * TH_
            OP("gpsimd", (lambda e: e.tensor_tensor(out=yT[:, cj, t0:t0 + TH_], in0=II, in1=SGA[cj % 2][:, t0:t0 + TH_], op=ALU.mult)),
               reads=[K("II"), ("SGA", cj % 2)], writes=[("yT", cj)])
            OP("gpsimd", (lambda e: e.tensor_tensor(out=SQall[:, cj, t0:t0 + TH_], in0=yT[:, cj, t0:t0 + TH_], in1=yT[:, cj, t0:t0 + TH_],
                                                    op=ALU.mult)),
               reads=[("yT", cj)], writes=[("SQall", cj)])

        V1 = b16v(RA, NB * 512).rearrange("p (b h m) -> p b h m", b=NB, h=8)

        def v_mm(batch):
            for q in range(4):
                blk = 4 * batch + q
                b = 4 + q
                for kc in range(8):
                    OP("tensor", (lambda e, kc=kc, b=b, blk=blk: e.matmul(bank(b), lhsT=hT[:, kc, blk * 128:(blk + 1) * 128],
                                                                           rhs=Wsl[0][:, kc, :], start=(kc == 0), stop=(kc == 7))),
                       reads=[("W", 0), ("hT", kc, blk)], writes=[("pb", b)])

        def v_evac(batch, dep):
            for q in range(4):
                blk = 4 * batch + q
                b = 4 + q
                OP("scalar", (lambda e, blk=blk, b=b: e.activation(out=V1[:, blk].rearrange("p h m -> p (h m)"), in_=bank(b), func=AF.Copy)),
                   reads=[("pb", b)], writes=[("V", blk)], extra=dep)

        xa_dead = {}
        inproj_mm(0, 0)
        evac_xa(0)
        inproj_mm(1, 0)
        evac_ga(0)
        rec_conv0(0)
        rec_conv(0)
        for t in range(9):
            cjn = t // 2 + 1
            ev = None
            if t < 8 and cjn <= 3:
                inproj_mm(t % 2, cjn)
                ev = (lambda cjn=cjn: evac_xa(cjn)) if t % 2 == 0 else (lambda cjn=cjn: evac_ga(cjn))
                if cjn == 3:
                    load_w(4 if t % 2 == 0 else 3, t % 2)
            if t >= 1:
                rec_s2(t - 1)
            if t >= 1:
                rec_s2b(t - 1)
            if t < 8:
                rec_act(t, ev)
            if t in (7, 8):
                v_evac(t - 7, xa_dead[0])
            if t + 1 < 8:
                rec_conv0(t + 1)
                rec_conv(t + 1)
                if t + 1 == 5:
                    xa_dead[0] = p.snapshot()
                if t + 1 == 7:
                    xa_dead[1] = p.snapshot()
            if t in (6, 7, 8):
                v_mm(t - 6)
            if t >= 1:
                rec_s2c(t - 1)
        v_evac(2, xa_dead[1])
        v_mm(3)
        v_evac(3, xa_dead[1])
        snap3 = p.snapshot()
        tail_last = [None]

        def rec_tail():
            for q in (3, 0, 1, 2):
                for cj in range(4):
                    OP("tensor", (lambda e, q=q, cj=cj: e.matmul(bank(q), lhsT=onesb, rhs=SQall[:, cj, q * 512:(q + 1) * 512],
                                                                 start=(cj == 0), stop=(cj == 3))),
                       reads=[("SQall", cj), "onesb"], writes=[("pb", q)])
            for q in (3, 0, 1, 2):
                OP("scalar", (lambda e, q=q: e.activation(out=RBC[:, q * 512:(q + 1) * 512], in_=bank(q), func=AF.Ln, scale=1.0 / 512, bias=EPS)),
                   reads=[("pb", q)], writes=[("RBCq", q)])
            OP("scalar", lambda e: e.activation(out=RBC, in_=RBC, func=AF.Exp, scale=-0.5), reads=[("RBCq", q) for q in range(4)], writes=["RBC"])
            for cj in range(4):
                tail_last[0] = OP("vector", (lambda e, cj=cj: e.scalar_tensor_tensor(out=yT[:, cj, :], in0=yT[:, cj, :], scalar=ppc(PP_NREC + cj),
                                                                                     in1=RBC, op0=ALU.mult, op1=ALU.mult)),
                                  reads=["RBC", ("yT", cj), "pp"], writes=[("yT", cj)])

        p.fence(ALL_E, dep=snap3)
        cur[0] = RA
        o_V = alloc(NB * 512 // 2)
        assert o_V == RA
        o_KT = alloc(4 * S // 2)
        o_QT = alloc(4 * S // 2)
        o_V2 = alloc(NB * 512 // 2)
        o_sgb = alloc(4 * S // 2)
        o_rt = o_yT + 4096
        QT = b16v(o_QT, 4 * S).rearrange("p (j t) -> p j t", j=4)
        KT = b16v(o_KT, 4 * S).rearrange("p (j t) -> p j t", j=4)
        V2 = b16v(o_V2, NB * 512).rearrange("p (b h m) -> p b h m", b=NB, h=8)
        o_spare = cur[0]
        V3h = [b16v(o_cs, 8 * 512).rearrange("p (b h m) -> p b h m", b=8, h=8),
               b16v(o_spare, 8 * 512).rearrange("p (b h m) -> p b h m", b=8, h=8)]
        sgb = b16v(o_sgb, 4 * S).rearrange("p (j t) -> p j t", j=4)
        T1 = [f32v(o_rt + s * 512, 512) for s in range(3)]
        T2 = [f32v(o_rt + 1536 + s * 512, 512) for s in range(3)]
        QR = [b16v(o_rt + 3072 + s * 256, 512) for s in range(3)]
        load_w(2, 0)
        it = [0]

        def qk_a(i):
            if i == 3:
                rec_tail()
            if i == 4:
                OP("sync", lambda e: e.dma_start(out=vscr_d.rearrange("(b p) e -> p b e", p=128), in_=V1.rearrange("p b h m -> p b (h m)")),
                   reads=[("V", b) for b in range(NB)], writes=["vscr"], dma_sem="vs0")
                OP("sync", lambda e: e.dma_start(out=V2.rearrange("p (n r) h m -> p n (r h m)", n=4),
                                                 in_=vscr_d.rearrange("(n i r) e -> i n (r e)", n=4, r=4)),
                   reads=["vscr"], writes=["V2"], dma_sem="vs1", extra=[tail_last[0]])
            if i == NB + 2:
                load_w(5, 1)
            s, blk = 1 - i // NB, i % NB
            b = i % 4
            r = i % 3
            for kc in range(8):
                OP("tensor", (lambda e, kc=kc: e.matmul(bank(b), lhsT=hT[:, kc, blk * 128:(blk + 1) * 128],
                                                        rhs=Wsl[s][:, kc, :], start=(kc == 0), stop=(kc == 7))),
                   reads=[("W", s), ("hT", kc, blk)], writes=[("pb", b)])
            pbk = bank(b)
            cos_b = view(cosF[:, blk * 64:blk * 64 + 1], [[0, 8], [1, 64]])
            sin_b = view(sinS[:, blk * 64:blk * 64 + 1], [[0, 8], [32, 2], [1, 32]])
            swp = view(pbk[:, 32:33], [[64, 8], [-32, 2], [1, 32]])
            OP("vector", (lambda e: e.tensor_tensor(
                out=T1[r].rearrange("p (h d) -> p h d", h=8), in0=pbk.rearrange("p (h d) -> p h d", h=8), in1=cos_b, op=ALU.mult)),
               reads=[("pb", b), "cosF"], writes=[("T1", r)])
            OP("vector", (lambda e: e.tensor_tensor(
                out=T2[r].rearrange("p (h s d) -> p h s d", h=8, s=2), in0=swp, in1=sin_b, op=ALU.mult)),
               reads=[("pb", b), "sinS"], writes=[("T2", r)])
            OP("gpsimd", (lambda e: e.tensor_tensor(out=QR[r], in0=T1[r], in1=T2[r], op=ALU.add)),
               reads=[("T1", r), ("T2", r)], writes=[("QR", r)])

        def qk_c(i):
            s, blk = 1 - i // NB, i % NB
            r = i % 3
            tb_ = 4 + (i % 4)
            dstT = QT if s == 0 else KT
            for j in range(4):
                OP("tensor", (lambda e, j=j: e.transpose(out=bank16(tb_)[:, j * 128:(j + 1) * 128],
                                                         in_=QR[r][:, j * 128:(j + 1) * 128], identity=ident)),
                   reads=[("QR", r), "ident"], writes=[("pb", tb_)])
            OP("scalar", (lambda e: e.activation(
                out=dstT[:, :, blk * 128:(blk + 1) * 128], in_=bank16(tb_)[:, 0:512].rearrange("p (j t) -> p j t", j=4), func=AF.Copy)),
               reads=[("pb", tb_)], writes=[("QKT", s, blk)])

        pipeline(2 * NB, [(0, qk_a), (2, qk_c)])
        vsrc3 = vscr_d.rearrange("(i r) e -> i r e", r=16)
        for hf in range(2):
            OP("sync", (lambda e, hf=hf: e.dma_start(out=V3h[hf].rearrange("p b h m -> p b (h m)"), in_=vsrc3[:, 8 * hf:8 * hf + 8, :])),
               reads=["vscr"], writes=["V3", "cosF", "sinS"], dma_sem="vs2", extra=[tail_last[0]])
        for cj in range(4):
            for tc in range(4):
                fm_proj(1, cj, tc, lambda b, cj=cj, tc=tc: OP(
                    "scalar", (lambda e: e.activation(out=sgb[:, cj, tc * 512:(tc + 1) * 512], in_=bank(b), func=AF.Silu)),
                    reads=[("pb", b)], writes=[("sgb", cj)], extra=[tail_last[0]]))

        last_pe4 = max(i_ for i_, o_ in enumerate(p.ops) if o_["eng"] == "tensor" and o_["fn"] is not None)
        p.fence(ALL_E, dep={last_pe4})
        cur[0] = o_hT
        o_P = alloc(5 * 512)
        o_LD = alloc(2 * 512)
        o_TN = alloc(512)
        o_YB = alloc(4 * 512)
        o_SQ = alloc(512)
        o_SQS = alloc(512)
        o_SQB = alloc(256)
        o_RB = alloc(512)
        assert cur[0] <= o_W
        Pt = [b16v(o_P + s * 512, 1024).rearrange("p (a n) -> p a n", a=2) for s in range(5)]
        LD = [f32v(o_LD + s * 512, 512) for s in range(2)]
        RD = LD
        TN = [f32v(o_TN, 512)] * 2
        YB4 = [f32v(o_YB + s * 512, 512) for s in range(4)]
        SQ4 = f32v(o_SQ, 512)
        SQS = f32v(o_SQS, 512)
        SQB = b16v(o_SQB, 512)
        RB = f32v(o_RB, 512)
        TRI_LE = maskT[:, 0:128]
        TRI_GE = maskT[:, 128:256]
        items = [(c, j, k) for c in range(4) for j in range(4) for k in range(5) if not (c == 0 and k == 3)]

        def s_specs(c, j, a, kind):
            rows = slice(a * 64, (a + 1) * 64)
            out = []
            if kind in (0, 1):
                for u in range(4):
                    qb = 4 * c + u
                    kb = qb - kind
                    if kb < 0:
                        continue
                    out.append((KT[rows, j, 128 * kb:128 * kb + 128], QT[rows, j, 128 * qb:128 * qb + 128], 128 * u, 128 * u + 128, 128))
            elif kind in (2, 3):
                ck = c - (kind - 2)
                for r in range(4):
                    out.append((KT[rows, j, 512 * ck + r:512 * (ck + 1):4], QT[rows, j, 512 * c + r:512 * (c + 1):4],
                                128 * r, 128 * r + 128, 128))
            else:
                Mk = 128
                for r in range(16):
                    out.append((KT[rows, j, r:S:16], QT[rows, j, 512 * c + r:512 * (c + 1):16], 32 * r, 32 * r + 32, Mk))
            return out

        def at_a(i):
            c, j, kind = items[i]
            sb_ = (i % 2) * 2
            pt = i % 5
            Pn = 128
            sp = [s_specs(c, j, a, kind) for a in range(2)]
            for idx in range(len(sp[0])):
                for a in range(2):
                    (lh, rh, lo, hi, M) = sp[a][idx]
                    OP("tensor", (lambda e, lh=lh, rh=rh, lo=lo, hi=hi, M=M, a=a: e.matmul(
                        bank(sb_ + a, lo, hi, 0, M), lhsT=lh, rhs=rh, start=True, stop=True)),
                       reads=[("QKT", 0, 4 * c + u) for u in range(4)] + [("QKT", 1, u) for u in range(4 * c + 4)], writes=[("pb", sb_ + a)])
            sview = view(ps[0:Pn, sb_ * 512:sb_ * 512 + 1], [[512, 2], [1, 512]])
            OP("scalar", (lambda e: e.activation(out=Pt[pt][0:Pn], in_=sview, func=AF.Exp, scale=0.125)),
               reads=[("pb", sb_), ("pb", sb_ + 1)], writes=[("P", pt)])
            if kind == 4:
                mview = view(TRI_LE[0:Pn, 32 * c:32 * c + 1], [[0, 2], [0, 16], [1, 32]])
                pv = Pt[pt][0:Pn].rearrange("p a (r i) -> p a r i", r=16)
            else:
                mview = view((TRI_LE if kind in (0, 2) else TRI_GE)[:, 0:1], [[0, 2], [0, 4], [1, 128]])
                pv = Pt[pt].rearrange("p a (u q) -> p a u q", u=4)
            OP("vector", (lambda e: e.tensor_tensor(out=pv, in0=pv, in1=mview, op=ALU.mult)),
               reads=[("P", pt), "maskT"], writes=[("P", pt)])

        def at_c(i):
            c, j, kind = items[i]
            pt = i % 5
            pair = c * 4 + j
            par = pair % 2
            bN, bD = 4 + 2 * par, 5 + 2 * par
            Pn = 128
            mms = []
            for a in range(2):
                h = 2 * j + a
                r0, r1 = a * 64, (a + 1) * 64
                mm = []
                if kind in (0, 1):
                    us = [u for u in range(4) if 4 * c + u - kind >= 0]
                    for u in us:
                        kb = 4 * c + u - kind
                        mm.append((bank(bN, 128 * u, 128 * u + 128, r0, r1), V1[:, kb, h, :], Pt[pt][:, a, 128 * u:128 * u + 128]))
                    lo = 128 * us[0]
                    mm.append((bank(bD, lo, 512, r0, r1), None, Pt[pt][:, a, lo:512]))
                elif kind in (2, 3):
                    ck = c - (kind - 2)
                    for r in range(4):
                        mm.append((bank(bN, r, 512, r0, r1)[:, ::4], V2[:, 4 * ck + r, h, :], Pt[pt][:, a, 128 * r:128 * r + 128]))
                        mm.append((bank(bD, r, 512, r0, r1)[:, ::4], None, Pt[pt][:, a, 128 * r:128 * r + 128]))
                else:
                    for r in range(16):
                        mm.append((bank(bN, r, 512, r0, r1)[:, ::16], V3h[r // 8][0:Pn, r % 8, h, :], Pt[pt][0:Pn, a, 32 * r:32 * r + 32]))
                        mm.append((bank(bD, r, 512, r0, r1)[:, ::16], None, Pt[pt][0:Pn, a, 32 * r:32 * r + 32]))
                mms.append(mm)
            first = [[kind == 0, kind == 0], [kind == 0, kind == 0]]
            for idx in range(len(mms[0])):
                for a in range(2):
                    (of, lh, rh) = mms[a][idx]
                    if lh is not None:
                        OP("tensor", (lambda e, of=of, lh=lh, rh=rh, st=first[a][0]: e.matmul(of, lhsT=lh, rhs=rh, start=st, stop=False,
                                                                                            skip_group_check=True)),
                           reads=[("P", pt), "V1", "V2", "V3"], writes=[("pb", bN)])
                        first[a][0] = False
                    else:
                        OP("tensor", (lambda e, of=of, rh=rh, st=first[a][1]: e.matmul(of, lhsT=ones64[0:Pn, :], rhs=rh, start=st, stop=False,
                                                                                      skip_group_check=True)),
                           reads=[("P", pt), "ones64"], writes=[("pb", bD)])
                        first[a][1] = False
            if kind != 4:
                return
            K = lambda n: (n, par)
            OP("scalar", lambda e: e.activation(out=LD[par], in_=bank(bD), func=AF.Ln), reads=[("pb", bD)], writes=[K("LD"), K("RD")])
            OP("scalar", lambda e: e.activation(out=RD[par], in_=LD[par], func=AF.Exp, scale=-1.0), reads=[K("LD")], writes=[K("RD")])
            OP("vector", lambda e: e.tensor_tensor(out=TN[par], in0=bank(bN), in1=RD[par], op=ALU.mult),
               reads=[("pb", bN), K("RD")], writes=["TN"])
            OP("gpsimd", (lambda e: e.tensor_tensor(out=YB4[j], in0=TN[par], in1=sgb[:, j, c * 512:(c + 1) * 512], op=ALU.mult)),
               reads=["TN", ("sgb", j)], writes=[("YB4", j)])
            if j == 0:
                OP("gpsimd", lambda e: e.tensor_tensor(out=SQS, in0=YB4[j], in1=YB4[j], op=ALU.mult), reads=[("YB4", j)], writes=["SQS"])
            else:
                OP("gpsimd", lambda e: e.tensor_tensor(out=SQ4, in0=YB4[j], in1=YB4[j], op=ALU.mult), reads=[("YB4", j)], writes=["SQ4"])
                OP("gpsimd", lambda e: e.tensor_tensor(out=SQS, in0=SQS, in1=SQ4, op=ALU.add), reads=["SQS", "SQ4"], writes=["SQS"])

        late_d = []

        def at_d0(i):
            c, j, kind = items[i]
            if kind == 4 and j == 3:
                OP("vector", lambda e: e.tensor_copy(out=SQB, in_=SQS), reads=["SQS"], writes=["SQB"])

        def at_d(i, force=False):
            c, j, kind = items[i]
            if kind != 4 or j != 3:
                return
            if i == len(items) - 1 and not force:
                late_d.append(i)
                return
            pair = c * 4 + j
            par = pair % 2
            bN, bD = 4 + 2 * par, 5 + 2 * par
            OP("tensor", (lambda e: e.matmul(bank(bD), lhsT=onesb, rhs=SQB, start=True, stop=True)),
               reads=["SQB", "onesb"], writes=[("pb", bD)])
            OP("scalar", lambda e: e.activation(out=RB, in_=bank(bD), func=AF.Ln, scale=1.0 / 512, bias=EPS),
               reads=[("pb", bD)], writes=["RB"])
            OP("scalar", lambda e: e.activation(out=RB, in_=RB, func=AF.Exp, scale=-0.5), reads=["RB"], writes=["RB"])
            for jj in range(4):
                OP("vector", (lambda e, jj=jj: e.scalar_tensor_tensor(out=yT[:, 4 + jj, c * 512:(c + 1) * 512], in0=YB4[jj],
                                                                      scalar=ppc(PP_NATT + jj), in1=RB, op0=ALU.mult, op1=ALU.mult)),
                   reads=["RB", ("YB4", jj), "pp"], writes=[("yT", 4 + jj)])

        pipeline(len(items), [(0, at_a), (3, at_c), (5, at_d0), (7, at_d)])

        last_pe = max(i_ for i_, o_ in enumerate(p.ops) if o_["eng"] == "tensor" and o_["fn"] is not None)
        p.fence(ALL_E, dep={last_pe})
        cur[0] = RA
        o_xs5 = alloc(4 * D)
        o_t5 = alloc(4 * D)
        o_j5 = alloc(D // 2)
        xs5 = [f32v(o_xs5 + s * D, D) for s in range(4)]
        T5 = [f32v(o_t5 + s * D, D) for s in range(4)]
        junk5 = b16v(o_j5, D)
        stores = []

        def p5_load(blk):
            s = blk % 4
            OP("sync", (lambda e: e.dma_start(out=xs5[s], in_=x_d[blk * 128:(blk + 1) * 128, :])),
               writes=[("xs5", s)], dma_sem="xr%d" % s)

        def p5_main(blk):
            s = blk % 4
            b0 = (blk % 2) * 2
            for h in range(2):
                for ec in range(8):
                    OP("tensor", (lambda e, h=h, ec=ec: e.matmul(
                        bank(b0 + h), lhsT=yT[:, ec, blk * 128:(blk + 1) * 128], rhs=woutb[:, ec, h * 512:(h + 1) * 512],
                        start=(ec == 0), stop=(ec == 7))),
                       reads=[("yT", ec), "wout"], writes=[("pb", b0 + h)])
            mix = ps[:, b0 * 512:b0 * 512 + 1024]
            OP("scalar", (lambda e: e.activation(out=junk5, in_=mix, func=AF.Square, accum_out=smc(SM_SSP + blk))),
               reads=[("pb", b0), ("pb", b0 + 1), "sm"], writes=["junk5", ("ssp", blk)])
            OP("scalar", (lambda e: e.activation(out=smc(SM_STDP + blk), in_=smc(SM_SSP + blk), func=AF.Sqrt,
                                                 scale=1.0 / D, bias=EPS)),
               reads=[("ssp", blk)], writes=[("stdp", blk)])
            OP("vector", (lambda e: e.reciprocal(out=smc(SM_RSTDP + blk), in_=smc(SM_STDP + blk))),
               reads=[("stdp", blk)], writes=[("rstdp", blk)])
            OP("vector", (lambda e: e.scalar_tensor_tensor(out=T5[s], in0=mix, scalar=smc(SM_RSTDP + blk), in1=gn_bc,
                                                           op0=ALU.mult, op1=ALU.mult)),
               reads=[("pb", b0), ("pb", b0 + 1), ("rstdp", blk), "gn_bc"], writes=[("T5", s)])
            OP("vector", (lambda e: e.tensor_tensor(out=T5[s], in0=T5[s], in1=xs5[s], op=ALU.add)),
               reads=[("T5", s), ("xs5", s)], writes=[("T5", s)])
            stores.append(OP("sync", (lambda e: e.dma_start(out=out_d[blk * 128:(blk + 1) * 128, :], in_=T5[s])),
                             reads=[("T5", s)], dma_sem="st%d" % s))
            if blk == 7:
                at_d(late_d[0], force=True)

        pipeline(NB, [(0, p5_load), (2, p5_main)])
        OP("sync", None, extra=stores)
        p.emit()
    return nc


_CACHE = {}


def _c_mult(j):
    c = np.zeros_like(j, dtype=np.float32)
    c += ((j >= 0) & (j <= 128))
    c += ((j >= 0) & (j % 4 == 0) & (j <= 512))
    c += ((j >= 0) & (j % 16 == 0))
    return c.astype(np.float32)


def kernel(x, c, positions, w_ada, b_ada, norm_pre, norm_post, w_in, conv_w, conv_b,
           w_rg_a, b_rg_a, w_rg_x, b_rg_x, lru_lambda, norm_rec, norm_att, w_out):
    x = np.asarray(x, np.float32)
    B = x.shape[0]
    if "nc" not in _CACHE:
        _CACHE["nc"] = build_program()
    nc = _CACHE["nc"]
    f = lambda a: np.ascontiguousarray(np.asarray(a, np.float32))
    col = lambda v, n: f(v).reshape(n, 128).T
    rows = np.concatenate([f(b_ada)[0], f(norm_post)[0]])[None, :]
    wbd = np.zeros((128, 2, 4, 128), np.float32)
    for g, w in enumerate((f(w_rg_a)[0], f(w_rg_x)[0])):
        for cj in range(4):
            wbd[0:64, g, cj, 0:64] = w[2 * cj]
            wbd[64:128, g, cj, 64:128] = w[2 * cj + 1]
    wbd = wbd.reshape(128, 1024)
    pidx = np.arange(128)[:, None]
    xidx = np.arange(S)[None, :]
    q128 = np.arange(128)[None, :]
    maskT = np.concatenate([(pidx <= q128), (pidx >= q128)], axis=1).astype(np.float32)
    half = 32
    invf = (10000.0 ** (-np.arange(half, dtype=np.float32) / half)).astype(np.float32)
    pp_shared = np.zeros((128, NPP), np.float32)
    pp_shared[:, PP_NPRE:PP_NPRE + 8] = col(norm_pre[0], 8)
    pp_shared[:, PP_CONVW:PP_CONVW + 16] = f(conv_w)[0].reshape(4, 4, 128).transpose(2, 1, 0).reshape(128, 16)
    pp_shared[:, PP_CONVB:PP_CONVB + 4] = col(conv_b[0], 4)
    pp_shared[:, PP_BA:PP_BA + 4] = col(b_rg_a[0], 4)
    pp_shared[:, PP_BX:PP_BX + 4] = col(b_rg_x[0], 4)
    pp_shared[:, PP_LAM:PP_LAM + 4] = col(lru_lambda[0], 4)
    pp_shared[:, PP_NREC:PP_NREC + 4] = col(norm_rec[0], 4)
    pp_shared[:, PP_NATT:PP_NATT + 4] = col(norm_att[0], 4)
    pp_shared[:, PP_INVF:PP_INVF + 32] = invf[None, :]
    w_ada2 = f(w_ada)[0]
    w_in2 = f(w_in)[0]
    w_out2 = f(w_out)[0]
    in_maps = []
    for b in range(B):
        pp = pp_shared.copy()
        pp[:, PP_C:PP_C + 8] = col(np.asarray(c)[b], 8)
        pos = np.ascontiguousarray(np.asarray(positions)[b].astype(np.int32).reshape(NB, 128).T)
        in_maps.append({"x": np.ascontiguousarray(x[b]), "pp": pp, "pos": pos, "w_ada": w_ada2, "rows": rows,
                        "w_in": w_in2, "w_out": w_out2, "wbd": wbd, "maskT": maskT})
    res = run_bass_kernel_spmd(nc, in_maps, core_ids=list(range(B)))
    return np.stack([np.asarray(r["out"], np.float32) for r in res.results], axis=0)
```

```python
import contextlib
import numpy as np
import concourse.bass as bass
import concourse.mybir as mybir
from concourse.ap import AP
from concourse.bass_utils import run_bass_kernel_spmd

F32 = mybir.dt.float32
BF16 = mybir.dt.bfloat16
I32 = mybir.dt.int32
ALU = mybir.AluOpType
AF = mybir.ActivationFunctionType

D = 1024
S = 2048
NB = 16
EPS = 1e-6
PI = float(np.pi)
ENGS = ("tensor", "scalar", "vector", "gpsimd", "sync")
ALL_E = ENGS


class Prog:
    def __init__(self, nc):
        self.nc = nc
        self.ops = []
        self.touch = {}

    def op(self, eng, fn, reads=(), writes=(), dma_sem=None, ndma=1, extra=()):
        i = len(self.ops)
        self.ops.append(dict(eng=eng, fn=fn, reads=tuple(reads), writes=tuple(writes),
                             dma_sem=dma_sem, ndma=ndma, extra=set(extra)))
        for k in tuple(reads) + tuple(writes):
            self.touch.setdefault(k, set()).add(i)
        return i

    def snapshot(self):
        last = {}
        for i, o in enumerate(self.ops):
            if o["fn"] is None:
                continue
            last[("d", o["dma_sem"]) if o["dma_sem"] else ("e", o["eng"])] = i
        return set(last.values())

    def fence(self, engs, keys=None, dep=None):
        if dep is None:
            dep = self.snapshot()
        for e in engs:
            self.op(e, None, extra=dep)

    def emit(self):
        nc = self.nc
        ops = self.ops
        last_writer = {}
        readers = {}
        for i, o in enumerate(ops):
            deps = set(o["extra"])
            for k in o["reads"]:
                if k in last_writer:
                    deps.add(last_writer[k])
            for k in o["writes"]:
                if k in last_writer:
                    deps.add(last_writer[k])
                for r in readers.get(k, ()):
                    deps.add(r)
            deps.discard(i)
            deps = {d for d in deps if ops[d]["fn"] is not None}
            if o["eng"] == "tensor":
                deps = {d for d in deps if ops[d]["eng"] != "tensor"}
            o["deps"] = deps
            for k in o["reads"]:
                readers.setdefault(k, []).append(i)
            for k in o["writes"]:
                last_writer[k] = i
                readers[k] = []
        signal = set()
        for o in ops:
            signal |= o["deps"]
        counts = {}
        sem_names = []
        for i, o in enumerate(ops):
            o["sem"] = None
            if o["fn"] is None:
                continue
            if o["dma_sem"] is not None:
                name = "d_" + o["dma_sem"]
                counts[name] = counts.get(name, 0) + 16 * o["ndma"]
                o["sem"], o["val"] = name, counts[name]
            elif i in signal:
                name = "e_" + o["eng"]
                counts[name] = counts.get(name, 0) + 1
                o["sem"], o["val"] = name, counts[name]
            if o["sem"] and o["sem"] not in sem_names:
                sem_names.append(o["sem"])
        with contextlib.ExitStack() as st:
            sems = {n: st.enter_context(nc.semaphore(n)) for n in sem_names}
            block = st.enter_context(nc.Block())
            per_eng = {e: [i for i, o in enumerate(ops) if o["eng"] == e] for e in ENGS}

            def make(ename):
                def body(eng):
                    waited = {}
                    for i in per_eng[ename]:
                        o = ops[i]
                        need = {}
                        for d in o["deps"]:
                            od = ops[d]
                            need[od["sem"]] = max(need.get(od["sem"], 0), od["val"])
                        for s, v in need.items():
                            if waited.get(s, 0) < v:
                                eng.wait_ge(sems[s], v)
                                waited[s] = v
                        if o["fn"] is None:
                            continue
                        ins = o["fn"](eng)
                        if o["sem"] is not None:
                            if o["dma_sem"] is not None:
                                lst = ins if isinstance(ins, (list, tuple)) else [ins]
                                assert len(lst) == o["ndma"]
                                for x in lst:
                                    x.then_inc(sems[o["sem"]], 16)
                            else:
                                ins.then_inc(sems[o["sem"]], 1)
                return body

            for e in ENGS:
                if per_eng[e]:
                    getattr(block, e)(make(e))


def pipeline(n, stages):
    ml = max(l for l, _ in stages)
    for t in range(n + ml):
        for lag, fn in stages:
            i = t - lag
            if 0 <= i < n:
                fn(i)


def view(ap, dims, off=0):
    return AP(ap.tensor, ap.offset + off, [list(ap.ap[0])] + [list(d) for d in dims])


PP_C, PP_NPRE, PP_CONVW, PP_CONVB, PP_BA, PP_BX, PP_LAM, PP_NREC, PP_NATT, PP_INVF = 0, 8, 16, 32, 36, 40, 44, 48, 52, 56
NPP = 88
SM_SC, SM_G1, SM_SHIFT, SM_C8, SM_NC8, SM_N2C8, SM_SS, SM_STD, SM_RSTD = 0, 8, 16, 24, 28, 32, 36, 52, 68
SM_CARRY, SM_ONE, SM_SSP, SM_STDP, SM_RSTDP, SM_POSF, SM_SP, SM_ZERO = 84, 88, 96, 112, 128, 144, 160, 200


def build_program():
    nc = bass.Bass("TRN2", target_bir_lowering=False)
    x_d = nc.dram_tensor("x", [S, D], F32, kind="ExternalInput").ap()
    pp_d = nc.dram_tensor("pp", [128, NPP], F32, kind="ExternalInput").ap()
    pos_d = nc.dram_tensor("pos", [128, NB], I32, kind="ExternalInput").ap()
    wada_d = nc.dram_tensor("w_ada", [D, 3 * D], F32, kind="ExternalInput").ap()
    rows_d = nc.dram_tensor("rows", [1, 4 * D], F32, kind="ExternalInput").ap()
    win_d = nc.dram_tensor("w_in", [D, 3 * D], F32, kind="ExternalInput").ap()
    wout_d = nc.dram_tensor("w_out", [D, D], F32, kind="ExternalInput").ap()
    wbd_d = nc.dram_tensor("wbd", [128, 8 * 128], F32, kind="ExternalInput").ap()
    mask_d = nc.dram_tensor("maskT", [128, 256], F32, kind="ExternalInput").ap()
    vscr_d = nc.dram_tensor("vscr", [S, 512], BF16, kind="Internal").ap()
    out_d = nc.dram_tensor("out", [S, D], F32, kind="ExternalOutput").ap()

    NW = 53000
    with contextlib.ExitStack() as st:
        big = st.enter_context(nc.sbuf_tensor("big", [128, NW], F32))
        ps = st.enter_context(nc.psum_tensor("ps", [128, 4096], F32))
        p = Prog(nc)
        OP = p.op

        def bank(b, lo=0, hi=512, p0=0, p1=128):
            return ps[p0:p1, b * 512 + lo:b * 512 + hi]

        def bank16(b, p0=0, p1=128):
            return ps[p0:p1, b * 512:(b + 1) * 512].bitcast(BF16)

        cur = [0]

        def alloc(nwords):
            o = cur[0]
            cur[0] += nwords
            assert cur[0] <= NW, cur[0]
            return o

        def f32v(off, n, p0=0, p1=128):
            return big[p0:p1, off:off + n]

        def b16v(off, n, p0=0, p1=128):
            return big[p0:p1, off:off + n // 2].bitcast(BF16)

        o_yT = alloc(8 * S // 2)
        o_wout = alloc(8 * D // 2)
        o_gn = alloc(D)
        o_mask = alloc(128)
        o_ones64 = alloc(32)
        o_ident = alloc(64)
        o_ones = alloc(64)
        o_onesrow = alloc(128)
        o_pp = alloc(NPP)
        o_small = alloc(256)
        o_wbd = alloc(1024)
        yT = b16v(o_yT, 8 * S).rearrange("p (e t) -> p e t", e=8)
        woutb = b16v(o_wout, 8 * D).rearrange("p (e n) -> p e n", e=8)
        gn_bc = f32v(o_gn, D)
        maskT = b16v(o_mask, 256)
        ones64 = b16v(o_ones64, 64)
        ident = b16v(o_ident, 128)
        onesb = b16v(o_ones, 128)
        onesrow = f32v(o_onesrow, 128, 0, 1)
        pp = f32v(o_pp, NPP)
        sm = f32v(o_small, 256)
        wbd = f32v(o_wbd, 1024).rearrange("p (g m) -> p g m", g=8)

        def smc(c0, n=1):
            return sm[:, c0:c0 + n]

        def ppc(c0, n=1):
            return pp[:, c0:c0 + n]

        o_hT = alloc(8 * S // 2)
        o_W = alloc(2 * 2048)
        o_cs = alloc(2 * NB * 64)
        RA = cur[0]
        hT = b16v(o_hT, 8 * S).rearrange("p (k t) -> p k t", k=8)
        Wsl = [b16v(o_W + s * 2048, 8 * 512).rearrange("p (k n) -> p k n", k=8) for s in range(2)]
        cosF = f32v(o_cs, NB * 64)
        sinS = f32v(o_cs + NB * 64, NB * 64)

        cur[0] = RA
        o_rows = alloc(4 * D)
        o_wada = alloc(3 * 3 * D)
        o_modrow = alloc(3 * D)
        o_gnrow = alloc(D)
        o_ident32 = o_yT
        o_mask32 = o_yT + 128
        o_wbd32 = o_yT + 128 + 256
        o_ang = o_yT + 128 + 256 + 1024
        assert o_ang + 4 * 512 + 64 <= o_yT + 8192
        rows_sb = f32v(o_rows, 4 * D, 0, 1)
        wada_sl = [f32v(o_wada + s * 3 * D, 3 * D) for s in range(3)]
        modrow = f32v(o_modrow, 3 * D, 0, 1)
        gnrow = f32v(o_gnrow, D, 0, 1)
        ident32 = f32v(o_ident32, 128)
        mask32 = f32v(o_mask32, 256)
        wbd32 = f32v(o_wbd32, 1024)
        posi = big[:, o_ang + 2048:o_ang + 2048 + NB].bitcast(I32)
        ANG = f32v(o_ang, 512)
        KI = big[:, o_ang + 512:o_ang + 1024].bitcast(I32)
        KF = f32v(o_ang + 1024, 512)
        SN = f32v(o_ang + 1536, 512)

        OP("sync", lambda e: e.dma_start(out=pp, in_=pp_d), writes=["pp"], dma_sem="c0")
        OP("sync", lambda e: e.dma_start(out=posi, in_=pos_d), writes=["posi"], dma_sem="c1")
        OP("sync", lambda e: e.dma_start(out=rows_sb, in_=rows_d), writes=["rows"], dma_sem="c2")
        OP("sync", lambda e: e.dma_start(out=mask32, in_=mask_d), writes=["mask32"], dma_sem="c3")
        OP("sync", lambda e: e.dma_start(out=wbd.rearrange("p g m -> p (g m)"), in_=wbd_d), writes=["wbd"], dma_sem="c4")
        win_v = win_d.rearrange("(k p) n -> p k n", p=128)

        def load_w(g, s):
            OP("gpsimd", (lambda e, g=g, s=s: e.dma_start(out=Wsl[s], in_=win_v[:, :, g * 512:(g + 1) * 512])),
               writes=[("W", s)], dma_sem="W%d" % s)

        OP("gpsimd", lambda e: e.memset(sm, 0.0), writes=["sm"])
        OP("gpsimd", lambda e: e.memset(smc(SM_ONE, 8), 1.0), reads=["sm"], writes=["sm_one"])
        OP("gpsimd", lambda e: e.memset(onesrow, 1.0), writes=["onesrow"])
        OP("gpsimd", lambda e: e.memset(ident32, 0.0), writes=["ident32"])
        OP("gpsimd", lambda e: e.affine_select(out=ident32, in_=ident32, pattern=[[-1, 128]],
                                                compare_op=ALU.not_equal, fill=1.0, base=0, channel_multiplier=1),
           reads=["ident32"], writes=["ident32"])
        OP("vector", lambda e: e.tensor_copy(out=ident, in_=ident32), reads=["ident32"], writes=["ident"])
        OP("gpsimd", lambda e: e.memset(onesb, 1.0), writes=["onesb"])
        OP("gpsimd", lambda e: e.memset(ones64, 1.0), writes=["ones64"])
        load_w(0, 0)
        load_w(1, 1)
        OP("vector", lambda e: e.tensor_copy(out=maskT, in_=mask32), reads=["mask32"], writes=["maskT"])
        OP("scalar", lambda e: e.activation(out=smc(SM_SC, 8), in_=ppc(PP_C, 8), func=AF.Silu),
           reads=["pp", "sm"], writes=["sc"])
        OP("scalar", lambda e: e.activation(out=smc(SM_SP, 4), in_=ppc(PP_LAM, 4), func=AF.Exp, scale=-1.0),
           reads=["pp", "sm"], writes=["sp"])
        OP("scalar", lambda e: e.activation(out=smc(SM_SP, 4), in_=smc(SM_SP, 4), func=AF.Ln, bias=1.0),
           reads=["sp"], writes=["sp"])
        OP("vector", lambda e: e.tensor_scalar(out=smc(SM_C8, 4), in0=smc(SM_SP, 4), scalar1=8.0, scalar2=None, op0=ALU.mult),
           reads=["sp", "sm"], writes=["c8"])
        OP("vector", lambda e: e.tensor_scalar(out=smc(SM_NC8, 4), in0=smc(SM_SP, 4), scalar1=-8.0, scalar2=None, op0=ALU.mult),
           reads=["sp", "sm"], writes=["nc8"])
        OP("vector", lambda e: e.tensor_scalar(out=smc(SM_N2C8, 4), in0=smc(SM_SP, 4), scalar1=-16.0, scalar2=None, op0=ALU.mult),
           reads=["sp", "sm"], writes=["n2c8"])
        OP("vector", lambda e: e.tensor_copy(out=smc(SM_POSF, NB), in_=posi), reads=["posi", "sm"], writes=["posf"])
        OP("vector", lambda e: e.tensor_tensor(
            out=ANG.rearrange("p (b f) -> p b f", b=NB),
            in0=view(smc(SM_POSF, NB), [[1, NB], [0, 32]]),
            in1=view(ppc(PP_INVF, 32), [[0, NB], [1, 32]]), op=ALU.mult),
           reads=["posf", "pp"], writes=["ANG"])
        C1 = 6.28125
        C2 = 2.0 * np.pi - 6.28125
        OP("vector", lambda e: e.tensor_scalar(out=KI, in0=ANG, scalar1=1.0 / (2 * PI), scalar2=None, op0=ALU.mult),
           reads=["ANG"], writes=["KI"])
        OP("vector", lambda e: e.tensor_copy(out=KF, in_=KI), reads=["KI"], writes=["KF"])
        OP("vector", lambda e: e.scalar_tensor_tensor(out=ANG, in0=KF, scalar=-C1, in1=ANG, op0=ALU.mult, op1=ALU.add),
           reads=["KF", "ANG"], writes=["ANG"])
        OP("vector", lambda e: e.scalar_tensor_tensor(out=ANG, in0=KF, scalar=-float(C2), in1=ANG, op0=ALU.mult, op1=ALU.add),
           reads=["KF", "ANG"], writes=["ANG"])

        def wrap(T):
            OP("vector", lambda e: e.tensor_scalar(out=KF, in0=T, scalar1=PI, scalar2=-2 * PI, op0=ALU.is_gt, op1=ALU.mult),
               reads=["ANG", "KF"], writes=["KF"])
            OP("vector", lambda e: e.tensor_tensor(out=T, in0=T, in1=KF, op=ALU.add), reads=["KF", "ANG"], writes=["ANG"])
            OP("vector", lambda e: e.tensor_scalar(out=KF, in0=T, scalar1=-PI, scalar2=2 * PI, op0=ALU.is_lt, op1=ALU.mult),
               reads=["ANG", "KF"], writes=["KF"])
            OP("vector", lambda e: e.tensor_tensor(out=T, in0=T, in1=KF, op=ALU.add), reads=["KF", "ANG"], writes=["ANG"])

        wrap(ANG)
        OP("scalar", lambda e: e.activation(out=SN, in_=ANG, func=AF.Sin), reads=["ANG"], writes=["SN"])
        sin3 = SN.rearrange("p (b f) -> p b f", b=NB)
        sinS3 = sinS.rearrange("p (b f) -> p b f", b=NB)
        cosF3 = cosF.rearrange("p (b f) -> p b f", b=NB)
        OP("vector", lambda e: e.tensor_scalar(out=sinS3[:, :, 0:32], in0=sin3, scalar1=-1.0, scalar2=None, op0=ALU.mult),
           reads=["SN"], writes=["sinS"])
        OP("vector", lambda e: e.tensor_copy(out=sinS3[:, :, 32:64], in_=sin3), reads=["SN"], writes=["sinS"])
        OP("vector", lambda e: e.tensor_scalar(out=ANG, in0=ANG, scalar1=PI / 2, scalar2=None, op0=ALU.add),
           reads=["ANG", "SN"], writes=["ANG"])
        wrap(ANG)
        OP("scalar", lambda e: e.activation(out=SN, in_=ANG, func=AF.Sin), reads=["ANG", "sinS"], writes=["SN"])
        OP("vector", lambda e: e.tensor_copy(out=cosF3[:, :, 0:32], in_=sin3), reads=["SN"], writes=["cosF"])
        OP("vector", lambda e: e.tensor_copy(out=cosF3[:, :, 32:64], in_=sin3), reads=["SN"], writes=["cosF"])

        o_xs = alloc(3 * D)
        o_xn = alloc(3 * D // 2)
        o_junk = alloc(D // 2)
        xs = [f32v(o_xs + s * D, D) for s in range(3)]
        xn = [b16v(o_xn + s * (D // 2), D) for s in range(3)]
        junk = b16v(o_junk, D)

        def wada_step(kc):
            s = kc % 3
            OP("sync", (lambda e: e.dma_start(out=wada_sl[s], in_=wada_d[kc * 128:(kc + 1) * 128, :])),
               writes=[("wada", s)], dma_sem="wada%d" % s)
            for n in range(6):
                OP("tensor", (lambda e, n=n: e.matmul(
                    bank(n, 0, 512, 0, 1), lhsT=smc(SM_SC + kc), rhs=wada_sl[s][:, n * 512:(n + 1) * 512],
                    start=(kc == 0), stop=(kc == 7))), reads=[("wada", s), "sc"], writes=[("pb", n)])

        def p1_a(blk):
            s = blk % 3
            r = blk % 3
            OP("sync", (lambda e: e.dma_start(out=xs[s], in_=x_d[blk * 128:(blk + 1) * 128, :])),
               writes=[("xs", s)], dma_sem="xs%d" % s)
            OP("scalar", (lambda e: e.activation(out=junk, in_=xs[s], func=AF.Square, accum_out=smc(SM_SS + blk))),
               reads=[("xs", s), "sm"], writes=["junk", ("ss", blk)])
            OP("scalar", (lambda e: e.activation(out=smc(SM_STD + blk), in_=smc(SM_SS + blk), func=AF.Sqrt,
                                                 scale=1.0 / D, bias=EPS)),
               reads=[("ss", blk)], writes=[("std", blk)])
            OP("vector", (lambda e: e.reciprocal(out=smc(SM_RSTD + blk), in_=smc(SM_STD + blk))),
               reads=[("std", blk)], writes=[("rstd", blk)])
            OP("vector", (lambda e: e.tensor_scalar(out=xn[r], in0=xs[s], scalar1=smc(SM_RSTD + blk), scalar2=None,
                                                    op0=ALU.mult)),
               reads=[("xs", s), ("rstd", blk)], writes=[("xn", r)])

        def p1_c(blk):
            r = blk % 3
            pb = 6 + (blk % 2)
            for kc in range(8):
                OP("tensor", (lambda e, kc=kc: e.transpose(out=bank16(pb)[:, kc * 128:(kc + 1) * 128],
                                                           in_=xn[r][:, kc * 128:(kc + 1) * 128], identity=ident)),
                   reads=[("xn", r), "ident"], writes=[("pb", pb)])
            OP("scalar", (lambda e: e.activation(out=hT[:, :, blk * 128:(blk + 1) * 128],
                                                 in_=bank16(pb).rearrange("p (k t) -> p k t", k=8), func=AF.Copy)),
               reads=[("pb", pb)], writes=[("hT", kc_, blk) for kc_ in range(8)])

        def p01(i):
            if i % 2 == 0 and i // 2 < 8:
                wada_step(i // 2)
            p1_a(i)

        pipeline(NB, [(0, p01), (1, p1_c)])
        OP("gpsimd", lambda e: e.dma_start(out=woutb, in_=wout_d.rearrange("(e p) n -> p e n", p=128)),
           writes=["wout"], dma_sem="c5", extra=[len(p.ops) - 1])
        OP("vector", lambda e: e.tensor_tensor(out=modrow, in0=ps[0:1, 0:3 * D], in1=rows_sb[:, 0:3 * D], op=ALU.add),
           reads=[("pb", n) for n in range(6)] + ["rows"], writes=["modrow"])
        for kc in range(8):
            for w in range(2):
                OP("tensor", (lambda e, kc=kc, w=w: e.matmul(
                    bank(6, w * 8 + kc, w * 8 + kc + 1), lhsT=modrow[:, w * D + kc * 128:w * D + (kc + 1) * 128],
                    rhs=sm[0:1, SM_ONE:SM_ONE + 1], start=True, stop=True)),
                   reads=["modrow", "sm_one"], writes=[("pb", 6)])
        OP("vector", lambda e: e.tensor_copy(out=smc(SM_SHIFT, 8), in_=bank(6, 0, 8)),
           reads=[("pb", 6), "sm"], writes=["shift"])
        OP("vector", lambda e: e.scalar_tensor_tensor(out=smc(SM_G1, 8), in0=bank(6, 8, 16), scalar=1.0,
                                                       in1=ppc(PP_NPRE, 8), op0=ALU.add, op1=ALU.mult),
           reads=[("pb", 6), "pp", "sm"], writes=["g1"])
        snapP1 = p.snapshot()
        for kc in range(8):
            OP("vector", (lambda e, kc=kc: e.tensor_scalar(out=hT[:, kc, :], in0=hT[:, kc, :], scalar1=smc(SM_G1 + kc),
                                                           scalar2=smc(SM_SHIFT + kc), op0=ALU.mult, op1=ALU.add)),
               reads=[("hT", kc, b) for b in range(NB)] + ["g1", "shift"], writes=[("hT", kc, b) for b in range(NB)])

        gn_last = [None]

        def rec_gn():
            OP("vector", lambda e: e.tensor_tensor(out=gnrow, in0=modrow[:, 2 * D:3 * D], in1=rows_sb[:, 3 * D:4 * D], op=ALU.mult),
               reads=["modrow", "rows"], writes=["gnrow"])
            for h in range(2):
                OP("tensor", (lambda e, h=h: e.matmul(bank(4 + h), lhsT=onesrow, rhs=gnrow[:, h * 512:(h + 1) * 512],
                                                      start=True, stop=True)),
                   reads=["gnrow", "onesrow", "modrow"], writes=[("pb", 4 + h)])
                gn_last[0] = OP("vector", (lambda e, h=h: e.tensor_copy(out=gn_bc[:, h * 512:(h + 1) * 512], in_=bank(4 + h))),
                                reads=[("pb", 4 + h)], writes=["gn_bc"])

        p.fence(ALL_E, dep=snapP1)
        cur[0] = RA
        TH_ = 1024
        o_xa = alloc(2 * 2052)
        o_sg = alloc(2 * S // 2)
        o_tA = alloc(6 * TH_ + TH_ + 512 + 512)
        o_sqa = alloc(4 * S // 2)
        o_tB2 = alloc(4096)
        o_sgt = alloc(2 * 512)
        SGT = [f32v(o_sgt + s_ * 512, 512) for s_ in range(2)]
        o_tB1 = o_yT + 4096
        XA = [f32v(o_xa + s * 2052, 2052) for s in range(2)]
        SGA = [b16v(o_sg + s * (S // 2), S) for s in range(2)]
        SQall = b16v(o_sqa, 4 * S).rearrange("p (c t) -> p c t", c=4)
        RBC = f32v(o_tB2 + 2048, S)
        for s_ in range(2):
            OP("gpsimd", (lambda e, s_=s_: e.memset(XA[s_][:, 0:4], 0.0)), writes=[("XA", s_)])
        pbi = [0]

        def fm_proj(s, cj, tc, evac, b=None):
            if b is None:
                b = pbi[0] % 4
                pbi[0] += 1
            for kc in range(8):
                OP("tensor", (lambda e, kc=kc, b=b: e.matmul(bank(b), lhsT=Wsl[s][:, kc, cj * 128:(cj + 1) * 128],
                                                            rhs=hT[:, kc, tc * 512:(tc + 1) * 512], start=(kc == 0), stop=(kc == 7))),
                   reads=[("W", s)] + [("hT", kc, 4 * tc + i) for i in range(4)], writes=[("pb", b)])
            evac(b)

        def inproj_mm(s_, cj):
            for tc in range(4):
                b = 4 + tc
                for kc in range(8):
                    OP("tensor", (lambda e, kc=kc, b=b, tc=tc: e.matmul(bank(b), lhsT=Wsl[s_][:, kc, cj * 128:(cj + 1) * 128],
                                                                        rhs=hT[:, kc, tc * 512:(tc + 1) * 512], start=(kc == 0), stop=(kc == 7))),
                       reads=[("W", s_)] + [("hT", kc, 4 * tc + i) for i in range(4)], writes=[("pb", b)])

        def evac_xa(cj):
            sl = cj % 2
            for tc in range(4):
                OP("scalar", (lambda e, tc=tc: e.activation(out=XA[sl][:, 4 + tc * 512:4 + (tc + 1) * 512], in_=bank(4 + tc), func=AF.Copy)),
                   reads=[("pb", 4 + tc)], writes=[("XA", sl)], extra=([gn_last[0]] if cj == 1 else ()))

        def evac_ga(cj):
            sl = cj % 2
            for tc in range(4):
                OP("scalar", (lambda e, tc=tc: e.activation(out=SGT[tc % 2], in_=bank(4 + tc), func=AF.Sigmoid)),
                   reads=[("pb", 4 + tc)], writes=[("SGT", tc % 2)])
                OP("vector", (lambda e, tc=tc: e.tensor_tensor(out=SGA[sl][:, tc * 512:(tc + 1) * 512], in0=bank(4 + tc), in1=SGT[tc % 2], op=ALU.mult)),
                   reads=[("pb", 4 + tc), ("SGT", tc % 2)], writes=[("SGA", sl)])

        tb = [
            [f32v(o_tA + i * TH_, TH_) for i in range(7)] + [b16v(o_tA + 7 * TH_, TH_), b16v(o_tA + 7 * TH_ + 512, TH_)],
            [f32v(o_tB1 + i * TH_, TH_) for i in range(4)] + [f32v(o_tB2 + i * TH_, TH_) for i in range(3)]
            + [b16v(o_tB2 + 3 * TH_, TH_), b16v(o_tB2 + 3 * TH_ + 512, TH_)],
        ]

        def rec_conv0(it):
            cj, hh = it // 2, it % 2
            par = it % 2
            XC = tb[par][0]
            t0 = hh * TH_
            X = XA[cj % 2]
            OP("vector", (lambda e: e.tensor_scalar(out=XC, in0=X[:, 4 + t0:4 + t0 + TH_], scalar1=ppc(PP_CONVW + cj * 4 + 3),
                                                    scalar2=ppc(PP_CONVB + cj), op0=ALU.mult, op1=ALU.add)),
               reads=[("XA", cj % 2), "pp"], writes=[("XC", par)])

        def rec_conv(it):
            cj, hh = it // 2, it % 2
            par = it % 2
            XC, RR, II, TT, AA, A2, CT, XCB, SQ = tb[par]
            K = lambda n: (n, par)
            t0 = hh * TH_
            X = XA[cj % 2]
            cw = lambda k: ppc(PP_CONVW + cj * 4 + k)
            for k in (2, 1, 0):
                OP("vector", (lambda e, k=k: e.scalar_tensor_tensor(out=XC, in0=X[:, 1 + k + t0:1 + k + t0 + TH_], scalar=cw(k), in1=XC,
                                                                    op0=ALU.mult, op1=ALU.add)),
                   reads=[("XA", cj % 2), "pp", K("XC")], writes=[K("XC")])
            for g in range(2):
                for q in range(2):
                    b = g * 2 + q
                    OP("tensor", (lambda e, g=g, q=q, b=b: e.matmul(bank(b), lhsT=wbd[:, g * 4 + cj, :], rhs=XC[:, q * 512:(q + 1) * 512],
                                                                   start=True, stop=True)),
                       reads=[K("XC"), "wbd"], writes=[("pb", b)])

        def rec_act(it, evac=None):
            cj, hh = it // 2, it % 2
            par = it % 2
            XC, RR, II, TT, AA, A2, CT, XCB, SQ = tb[par]
            K = lambda n: (n, par)
            OP("scalar", (lambda e: e.activation(out=RR, in_=ps[:, 0:1024], func=AF.Sigmoid, bias=ppc(PP_BA + cj))),
               reads=[("pb", 0), ("pb", 1), "pp"], writes=[K("RR")])
            OP("scalar", (lambda e: e.activation(out=II, in_=ps[:, 1024:2048], func=AF.Sigmoid, bias=ppc(PP_BX + cj))),
               reads=[("pb", 2), ("pb", 3), "pp"], writes=[K("II")])
            OP("scalar", (lambda e: e.activation(out=TT, in_=RR, func=AF.Tanh, scale=smc(SM_C8 + cj))),
               reads=[K("RR"), "c8"], writes=[K("TT")])
            if evac is not None:
                evac()
            OP("scalar", (lambda e: e.activation(out=AA, in_=RR, func=AF.Exp, scale=smc(SM_NC8 + cj))),
               reads=[K("RR"), "nc8"], writes=[K("AA")])
            OP("scalar", (lambda e: e.activation(out=A2, in_=RR, func=AF.Exp, scale=smc(SM_N2C8 + cj))),
               reads=[K("RR"), "n2c8"], writes=[K("A2")])

        def rec_s2(it):
            cj, hh = it // 2, it % 2
            par = it % 2
            XC, RR, II, TT, AA, A2, CT, XCB, SQ = tb[par]
            K = lambda n: (n, par)
            t0 = hh * TH_
            OP("vector", lambda e: e.scalar_tensor_tensor(out=A2, in0=A2, scalar=1.0, in1=TT, op0=ALU.add, op1=ALU.mult),
               reads=[K("A2"), K("TT")], writes=[K("A2")])
            OP("scalar", lambda e: e.activation(out=A2, in_=A2, func=AF.Ln), reads=[K("A2")], writes=[K("A2")])
            OP("scalar", lambda e: e.activation(out=A2, in_=A2, func=AF.Exp, scale=0.5), reads=[K("A2")], writes=[K("A2")])

        def rec_s2b(it):
            cj, hh = it // 2, it % 2
            par = it % 2
            XC, RR, II, TT, AA, A2, CT, XCB, SQ = tb[par]
            K = lambda n: (n, par)
            t0 = hh * TH_
            OP("vector", lambda e: e.tensor_tensor(out=TT, in0=II, in1=XC, op=ALU.mult), reads=[K("II"), K("XC"), K("TT")], writes=[K("TT")])
            OP("vector", lambda e: e.tensor_tensor(out=TT, in0=TT, in1=A2, op=ALU.mult), reads=[K("TT"), K("A2")], writes=[K("TT")])
            init = 0.0 if hh == 0 else smc(SM_CARRY + cj)
            OP("vector", (lambda e: e.tensor_tensor_scan(out=II, data0=AA, data1=TT, initial=init, op0=ALU.mult, op1=ALU.add)),
               reads=[K("AA"), K("TT"), K("II"), ("carry", cj)], writes=[K("II")])
            if hh == 0:
                OP("vector", lambda e: e.tensor_copy(out=smc(SM_CARRY + cj), in_=II[:, TH_ - 1:TH_]),
                   reads=[K("II"), "sm"], writes=[("carry", cj)])

        def rec_s2c(it):
            cj, hh = it // 2, it % 2
            par = it % 2
            XC, RR, II, TT, AA, A2, CT, XCB, SQ = tb[par]
            K = lambda n: (n, par)
            t0 = hh * TH_
            OP("gpsimd", (lambda e: e.tensor_tensor(out=yT[:, cj, t0:t0 + TH_], in0=II, in1=SGA[cj % 2][:, t0:t0 + TH_], op=ALU.mult)),
               reads=[K("II"), ("SGA", cj % 2)], writes=[("yT", cj)])
            OP("gpsimd", (lambda e: e.tensor_tensor(out=SQall[:, cj, t0:t0 + TH_], in0=yT[:, cj, t0:t0 + TH_], in1=yT[:, cj, t0:t0 + TH_],
                                                    op=ALU.mult)),
               reads=[("yT", cj)], writes=[("SQall", cj)], extra=([gn_last[0]] if it == 0 else ()))

        V1 = b16v(RA, NB * 512).rearrange("p (b h m) -> p b h m", b=NB, h=8)

        def v_mm(batch):
            for q in range(4):
                blk = 4 * batch + q
                b = 4 + q
                for kc in range(8):
                    OP("tensor", (lambda e, kc=kc, b=b, blk=blk: e.matmul(bank(b), lhsT=hT[:, kc, blk * 128:(blk + 1) * 128],
                                                                           rhs=Wsl[0][:, kc, :], start=(kc == 0), stop=(kc == 7))),
                       reads=[("W", 0), ("hT", kc, blk)], writes=[("pb", b)])

        def v_evac(batch, dep):
            for q in range(4):
                blk = 4 * batch + q
                b = 4 + q
                OP("scalar", (lambda e, blk=blk, b=b: e.activation(out=V1[:, blk].rearrange("p h m -> p (h m)"), in_=bank(b), func=AF.Copy)),
                   reads=[("pb", b)], writes=[("V", blk)], extra=dep)

        xa_dead = {}
        inproj_mm(0, 0)
        evac_xa(0)
        inproj_mm(1, 0)
        evac_ga(0)
        rec_conv0(0)
        rec_conv(0)
        rec_gn()
        for t in range(9):
            cjn = t // 2 + 1
            ev = None
            if t < 8 and cjn <= 3:
                inproj_mm(t % 2, cjn)
                ev = (lambda cjn=cjn: evac_xa(cjn)) if t % 2 == 0 else (lambda cjn=cjn: evac_ga(cjn))
                if cjn == 3:
                    load_w(4 if t % 2 == 0 else 3, t % 2)
            if t >= 1:
                rec_s2(t - 1)
            if t >= 1:
                rec_s2b(t - 1)
            if t < 8:
                rec_act(t, ev)
            if t in (7, 8):
                v_evac(t - 7, xa_dead[0])
            if t + 1 < 8:
                rec_conv0(t + 1)
                rec_conv(t + 1)
                if t + 1 == 5:
                    xa_dead[0] = p.snapshot()
                if t + 1 == 7:
                    xa_dead[1] = p.snapshot()
            if t in (6, 7, 8):
                v_mm(t - 6)
            if t >= 1:
                rec_s2c(t - 1)
        v_evac(2, xa_dead[1])
        v_mm(3)
        v_evac(3, xa_dead[1])
        snap3 = p.snapshot()
        tail_last = [None]

        def rec_tail():
            for q in (3, 0, 1, 2):
                for cj in range(4):
                    OP("tensor", (lambda e, q=q, cj=cj: e.matmul(bank(q), lhsT=onesb, rhs=SQall[:, cj, q * 512:(q + 1) * 512],
                                                                 start=(cj == 0), stop=(cj == 3))),
                       reads=[("SQall", cj), "onesb"], writes=[("pb", q)])
            for q in (3, 0, 1, 2):
                OP("scalar", (lambda e, q=q: e.activation(out=RBC[:, q * 512:(q + 1) * 512], in_=bank(q), func=AF.Ln, scale=1.0 / 512, bias=EPS)),
                   reads=[("pb", q)], writes=[("RBCq", q)])
            OP("scalar", lambda e: e.activation(out=RBC, in_=RBC, func=AF.Exp, scale=-0.5), reads=[("RBCq", q) for q in range(4)], writes=["RBC"])
            for cj in range(4):
                tail_last[0] = OP("vector", (lambda e, cj=cj: e.scalar_tensor_tensor(out=yT[:, cj, :], in0=yT[:, cj, :], scalar=ppc(PP_NREC + cj),
                                                                                     in1=RBC, op0=ALU.mult, op1=ALU.mult)),
                                  reads=["RBC", ("yT", cj), "pp"], writes=[("yT", cj)])

        p.fence(ALL_E, dep=snap3)
        cur[0] = RA
        o_V = alloc(NB * 512 // 2)
        assert o_V == RA
        o_KT = alloc(4 * S // 2)
        o_QT = alloc(4 * S // 2)
        o_V2 = alloc(NB * 512 // 2)
        o_sgb = alloc(4 * S // 2)
        o_rt = o_yT + 4096
        QT = b16v(o_QT, 4 * S).rearrange("p (j t) -> p j t", j=4)
        KT = b16v(o_KT, 4 * S).rearrange("p (j t) -> p j t", j=4)
        V2 = b16v(o_V2, NB * 512).rearrange("p (b h m) -> p b h m", b=NB, h=8)
        o_spare = cur[0]
        V3h = [b16v(o_cs, 8 * 512).rearrange("p (b h m) -> p b h m", b=8, h=8),
               b16v(o_spare, 8 * 512).rearrange("p (b h m) -> p b h m", b=8, h=8)]
        sgb = b16v(o_sgb, 4 * S).rearrange("p (j t) -> p j t", j=4)
        T1 = [f32v(o_rt + s * 512, 512) for s in range(3)]
        T2 = [f32v(o_rt + 1536 + s * 512, 512) for s in range(3)]
        QR = [b16v(o_rt + 3072 + s * 256, 512) for s in range(3)]
        load_w(2, 0)
        it = [0]

        def qk_a(i):
            if i == 3:
                rec_tail()
            if i == 4:
                OP("sync", lambda e: e.dma_start(out=vscr_d.rearrange("(b p) e -> p b e", p=128), in_=V1.rearrange("p b h m -> p b (h m)")),
                   reads=[("V", b) for b in range(NB)], writes=["vscr"], dma_sem="vs0")
                OP("sync", lambda e: e.dma_start(out=V2.rearrange("p (n r) h m -> p n (r h m)", n=4),
                                                 in_=vscr_d.rearrange("(n i r) e -> i n (r e)", n=4, r=4)),
                   reads=["vscr"], writes=["V2"], dma_sem="vs1", extra=[tail_last[0]])
            if i == NB + 2:
                load_w(5, 1)
            s, blk = 1 - i // NB, i % NB
            b = i % 4
            r = i % 3
            for kc in range(8):
                OP("tensor", (lambda e, kc=kc: e.matmul(bank(b), lhsT=hT[:, kc, blk * 128:(blk + 1) * 128],
                                                        rhs=Wsl[s][:, kc, :], start=(kc == 0), stop=(kc == 7))),
                   reads=[("W", s), ("hT", kc, blk)], writes=[("pb", b)])
            pbk = bank(b)
            cos_b = view(cosF[:, blk * 64:blk * 64 + 1], [[0, 8], [1, 64]])
            sin_b = view(sinS[:, blk * 64:blk * 64 + 1], [[0, 8], [32, 2], [1, 32]])
            swp = view(pbk[:, 32:33], [[64, 8], [-32, 2], [1, 32]])
            OP("vector", (lambda e: e.tensor_tensor(
                out=T1[r].rearrange("p (h d) -> p h d", h=8), in0=pbk.rearrange("p (h d) -> p h d", h=8), in1=cos_b, op=ALU.mult)),
               reads=[("pb", b), "cosF"], writes=[("T1", r)])
            OP("vector", (lambda e: e.tensor_tensor(
                out=T2[r].rearrange("p (h s d) -> p h s d", h=8, s=2), in0=swp, in1=sin_b, op=ALU.mult)),
               reads=[("pb", b), "sinS"], writes=[("T2", r)])
            OP("gpsimd", (lambda e: e.tensor_tensor(out=QR[r], in0=T1[r], in1=T2[r], op=ALU.add)),
               reads=[("T1", r), ("T2", r)], writes=[("QR", r)])

        def qk_c(i):
            s, blk = 1 - i // NB, i % NB
            r = i % 3
            tb_ = 4 + (i % 4)
            dstT = QT if s == 0 else KT
            for j in range(4):
                OP("tensor", (lambda e, j=j: e.transpose(out=bank16(tb_)[:, j * 128:(j + 1) * 128],
                                                         in_=QR[r][:, j * 128:(j + 1) * 128], identity=ident)),
                   reads=[("QR", r), "ident"], writes=[("pb", tb_)])
            OP("scalar", (lambda e: e.activation(
                out=dstT[:, :, blk * 128:(blk + 1) * 128], in_=bank16(tb_)[:, 0:512].rearrange("p (j t) -> p j t", j=4), func=AF.Copy)),
               reads=[("pb", tb_)], writes=[("QKT", s, blk)])

        pipeline(2 * NB, [(0, qk_a), (2, qk_c)])
        vsrc3 = vscr_d.rearrange("(i r) e -> i r e", r=16)
        for hf in range(2):
            OP("sync", (lambda e, hf=hf: e.dma_start(out=V3h[hf].rearrange("p b h m -> p b (h m)"), in_=vsrc3[:, 8 * hf:8 * hf + 8, :])),
               reads=["vscr"], writes=["V3", "cosF", "sinS"], dma_sem="vs2", extra=[tail_last[0]])
        for cj in range(4):
            for tc in range(4):
                fm_proj(1, cj, tc, lambda b, cj=cj, tc=tc: OP(
                    "scalar", (lambda e: e.activation(out=sgb[:, cj, tc * 512:(tc + 1) * 512], in_=bank(b), func=AF.Silu)),
                    reads=[("pb", b)], writes=[("sgb", cj)], extra=[tail_last[0]]))

        last_pe4 = max(i_ for i_, o_ in enumerate(p.ops) if o_["eng"] == "tensor" and o_["fn"] is not None)
        p.fence(ALL_E, dep={last_pe4})
        cur[0] = o_hT
        o_P = alloc(5 * 512)
        o_LD = alloc(2 * 512)
        o_TN = alloc(512)
        o_YB = alloc(4 * 512)
        o_SQ = alloc(512)
        o_SQS = alloc(512)
        o_SQB = alloc(256)
        o_RB = alloc(512)
        assert cur[0] <= o_W
        Pt = [b16v(o_P + s * 512, 1024).rearrange("p (a n) -> p a n", a=2) for s in range(5)]
        LD = [f32v(o_LD + s * 512, 512) for s in range(2)]
        RD = LD
        TN = [f32v(o_TN, 512)] * 2
        YB4 = [f32v(o_YB + s * 512, 512) for s in range(4)]
        SQ4 = f32v(o_SQ, 512)
        SQS = f32v(o_SQS, 512)
        SQB = b16v(o_SQB, 512)
        RB = f32v(o_RB, 512)
        TRI_LE = maskT[:, 0:128]
        TRI_GE = maskT[:, 128:256]
        items = [(c, j, k) for c in range(4) for j in range(4) for k in range(5) if not (c == 0 and k == 3)]

        def s_specs(c, j, a, kind):
            rows = slice(a * 64, (a + 1) * 64)
            out = []
            if kind in (0, 1):
                for u in range(4):
                    qb = 4 * c + u
                    kb = qb - kind
                    if kb < 0:
                        continue
                    out.append((KT[rows, j, 128 * kb:128 * kb + 128], QT[rows, j, 128 * qb:128 * qb + 128], 128 * u, 128 * u + 128, 128))
            elif kind in (2, 3):
                ck = c - (kind - 2)
                for r in range(4):
                    out.append((KT[rows, j, 512 * ck + r:512 * (ck + 1):4], QT[rows, j, 512 * c + r:512 * (c + 1):4],
                                128 * r, 128 * r + 128, 128))
            else:
                Mk = 128
                for r in range(16):
                    out.append((KT[rows, j, r:S:16], QT[rows, j, 512 * c + r:512 * (c + 1):16], 32 * r, 32 * r + 32, Mk))
            return out

        def at_a(i):
            c, j, kind = items[i]
            sb_ = (i % 2) * 2
            pt = i % 5
            Pn = 128
            sp = [s_specs(c, j, a, kind) for a in range(2)]
            for idx in range(len(sp[0])):
                for a in range(2):
                    (lh, rh, lo, hi, M) = sp[a][idx]
                    OP("tensor", (lambda e, lh=lh, rh=rh, lo=lo, hi=hi, M=M, a=a: e.matmul(
                        bank(sb_ + a, lo, hi, 0, M), lhsT=lh, rhs=rh, start=True, stop=True)),
                       reads=[("QKT", 0, 4 * c + u) for u in range(4)] + [("QKT", 1, u) for u in range(4 * c + 4)], writes=[("pb", sb_ + a)])
            sview = view(ps[0:Pn, sb_ * 512:sb_ * 512 + 1], [[512, 2], [1, 512]])
            OP("scalar", (lambda e: e.activation(out=Pt[pt][0:Pn], in_=sview, func=AF.Exp, scale=0.125)),
               reads=[("pb", sb_), ("pb", sb_ + 1)], writes=[("P", pt)])
            if kind == 4:
                mview = view(TRI_LE[0:Pn, 32 * c:32 * c + 1], [[0, 2], [0, 16], [1, 32]])
                pv = Pt[pt][0:Pn].rearrange("p a (r i) -> p a r i", r=16)
            else:
                mview = view((TRI_LE if kind in (0, 2) else TRI_GE)[:, 0:1], [[0, 2], [0, 4], [1, 128]])
                pv = Pt[pt].rearrange("p a (u q) -> p a u q", u=4)
            OP("vector", (lambda e: e.tensor_tensor(out=pv, in0=pv, in1=mview, op=ALU.mult)),
               reads=[("P", pt), "maskT"], writes=[("P", pt)])

        def at_c(i):
            c, j, kind = items[i]
            pt = i % 5
            pair = c * 4 + j
            par = pair % 2
            bN, bD = 4 + 2 * par, 5 + 2 * par
            Pn = 128
            mms = []
            for a in range(2):
                h = 2 * j + a
                r0, r1 = a * 64, (a + 1) * 64
                mm = []
                if kind in (0, 1):
                    us = [u for u in range(4) if 4 * c + u - kind >= 0]
                    for u in us:
                        kb = 4 * c + u - kind
                        mm.append((bank(bN, 128 * u, 128 * u + 128, r0, r1), V1[:, kb, h, :], Pt[pt][:, a, 128 * u:128 * u + 128]))
                    lo = 128 * us[0]
                    mm.append((bank(bD, lo, 512, r0, r1), None, Pt[pt][:, a, lo:512]))
                elif kind in (2, 3):
                    ck = c - (kind - 2)
                    for r in range(4):
                        mm.append((bank(bN, r, 512, r0, r1)[:, ::4], V2[:, 4 * ck + r, h, :], Pt[pt][:, a, 128 * r:128 * r + 128]))
                        mm.append((bank(bD, r, 512, r0, r1)[:, ::4], None, Pt[pt][:, a, 128 * r:128 * r + 128]))
                else:
                    for r in range(16):
                        mm.append((bank(bN, r, 512, r0, r1)[:, ::16], V3h[r // 8][0:Pn, r % 8, h, :], Pt[pt][0:Pn, a, 32 * r:32 * r + 32]))
                        mm.append((bank(bD, r, 512, r0, r1)[:, ::16], None, Pt[pt][0:Pn, a, 32 * r:32 * r + 32]))
                mms.append(mm)
            first = [[kind == 0, kind == 0], [kind == 0, kind == 0]]
            for idx in range(len(mms[0])):
                for a in range(2):
                    (of, lh, rh) = mms[a][idx]
                    if lh is not None:
                        OP("tensor", (lambda e, of=of, lh=lh, rh=rh, st=first[a][0]: e.matmul(of, lhsT=lh, rhs=rh, start=st, stop=False,
                                                                                            skip_group_check=True)),
                           reads=[("P", pt), "V1", "V2", "V3"], writes=[("pb", bN)])
                        first[a][0] = False
                    else:
                        OP("tensor", (lambda e, of=of, rh=rh, st=first[a][1]: e.matmul(of, lhsT=ones64[0:Pn, :], rhs=rh, start=st, stop=False,
                                                                                      skip_group_check=True)),
                           reads=[("P", pt), "ones64"], writes=[("pb", bD)])
                        first[a][1] = False
            if kind != 4:
                return
            K = lambda n: (n, par)
            OP("scalar", lambda e: e.activation(out=LD[par], in_=bank(bD), func=AF.Ln), reads=[("pb", bD)], writes=[K("LD"), K("RD")])
            OP("scalar", lambda e: e.activation(out=RD[par], in_=LD[par], func=AF.Exp, scale=-1.0), reads=[K("LD")], writes=[K("RD")])
            OP("vector", lambda e: e.tensor_tensor(out=TN[par], in0=bank(bN), in1=RD[par], op=ALU.mult),
               reads=[("pb", bN), K("RD")], writes=["TN"])
            OP("gpsimd", (lambda e: e.tensor_tensor(out=YB4[j], in0=TN[par], in1=sgb[:, j, c * 512:(c + 1) * 512], op=ALU.mult)),
               reads=["TN", ("sgb", j)], writes=[("YB4", j)])
            if j == 0:
                OP("gpsimd", lambda e: e.tensor_tensor(out=SQS, in0=YB4[j], in1=YB4[j], op=ALU.mult), reads=[("YB4", j)], writes=["SQS"])
            else:
                OP("gpsimd", lambda e: e.tensor_tensor(out=SQ4, in0=YB4[j], in1=YB4[j], op=ALU.mult), reads=[("YB4", j)], writes=["SQ4"])
                OP("gpsimd", lambda e: e.tensor_tensor(out=SQS, in0=SQS, in1=SQ4, op=ALU.add), reads=["SQS", "SQ4"], writes=["SQS"])

        late_d = []

        def at_d0(i):
            c, j, kind = items[i]
            if kind == 4 and j == 3:
                OP("vector", lambda e: e.tensor_copy(out=SQB, in_=SQS), reads=["SQS"], writes=["SQB"])

        def at_d(i, force=False):
            c, j, kind = items[i]
            if kind != 4 or j != 3:
                return
            if i == len(items) - 1 and not force:
                late_d.append(i)
                return
            pair = c * 4 + j
            par = pair % 2
            bN, bD = 4 + 2 * par, 5 + 2 * par
            OP("tensor", (lambda e: e.matmul(bank(bD), lhsT=onesb, rhs=SQB, start=True, stop=True)),
               reads=["SQB", "onesb"], writes=[("pb", bD)])
            OP("scalar", lambda e: e.activation(out=RB, in_=bank(bD), func=AF.Ln, scale=1.0 / 512, bias=EPS),
               reads=[("pb", bD)], writes=["RB"])
            OP("scalar", lambda e: e.activation(out=RB, in_=RB, func=AF.Exp, scale=-0.5), reads=["RB"], writes=["RB"])
            for jj in range(4):
                OP("vector", (lambda e, jj=jj: e.scalar_tensor_tensor(out=yT[:, 4 + jj, c * 512:(c + 1) * 512], in0=YB4[jj],
                                                                      scalar=ppc(PP_NATT + jj), in1=RB, op0=ALU.mult, op1=ALU.mult)),
                   reads=["RB", ("YB4", jj), "pp"], writes=[("yT", 4 + jj)])

        pipeline(len(items), [(0, at_a), (3, at_c), (5, at_d0), (7, at_d)])

        last_pe = max(i_ for i_, o_ in enumerate(p.ops) if o_["eng"] == "tensor" and o_["fn"] is not None)
        p.fence(ALL_E, dep={last_pe})
        cur[0] = RA
        o_xs5 = alloc(4 * D)
        o_t5 = alloc(4 * D)
        o_j5 = alloc(D // 2)
        xs5 = [f32v(o_xs5 + s * D, D) for s in range(4)]
        T5 = [f32v(o_t5 + s * D, D) for s in range(4)]
        junk5 = b16v(o_j5, D)
        stores = []

        def p5_load(blk):
            s = blk % 4
            OP("sync", (lambda e: e.dma_start(out=xs5[s], in_=x_d[blk * 128:(blk + 1) * 128, :])),
               writes=[("xs5", s)], dma_sem="xr%d" % s)

        def p5_main(blk):
            s = blk % 4
            b0 = (blk % 2) * 2
            for h in range(2):
                for ec in range(8):
                    OP("tensor", (lambda e, h=h, ec=ec: e.matmul(
                        bank(b0 + h), lhsT=yT[:, ec, blk * 128:(blk + 1) * 128], rhs=woutb[:, ec, h * 512:(h + 1) * 512],
                        start=(ec == 0), stop=(ec == 7))),
                       reads=[("yT", ec), "wout"], writes=[("pb", b0 + h)])
            mix = ps[:, b0 * 512:b0 * 512 + 1024]
            OP("scalar", (lambda e: e.activation(out=junk5, in_=mix, func=AF.Square, accum_out=smc(SM_SSP + blk))),
               reads=[("pb", b0), ("pb", b0 + 1), "sm"], writes=["junk5", ("ssp", blk)])
            OP("scalar", (lambda e: e.activation(out=smc(SM_STDP + blk), in_=smc(SM_SSP + blk), func=AF.Sqrt,
                                                 scale=1.0 / D, bias=EPS)),
               reads=[("ssp", blk)], writes=[("stdp", blk)])
            OP("vector", (lambda e: e.reciprocal(out=smc(SM_RSTDP + blk), in_=smc(SM_STDP + blk))),
               reads=[("stdp", blk)], writes=[("rstdp", blk)])
            OP("vector", (lambda e: e.scalar_tensor_tensor(out=T5[s], in0=mix, scalar=smc(SM_RSTDP + blk), in1=gn_bc,
                                                           op0=ALU.mult, op1=ALU.mult)),
               reads=[("pb", b0), ("pb", b0 + 1), ("rstdp", blk), "gn_bc"], writes=[("T5", s)])
            OP("vector", (lambda e: e.tensor_tensor(out=T5[s], in0=T5[s], in1=xs5[s], op=ALU.add)),
               reads=[("T5", s), ("xs5", s)], writes=[("T5", s)])
            stores.append(OP("sync", (lambda e: e.dma_start(out=out_d[blk * 128:(blk + 1) * 128, :], in_=T5[s])),
                             reads=[("T5", s)], dma_sem="st%d" % s))
            if blk == 7:
                at_d(late_d[0], force=True)

        pipeline(NB, [(0, p5_load), (2, p5_main)])
        OP("sync", None, extra=stores)
        p.emit()
    return nc


_CACHE = {}


def _c_mult(j):
    c = np.zeros_like(j, dtype=np.float32)
    c += ((j >= 0) & (j <= 128))
    c += ((j >= 0) & (j % 4 == 0) & (j <= 512))
    c += ((j >= 0) & (j % 16 == 0))
    return c.astype(np.float32)


def kernel(x, c, positions, w_ada, b_ada, norm_pre, norm_post, w_in, conv_w, conv_b,
           w_rg_a, b_rg_a, w_rg_x, b_rg_x, lru_lambda, norm_rec, norm_att, w_out):
    x = np.asarray(x, np.float32)
    B = x.shape[0]
    if "nc" not in _CACHE:
        _CACHE["nc"] = build_program()
    nc = _CACHE["nc"]
    f = lambda a: np.ascontiguousarray(np.asarray(a, np.float32))
    col = lambda v, n: f(v).reshape(n, 128).T
    rows = np.concatenate([f(b_ada)[0], f(norm_post)[0]])[None, :]
    wbd = np.zeros((128, 2, 4, 128), np.float32)
    for g, w in enumerate((f(w_rg_a)[0], f(w_rg_x)[0])):
        for cj in range(4):
            wbd[0:64, g, cj, 0:64] = w[2 * cj]
            wbd[64:128, g, cj, 64:128] = w[2 * cj + 1]
    wbd = wbd.reshape(128, 1024)
    pidx = np.arange(128)[:, None]
    xidx = np.arange(S)[None, :]
    q128 = np.arange(128)[None, :]
    maskT = np.concatenate([(pidx <= q128), (pidx >= q128)], axis=1).astype(np.float32)
    half = 32
    invf = (10000.0 ** (-np.arange(half, dtype=np.float32) / half)).astype(np.float32)
    pp_shared = np.zeros((128, NPP), np.float32)
    pp_shared[:, PP_NPRE:PP_NPRE + 8] = col(norm_pre[0], 8)
    pp_shared[:, PP_CONVW:PP_CONVW + 16] = f(conv_w)[0].reshape(4, 4, 128).transpose(2, 1, 0).reshape(128, 16)
    pp_shared[:, PP_CONVB:PP_CONVB + 4] = col(conv_b[0], 4)
    pp_shared[:, PP_BA:PP_BA + 4] = col(b_rg_a[0], 4)
    pp_shared[:, PP_BX:PP_BX + 4] = col(b_rg_x[0], 4)
    pp_shared[:, PP_LAM:PP_LAM + 4] = col(lru_lambda[0], 4)
    pp_shared[:, PP_NREC:PP_NREC + 4] = col(norm_rec[0], 4)
    pp_shared[:, PP_NATT:PP_NATT + 4] = col(norm_att[0], 4)
    pp_shared[:, PP_INVF:PP_INVF + 32] = invf[None, :]
    w_ada2 = f(w_ada)[0]
    w_in2 = f(w_in)[0]
    w_out2 = f(w_out)[0]
    in_maps = []
    for b in range(B):
        pp = pp_shared.copy()
        pp[:, PP_C:PP_C + 8] = col(np.asarray(c)[b], 8)
        pos = np.ascontiguousarray(np.asarray(positions)[b].astype(np.int32).reshape(NB, 128).T)
        in_maps.append({"x": np.ascontiguousarray(x[b]), "pp": pp, "pos": pos, "w_ada": w_ada2, "rows": rows,
                        "w_in": w_in2, "w_out": w_out2, "wbd": wbd, "maskT": maskT})
    res = run_bass_kernel_spmd(nc, in_maps, core_ids=list(range(B)))
    return np.stack([np.asarray(r["out"], np.float32) for r in res.results], axis=0)
```
